# Optimizing a Trainium2 kernel written in Bass

```python
import math
import jax, jax.numpy as jnp
from jax import lax
import numpy as np

D_MODEL = 2048
BATCH = 16
SEQ = 2048
DEPTH = 4
DEC_BATCH = 16
DEC_SEQ = 64
PAST_LEN = 2048

CHUNK = 64
N_META = 16
N_HEADS = 8
HEAD_DIM = 64
V_DIM = 2 * HEAD_DIM
ATT_W = N_HEADS * V_DIM
CONV_W = D_MODEL - ATT_W
IN_W = 3 * ATT_W + 2 * CONV_W
CONV_K = 31
ROPE_DIM = HEAD_DIM // 4
ROPE_THETA = 500000.0
D_FF = 4 * D_MODEL
Q_BLOCK = 128
EPS = 1e-6
NEG = -1e30
PAD_CHUNK = 2 ** 30
ATTN_SCALE = HEAD_DIM ** -0.5

kernel_name = "hymba_diffattn_conformer_stream_step"


def rms_norm(x, g):
    xf = x.astype(jnp.float32)
    y = xf * lax.rsqrt(jnp.mean(xf * xf, axis=-1, keepdims=True) + EPS)
    return (y * g.astype(jnp.float32)).astype(x.dtype)


def layer_norm(x, g, b):
    xf = x.astype(jnp.float32)
    xc = xf - jnp.mean(xf, axis=-1, keepdims=True)
    y = xc * lax.rsqrt(jnp.mean(xc * xc, axis=-1, keepdims=True) + EPS)
    return (y * g.astype(jnp.float32) + b.astype(jnp.float32)).astype(x.dtype)


def partial_rope(x, pos):
    half = ROPE_DIM // 2
    inv_freq = ROPE_THETA ** (-(jnp.arange(half, dtype=jnp.float32) * 2.0) / ROPE_DIM)
    ang = pos.astype(jnp.float32)[:, None] * inv_freq[None, :]
    cos = jnp.cos(ang)[:, None, None, :]
    sin = jnp.sin(ang)[:, None, None, :]
    xr = x[..., :ROPE_DIM].astype(jnp.float32)
    x1, x2 = xr[..., :half], xr[..., half:]
    rot = jnp.concatenate([x1 * cos - x2 * sin, x2 * cos + x1 * sin], axis=-1).astype(x.dtype)
    return jnp.concatenate([rot, x[..., ROPE_DIM:]], axis=-1)


def mixer_inputs(h, w_in, q_norm_g, k_norm_g, pos):
    b, t, _ = h.shape
    z = jnp.einsum('btd,de->bte', h, w_in)
    q = z[..., :ATT_W].reshape(b, t, N_HEADS, 2, HEAD_DIM)
    k = z[..., ATT_W:2 * ATT_W].reshape(b, t, N_HEADS, 2, HEAD_DIM)
    v = z[..., 2 * ATT_W:3 * ATT_W].reshape(b, t, N_HEADS, V_DIM)
    a, gate = jnp.split(z[..., 3 * ATT_W:], 2, axis=-1)
    g = a * jax.nn.sigmoid(gate)
    q = partial_rope(rms_norm(q, q_norm_g), pos)
    k = partial_rope(rms_norm(k, k_norm_g), pos)
    return q, k, v, g


def lambda_init(layer):
    return 0.8 - 0.6 * math.exp(-0.3 * layer)


def diff_lambda(lq1, lk1, lq2, lk2, lam0):
    f = jnp.float32
    return (jnp.exp(jnp.sum(lq1.astype(f) * lk1.astype(f)))
            - jnp.exp(jnp.sum(lq2.astype(f) * lk2.astype(f))) + lam0)


def diff_attend(q, k, v, mask, lam):
    s = jnp.einsum('bqhcd,bkhcd->bhcqk', q.astype(jnp.float32), k.astype(jnp.float32)) * ATTN_SCALE
    if mask is not None:
        s = jnp.where(mask, s, NEG)
    p = jax.nn.softmax(s, axis=-1)
    a = p[:, :, 0] - lam * p[:, :, 1]
    return jnp.einsum('bhqk,bkhe->bqhe', a, v.astype(jnp.float32))


def chunk_causal_diff_attention(q, k, v, chunk_id, lam):
    b, L = q.shape[0], q.shape[1]
    nb = -(-L // Q_BLOCK)
    lp = nb * Q_BLOCK
    qp = jnp.pad(q, ((0, 0), (0, lp - L), (0, 0), (0, 0), (0, 0)))
    cq = jnp.pad(chunk_id, (0, lp - L), constant_values=PAD_CHUNK)
    qb = jnp.moveaxis(qp.reshape(b, nb, Q_BLOCK, N_HEADS, 2, HEAD_DIM), 1, 0)
    cb = cq.reshape(nb, Q_BLOCK)

    def one_block(args):
        q_blk, c_blk = args
        mask = chunk_id[None, :] <= c_blk[:, None]
        return diff_attend(q_blk, k, v, mask, lam)

    o = lax.map(one_block, (qb, cb))
    o = jnp.moveaxis(o, 0, 1).reshape(b, lp, N_HEADS, V_DIM)
    return o[:, :L]


def diff_head_out(o, g, lam0, dtype):
    o = rms_norm(o, g) * (1.0 - lam0)
    return o.reshape(o.shape[0], o.shape[1], ATT_W).astype(dtype)


def conv_module_tail(gpad, conv_w, conv_b, ln_g, ln_b):
    y = lax.conv_general_dilated(gpad.astype(jnp.float32), conv_w.astype(jnp.float32)[:, None, :],
                                 window_strides=(1,), padding='VALID',
                                 dimension_numbers=('NWC', 'WIO', 'NWC'),
                                 feature_group_count=CONV_W)
    y = layer_norm(y + conv_b.astype(jnp.float32), ln_g, ln_b)
    return jax.nn.silu(y).astype(gpad.dtype)


def sq_relu_mlp(h, w_up, w_down):
    a = jnp.einsum('btd,df->btf', h, w_up)
    return jnp.einsum('btf,fd->btd', jnp.square(jax.nn.relu(a)), w_down)


def finish_layer(x, att, gpad, conv_w, conv_b, ln_g, ln_b, w_out, n2, w_up, w_down):
    c = conv_module_tail(gpad, conv_w, conv_b, ln_g, ln_b)
    mixed = jnp.concatenate([att, c], axis=-1)
    x = x + jnp.einsum('bte,ed->btd', mixed, w_out)
    return x + sq_relu_mlp(rms_norm(x, n2), w_up, w_down)


def setup_inputs(seed: int = 0) -> dict:
    key = jax.random.key(seed)
    ks = jax.random.split(key, 24)
    f = jnp.float32

    def nrm(k, shape, s):
        return jax.random.normal(k, shape, f) * s

    return {
        "x_prompt": nrm(ks[0], (BATCH, SEQ, D_MODEL), 1.0),
        "x_sample": nrm(ks[1], (DEC_BATCH, DEC_SEQ, D_MODEL), 1.0),
        "cache_k": nrm(ks[2], (DEPTH, DEC_BATCH, PAST_LEN, N_HEADS, V_DIM), 1.0),
        "cache_v": nrm(ks[3], (DEPTH, DEC_BATCH, PAST_LEN, N_HEADS, V_DIM), 1.0),
        "state_conv": nrm(ks[4], (DEPTH, DEC_BATCH, CONV_K - 1, CONV_W), 0.5),
        "meta_tokens": nrm(ks[5], (N_META, D_MODEL), 1.0),
        "norm1_g": 1.0 + nrm(ks[6], (DEPTH, D_MODEL), 0.02),
        "w_in": nrm(ks[7], (DEPTH, D_MODEL, IN_W), D_MODEL ** -0.5),
        "q_norm_g": 1.0 + nrm(ks[8], (DEPTH, HEAD_DIM), 0.02),
        "k_norm_g": 1.0 + nrm(ks[9], (DEPTH, HEAD_DIM), 0.02),
        "lam_q1": nrm(ks[10], (DEPTH, HEAD_DIM), 0.1),
        "lam_k1": nrm(ks[11], (DEPTH, HEAD_DIM), 0.1),
        "lam_q2": nrm(ks[12], (DEPTH, HEAD_DIM), 0.1),
        "lam_k2": nrm(ks[13], (DEPTH, HEAD_DIM), 0.1),
        "attn_norm_g": 1.0 + nrm(ks[14], (DEPTH, V_DIM), 0.02),
        "conv_w": nrm(ks[15], (DEPTH, CONV_K, CONV_W), CONV_K ** -0.5),
        "conv_b": nrm(ks[16], (DEPTH, CONV_W), 0.01),
        "conv_ln_g": 1.0 + nrm(ks[17], (DEPTH, CONV_W), 0.02),
        "conv_ln_b": nrm(ks[18], (DEPTH, CONV_W), 0.01),
        "w_out": nrm(ks[19], (DEPTH, D_MODEL, D_MODEL), D_MODEL ** -0.5),
        "norm2_g": 1.0 + nrm(ks[20], (DEPTH, D_MODEL), 0.02),
        "w_up": nrm(ks[21], (DEPTH, D_MODEL, D_FF), D_MODEL ** -0.5),
        "w_down": nrm(ks[22], (DEPTH, D_FF, D_MODEL), D_FF ** -0.5),
    }


def reference(x_prompt, x_sample, cache_k, cache_v, state_conv, meta_tokens, norm1_g, w_in,
              q_norm_g, k_norm_g, lam_q1, lam_k1, lam_q2, lam_k2, attn_norm_g, conv_w, conv_b,
              conv_ln_g, conv_ln_b, w_out, norm2_g, w_up, w_down):
    L = N_META + SEQ
    pos_p = jnp.arange(L, dtype=jnp.int32)
    chunk_p = jnp.concatenate([jnp.zeros((N_META,), jnp.int32),
                               1 + jnp.arange(SEQ, dtype=jnp.int32) // CHUNK])
    pos_s = N_META + PAST_LEN + jnp.arange(DEC_SEQ, dtype=jnp.int32)

    xp = jnp.concatenate([jnp.broadcast_to(meta_tokens.astype(x_prompt.dtype)[None],
                                           (BATCH, N_META, D_MODEL)), x_prompt], axis=1)
    xs = x_sample
    k_p, v_p, c_p, k_s, v_s, c_s = [], [], [], [], [], []

    for l in range(DEPTH):
        lam0 = lambda_init(l)
        lam = diff_lambda(lam_q1[l], lam_k1[l], lam_q2[l], lam_k2[l], lam0)

        q, k, v, g = mixer_inputs(rms_norm(xp, norm1_g[l]), w_in[l], q_norm_g[l], k_norm_g[l], pos_p)
        att = diff_head_out(chunk_causal_diff_attention(q, k, v, chunk_p, lam), attn_norm_g[l], lam0, xp.dtype)
        gpad = jnp.pad(g, ((0, 0), (CONV_K - 1, 0), (0, 0)))
        xp = finish_layer(xp, att, gpad, conv_w[l], conv_b[l], conv_ln_g[l], conv_ln_b[l],
                          w_out[l], norm2_g[l], w_up[l], w_down[l])
        k_p.append(k.reshape(BATCH, L, N_HEADS, V_DIM))
        v_p.append(v)
        c_p.append(gpad[:, -(CONV_K - 1):])
        k_meta = jnp.broadcast_to(k[:1, :N_META], (DEC_BATCH, N_META, N_HEADS, 2, HEAD_DIM))
        v_meta = jnp.broadcast_to(v[:1, :N_META], (DEC_BATCH, N_META, N_HEADS, V_DIM))

        qs, ks_, vs_, gs = mixer_inputs(rms_norm(xs, norm1_g[l]), w_in[l], q_norm_g[l], k_norm_g[l], pos_s)
        k_all = jnp.concatenate([k_meta.astype(ks_.dtype),
                                 cache_k[l].reshape(DEC_BATCH, PAST_LEN, N_HEADS, 2, HEAD_DIM).astype(ks_.dtype),
                                 ks_], axis=1)
        v_all = jnp.concatenate([v_meta.astype(vs_.dtype), cache_v[l].astype(vs_.dtype), vs_], axis=1)
        att_s = diff_head_out(diff_attend(qs, k_all, v_all, None, lam), attn_norm_g[l], lam0, xs.dtype)
        gpad_s = jnp.concatenate([state_conv[l].astype(gs.dtype), gs], axis=1)
        xs = finish_layer(xs, att_s, gpad_s, conv_w[l], conv_b[l], conv_ln_g[l], conv_ln_b[l],
                          w_out[l], norm2_g[l], w_up[l], w_down[l])
        k_s.append(ks_.reshape(DEC_BATCH, DEC_SEQ, N_HEADS, V_DIM))
        v_s.append(vs_)
        c_s.append(gpad_s[:, -(CONV_K - 1):])

    y_prompt = xp[:, N_META:]
    y_sample = xs
    return (y_prompt, y_sample, jnp.stack(k_p), jnp.stack(v_p), jnp.stack(c_p),
            jnp.stack(k_s), jnp.stack(v_s), jnp.stack(c_s))
```

```python
import contextlib
import math
import numpy as np
import concourse.bass as bass
import concourse.mybir as mybir
from concourse.bass_utils import run_bass_kernel_spmd

F32 = mybir.dt.float32
BF16 = mybir.dt.bfloat16
AF = mybir.ActivationFunctionType
ALU = mybir.AluOpType
AX = mybir.AxisListType

D = 2048
DEPTH = 4
SEQ = 2048
DSEQ = 64
PAST = 2048
NMETA = 16
NH = 8
INW = 5120
DFF = 8192
CONVK = 31
EPS = 1e-6
NCORES = 8
L_P = NMETA + SEQ

ENGS = ("pe", "act", "dve", "pool", "sp")


class Sched:
    def __init__(self, nc, stack, n_epochs=1):
        self.nc = nc
        self.stack = stack
        self.ops = {e: [] for e in ENGS}
        self.epoch = 0
        self.psem = {}
        for e in ENGS:
            if e == "sp":
                continue
            for ep in range(n_epochs):
                self.psem[(e, ep)] = stack.enter_context(nc.semaphore(f"p_{e}_{ep}"))
        self.pcount = {k: 0 for k in self.psem}
        self.dsem = {}
        self.dcount = {}
        self.last_w = {}
        self.readers = {}
        self.waited = {e: {} for e in ENGS}
        self.sem_owner = {}
        for (e, ep), s in self.psem.items():
            self.sem_owner[id(s)] = e
        self.n_wait = 0

    def _dma_sem(self, key):
        if key not in self.dsem:
            self.dsem[key] = self.stack.enter_context(self.nc.semaphore(f"d_{key}"))
            self.dcount[key] = 0
        return self.dsem[key]

    def op(self, eng, fn, reads=(), writes=(), dma=None):
        px = [r for r in reads if isinstance(r, tuple) and r[0] in ("pf", "pb")]
        if px:
            reads = [r for r in reads if r not in px]
            writes = list(writes) + [r for r in px if r not in writes]
        deps = {}

        def add(tok):
            if tok is None:
                return
            s, v = tok
            k = id(s)
            if k not in deps or deps[k][1] < v:
                deps[k] = (s, v)

        for r in reads:
            add(self.last_w.get(r))
        for w in writes:
            add(self.last_w.get(w))
            rd = self.readers.get(w)
            if rd:
                for tok in rd.values():
                    add(tok)
        waits = []
        wd = self.waited[eng]
        for k, (s, v) in deps.items():
            if eng == "pe" and self.sem_owner.get(k) == "pe":
                continue
            if wd.get(k, 0) >= v:
                continue
            wd[k] = v
            waits.append((s, v))
        self.n_wait += len(waits)
        if dma is not None:
            s = self._dma_sem(dma)
            self.dcount[dma] += 16
            tok = (s, self.dcount[dma])
            inc = (s, 16)
        else:
            key = (eng, self.epoch)
            s = self.psem[key]
            self.pcount[key] += 1
            tok = (s, self.pcount[key])
            inc = (s, 1)
        self.ops[eng].append((waits, fn, inc))
        for w in writes:
            self.last_w[w] = tok
            self.readers[w] = {}
        for r in reads:
            d = self.readers.setdefault(r, {})
            k = id(tok[0])
            if k not in d or d[k][1] < tok[1]:
                d[k] = tok
        return tok

    def emit(self):
        nc = self.nc
        final = [(self.dsem[k], self.dcount[k]) for k in self.dsem if self.dcount[k] > 0]
        ops = self.ops
        with nc.Block() as block:
            def run(e, name):
                for waits, fn, inc in ops[name]:
                    for s, v in waits:
                        e.wait_ge(s, v)
                    ins = fn(e)
                    ins.then_inc(inc[0], inc[1])

            @block.tensor
            def _(e):
                run(e, "pe")

            @block.scalar
            def _(e):
                run(e, "act")

            @block.vector
            def _(e):
                run(e, "dve")

            @block.gpsimd
            def _(e):
                run(e, "pool")

            @block.sync
            def _(e):
                run(e, "sp")
                for s, v in final:
                    e.wait_ge(s, v)


class Tile:
    def __init__(self, kind, s, g, nv, tid, ropei):
        self.kind, self.s, self.g, self.nv, self.tid, self.ropei = kind, s, g, nv, tid, ropei


def lam_init(l):
    return 0.8 - 0.6 * math.exp(-0.3 * l)


def build_program(depth=DEPTH, nblocks=None):
    nc = bass.Bass("TRN2", target_bir_lowering=False)

    def din(name, shape, dt=F32):
        return nc.dram_tensor(name, list(shape), dt, kind="ExternalInput").ap()

    def dout(name, shape):
        return nc.dram_tensor(name, list(shape), F32, kind="ExternalOutput").ap()

    xp = din("xp", [2, SEQ, D])
    xs = din("xs", [2, DSEQ, D])
    ck = din("ck", [DEPTH, 2, PAST, 1024])
    cv = din("cv", [DEPTH, 2, PAST, 1024])
    sc = din("sc", [DEPTH, 2, 30, 1024])
    meta = din("meta", [NMETA, D])
    norm1_g = din("norm1_g", [DEPTH, D])
    w_in = din("w_in", [DEPTH, D, INW])
    q_norm_g = din("q_norm_g", [DEPTH, 64])
    k_norm_g = din("k_norm_g", [DEPTH, 64])
    lam_q1 = din("lam_q1", [DEPTH, 64])
    lam_k1 = din("lam_k1", [DEPTH, 64])
    lam_q2 = din("lam_q2", [DEPTH, 64])
    lam_k2 = din("lam_k2", [DEPTH, 64])
    attn_norm_g = din("attn_norm_g", [DEPTH, 128])
    conv_w = din("conv_w", [DEPTH, CONVK, 1024])
    conv_b = din("conv_b", [DEPTH, 1024])
    conv_ln_g = din("conv_ln_g", [DEPTH, 1024])
    conv_ln_b = din("conv_ln_b", [DEPTH, 1024])
    w_out = din("w_out", [DEPTH, D, D])
    norm2_g = din("norm2_g", [DEPTH, D])
    w_up = din("w_up", [DEPTH, D, DFF])
    w_down = din("w_down", [DEPTH, DFF, D])
    identd = din("identd", [128, 128])
    roped = din("roped", [128, 18, 32])

    yp = dout("yp", [2, SEQ, D])
    ys = dout("ys", [2, DSEQ, D])
    kp = dout("kp", [DEPTH, 2, L_P, 1024])
    vp = dout("vp", [DEPTH, 2, L_P, 1024])
    cp = dout("cp", [DEPTH, 2, 30, 1024])
    kso = dout("kso", [DEPTH, 2, DSEQ, 1024])
    vso = dout("vso", [DEPTH, 2, DSEQ, 1024])
    cso = dout("cso", [DEPTH, 2, 30, 1024])

    NTID = 3 + 32
    xscr = nc.dram_tensor("xscr", [NTID * 128, D], F32).ap()
    wb = {
        "in": nc.dram_tensor("wb_in", [DEPTH, D, INW], BF16).ap(),
        "out": nc.dram_tensor("wb_out", [DEPTH, D, D], BF16).ap(),
        "up": nc.dram_tensor("wb_up", [DEPTH, D, DFF], BF16).ap(),
        "down": nc.dram_tensor("wb_down", [DEPTH, DFF, D], BF16).ap(),
    }
    wsrc = {"in": w_in, "out": w_out, "up": w_up, "down": w_down}
    CVROWS = {"in": 256, "out": 512, "up": 256, "down": 1024}
    WROWS = {"in": D, "out": D, "up": D, "down": DFF}

    XA = Tile("A", 0, 0, 64, 0, 16)
    XB = Tile("B", 1, 0, 64, 1, 16)
    XM = Tile("M", 0, 0, 16, 2, 17)
    blocks = [[XA, XB, XM]]
    for s in range(2):
        for j in range(4):
            blocks.append([Tile("S", s, 4 * j + i, 128, 3 + 16 * s + 4 * j + i, 4 * j + i) for i in range(4)])
    if nblocks is not None:
        blocks = blocks[:nblocks]

    with contextlib.ExitStack() as st:
        S = Sched(nc, st, n_epochs=DEPTH)

        def sb(name, shape, dt):
            return st.enter_context(nc.sbuf_tensor(name, list(shape), dt))

        def psum(name, shape, dt):
            return st.enter_context(nc.psum_tensor(name, list(shape), dt))

        xres = sb("xres", [128, 4, D], F32)
        big = sb("big", [128, 16, 512], BF16)
        qT = sb("qT", [128, 8, 512], BF16)
        KCOLS = NMETA + PAST + 2 * DSEQ
        kT = sb("kT", [128, 8, KCOLS], BF16)
        vS = sb("vS", [128, 19, 1024], BF16)
        GW = 30 + 512
        gT = sb("gT", [128, 8, GW], F32)
        NRING = 3
        ring = [sb(f"ring{i}", [128, 8, 512], BF16) for i in range(NRING)]
        xn = [sb(f"xn{i}", [128, D], BF16) for i in range(1)]
        kst = [sb(f"kst{i}", [128, 1024], BF16) for i in range(2)]
        NTF = 4
        tf = [sb(f"tf{i}", [128, 512], F32) for i in range(NTF)]
        ded = [sb(f"ded{i}", [128, 512], F32) for i in range(3)]
        NTB = 4
        tb = [sb(f"tb{i}", [128, 512], BF16) for i in range(NTB)]
        stt = sb("stt", [128, 64], F32)
        constT = [sb("constT0", [128, 384], F32)] * 2
        gqk = [sb("gqk0", [128, 128], F32)] * 2
        ident = sb("ident", [128, 128], F32)
        identb = sb("identb", [128, 128], BF16)
        onesf = sb("onesf", [128, 128], F32)
        onesb = sb("onesb", [128, 128], BF16)
        rope = sb("rope", [128, 18, 32], F32)
        ropet = sb("ropet", [128, 8, 16], F32)
        metaG = sb("metaG", [128, 8, 16], F32)
        hist = sb("hist", [128, 8, 30], F32)
        lamt = sb("lamt", [128, 16], F32)
        gAs = sb("gAs", [128, 4], F32)

        psf = [psum(f"psf{i}", [128, 512], F32) for i in range(6)]
        psb = [psum(f"psb{i}", [128, 1024], BF16) for i in range(2)]

        cnt = {"tf": 0, "tb": 0, "st": 0, "pf": 0, "pb": 0, "ring": 0, "xn": 0, "kst": 0, "sc": 0, "acc": 0}

        def rr(name, n):
            i = cnt[name] % n
            cnt[name] += 1
            return i

        def a_tf():
            i = rr("tf", NTF)
            return tf[i], ("tf", i)

        def a_tb():
            i = rr("tb", NTB)
            return tb[i], ("tb", i)

        def a_st():
            i = rr("st", 8)
            return stt[:, i * 8:(i + 1) * 8], ("st", i)

        def a_pf():
            i = rr("pf", 6)
            return psf[i], ("pf", i)

        def a_pb():
            i = rr("pb", 2)
            return psb[i], ("pb", i)

        cv_next = {}

        def conv_chunks(mat):
            return WROWS[mat] // CVROWS[mat]

        def record_conversion(mat, l, ci):
            r0 = ci * CVROWS[mat]
            r1 = r0 + CVROWS[mat]
            S.op("pool", lambda e: e.dma_start(out=wb[mat][l, r0:r1, :], in_=wsrc[mat][l, r0:r1, :]),
                 reads=[], writes=[("wb", mat, l, ci), "cvchain"], dma="cv")

        def ensure_converted(mat, l, r0, r1):
            c0 = r0 // CVROWS[mat]
            c1 = (r1 - 1) // CVROWS[mat]
            nxt = cv_next.get((mat, l), 0)
            while nxt <= c1:
                record_conversion(mat, l, nxt)
                nxt += 1
            cv_next[(mat, l)] = nxt
            return [("wb", mat, l, c) for c in range(c0, c1 + 1)]

        conv_order = []
        for l in range(depth):
            for mat in ("in", "out", "up", "down"):
                for ci in range(conv_chunks(mat)):
                    conv_order.append((mat, l, ci))
        conv_pos = [0]

        def pump_conversions(n, upto_layer):
            k = 0
            while k < n and conv_pos[0] < len(conv_order):
                mat, l, ci = conv_order[conv_pos[0]]
                if l > upto_layer:
                    break
                if cv_next.get((mat, l), 0) <= ci:
                    ensure_converted(mat, l, ci * CVROWS[mat], (ci + 1) * CVROWS[mat])
                    k += 1
                conv_pos[0] += 1

        def load_piece(mat, l, rows, cols, view):
            r0, r1 = rows
            c0, c1 = cols
            res = ensure_converted(mat, l, r0, r1)
            i = rr("ring", NRING)
            nchunk = (r1 - r0) // 128
            ncol = c1 - c0
            src = wb[mat][l, r0:r1, c0:c1].rearrange("(c p) n -> p c n", p=128)
            dst = ring[i][:].rearrange("p c n -> p (c n)")[:, 0:nchunk * ncol].rearrange("p (c n) -> p c n", n=ncol)
            S.op("sp", lambda e: e.dma_start(out=dst, in_=src), reads=res, writes=[("ring", i)], dma=f"ring{i}")
            return dst, ("ring", i)

        S.op("sp", lambda e: e.dma_start(out=ident[:], in_=identd), writes=["ident"], dma="c0")
        S.op("sp", lambda e: e.dma_start(out=rope[:], in_=roped), writes=["rope"], dma="c1")
        S.op("dve", lambda e: e.tensor_copy(out=identb[:], in_=ident[:]), reads=["ident"], writes=["identb"])
        S.op("dve", lambda e: e.memset(onesf[:], 1.0), writes=["onesf"])
        S.op("dve", lambda e: e.memset(onesb[:], 1.0), writes=["onesb"])
        lA, rA = a_tf()
        lB, rB = a_tf()
        for j, (src, dstt, off) in enumerate([(lam_q1, lA, 0), (lam_q2, lA, 256), (lam_k1, lB, 0), (lam_k2, lB, 256)]):
            S.op("sp", lambda e, src=src, dstt=dstt, off=off: e.dma_start(
                out=dstt[:, off:off + 256], in_=src.rearrange("l d -> (l d)").partition_broadcast(128)),
                writes=[(rA if dstt is lA else rB)], dma=f"c{2 + j}")
        S.op("dve", lambda e: e.tensor_tensor(out=lA[:], in0=lA[:], in1=lB[:], op=ALU.mult), reads=[rA, rB], writes=[rA])
        S.op("dve", lambda e: e.tensor_reduce(out=lamt[:, 0:8], in_=lA[:].rearrange("p (g d) -> p g d", d=64), axis=AX.X, op=ALU.add),
             reads=[rA], writes=["lamt"])
        S.op("act", lambda e: e.activation(out=lamt[:, 0:8], in_=lamt[:, 0:8], func=AF.Exp), reads=["lamt"], writes=["lamt"])
        S.op("dve", lambda e: e.tensor_tensor(out=lamt[:, 8:12], in0=lamt[:, 0:4], in1=lamt[:, 4:8], op=ALU.subtract),
             reads=["lamt"], writes=["lamt"])
        for l in range(DEPTH):
            S.op("dve", lambda e, l=l: e.tensor_scalar(out=lamt[:, 12 + l:13 + l], in0=lamt[:, 8 + l:9 + l], scalar1=-1.0,
                                                       scalar2=-lam_init(l), op0=ALU.mult, op1=ALU.add),
                 reads=["lamt"], writes=["lamt"])

        def load_layer_consts(l):
            pstage = ded[2][:, 0:384].rearrange("p (g w) -> p g w", w=128)
            cT = constT[l % 2]
            rc = ("constT", 0)
            g = gqk[l % 2]
            rg = ("gqk", 0)
            S.op("dve", lambda e: e.memset(pstage[:], 0.0), writes=[("ded", 2)])
            loads = [
                (pstage[0:16, 0, :], norm1_g[l].rearrange("(c p) -> c p", p=128)),
                (pstage[16:32, 0, :], norm2_g[l].rearrange("(c p) -> c p", p=128)),
                (pstage[32:40, 0, :], conv_b[l].rearrange("(c p) -> c p", p=128)),
                (pstage[40:48, 0, :], conv_ln_g[l].rearrange("(c p) -> c p", p=128)),
                (pstage[48:56, 0, :], conv_ln_b[l].rearrange("(c p) -> c p", p=128)),
                (pstage[56:57, 0, :], attn_norm_g[l].rearrange("(c p) -> c p", p=128)),
                (pstage[0:128, 1, :], conv_w[l].rearrange("j (c p) -> (j c) p", p=128)[0:128, :]),
                (pstage[0:120, 2, :], conv_w[l].rearrange("j (c p) -> (j c) p", p=128)[128:248, :]),
            ]
            for (o, i_) in loads:
                S.op("sp", lambda e, o=o, i_=i_: e.dma_start(out=o, in_=i_), writes=[("ded", 2)], dma="pst")
            pf, rpf = a_pf()
            for gi in range(3):
                S.op("pe", lambda e, gi=gi: e.transpose(out=pf[:, gi * 128:(gi + 1) * 128], in_=pstage[:, gi, :], identity=ident[:]),
                     reads=[("ded", 2), "ident"], writes=[rpf])
            S.op("dve", lambda e: e.tensor_copy(out=cT[:], in_=pf[:, 0:384]), reads=[rpf], writes=[rc])
            S.op("sp", lambda e: e.dma_start(out=g[:, 0:64], in_=q_norm_g[l].partition_broadcast(128)), writes=[rg], dma="gq")
            S.op("sp", lambda e: e.dma_start(out=g[:, 64:128], in_=k_norm_g[l].partition_broadcast(128)), writes=[rg], dma="gq")
            S.op("dve", lambda e: e.tensor_scalar(out=gAs[:, l:l + 1], in0=cT[:, 56:57], scalar1=1.0 - lam_init(l), scalar2=None,
                                                  op0=ALU.mult), reads=[rc], writes=[("gAs", l)])

        def rmsnorm_to_big(l, i, gcol0, rc):
            cT = constT[l % 2]
            x = xres[:, i, :]
            xi = rr("xn", 1)
            xb, rxb = xn[xi], ("xn", xi)
            stv, rst = a_st()
            S.op("act", lambda e: e.activation(out=xb[:], in_=x, func=AF.Square, accum_out=stv[:, 0:1]),
                 reads=[("xres", i)], writes=[rxb, rst])
            S.op("act", lambda e: e.activation(out=stv[:, 1:2], in_=stv[:, 0:1], func=AF.Sqrt, scale=1.0 / D, bias=EPS),
                 reads=[rst], writes=[rst])
            S.op("dve", lambda e: e.reciprocal(out=stv[:, 2:3], in_=stv[:, 1:2]), reads=[rst], writes=[rst])
            S.op("act", lambda e: e.activation(out=xb[:], in_=x, func=AF.Copy, scale=stv[:, 2:3]),
                 reads=[("xres", i), rst], writes=[rxb])
            for half in range(2):
                pb, rpb = a_pb()
                for c8 in range(8):
                    c = half * 8 + c8
                    S.op("pe", lambda e, c=c, c8=c8, pb=pb: e.transpose(out=pb[:, c8 * 128:(c8 + 1) * 128],
                                                                      in_=xb[:, c * 128:(c + 1) * 128], identity=identb[:]),
                         reads=[rxb, "identb"], writes=[rpb])
                S.op("dve", lambda e, half=half, pb=pb: e.tensor_tensor(
                    out=big[:, half * 8:(half + 1) * 8, i * 128:(i + 1) * 128],
                    in0=pb[:].rearrange("p (c t) -> p c t", t=128),
                    in1=cT[:, gcol0 + half * 8:gcol0 + (half + 1) * 8].unsqueeze(2).to_broadcast([128, 8, 128]),
                    op=ALU.mult), reads=[rpb, rc], writes=[("big", half * 8 + c8, i) for c8 in range(8)])

        def kslot(t):
            if t.kind == "M":
                return 0, 0
            if t.kind == "S":
                return NMETA + 128 * t.g, 1 + t.g
            if t.kind == "A":
                return NMETA + PAST, 17
            return NMETA + PAST + DSEQ, 18

        def kv_out_aps(l, t, which, c0):
            if t.kind == "S":
                dst = (kp if which == "k" else vp)[l, t.s, NMETA + 128 * t.g:NMETA + 128 * (t.g + 1), c0:c0 + 512]
                return [(dst, 128)]
            if t.kind == "M":
                o = kp if which == "k" else vp
                return [(o[l, 0, 0:NMETA, c0:c0 + 512], NMETA), (o[l, 1, 0:NMETA, c0:c0 + 512], NMETA)]
            o = kso if which == "k" else vso
            return [(o[l, t.s, :, c0:c0 + 512], DSEQ)]

        def attend(l, h, qc0, nq, entries):
            T1, T2 = ded[0], ded[1]
            for s in range(2):
                par = rr("acc", 2)
                po, rpo = psf[2 + 2 * par], ("pf", 2 + 2 * par)
                pm, rpm = psf[3 + 2 * par], ("pf", 3 + 2 * par)
                ne = len(entries)
                for ei, (kc0, nk, vtile, qoff, diag) in enumerate(entries):
                    n = nq - qoff
                    sci = rr("sc", 2)
                    psc, rsc = psf[sci], ("pf", sci)
                    S.op("pe", lambda e, s=s, kc0=kc0, nk=nk, qoff=qoff, n=n, psc=psc: e.matmul(
                        psc[0:nk, 0:n], lhsT=kT[64 * s:64 * s + 64, h, kc0:kc0 + nk],
                        rhs=qT[64 * s:64 * s + 64, h, qc0 + qoff:qc0 + nq], start=True, stop=True),
                        reads=[("kT", h), ("qT", h)], writes=[rsc])
                    pt, rpt = a_tb()
                    S.op("act", lambda e, nk=nk, n=n, psc=psc, pt=pt: e.activation(out=pt[0:nk, 0:n], in_=psc[0:nk, 0:n],
                                                                                 func=AF.Exp, scale=0.125),
                         reads=[rsc], writes=[rpt])
                    if diag:
                        S.op("dve", lambda e, pt=pt: e.memset(pt[64:128, 0:64], 0.0), writes=[rpt])
                    S.op("pe", lambda e, nk=nk, n=n, vtile=vtile, qoff=qoff, pt=pt, ei=ei, po=po: e.matmul(
                        po[:, qoff:nq], lhsT=vS[0:nk, vtile, h * 128:(h + 1) * 128], rhs=pt[0:nk, 0:n],
                        start=(ei == 0), stop=(ei == ne - 1)), reads=[("vS", vtile), rpt], writes=[rpo])
                    S.op("pe", lambda e, nk=nk, n=n, qoff=qoff, pt=pt, ei=ei, pm=pm: e.matmul(
                        pm[:, qoff:nq], lhsT=onesb[0:nk, :], rhs=pt[0:nk, 0:n],
                        start=(ei == 0), stop=(ei == ne - 1)), reads=["onesb", rpt], writes=[rpm])
                rt, rrt = a_tf()
                S.op("dve", lambda e, pm=pm, rt=rt: e.reciprocal(out=rt[:, 0:nq], in_=pm[:, 0:nq]), reads=[rpm], writes=[rrt])
                Tt = T1 if s == 0 else T2
                S.op("dve", lambda e, po=po, rt=rt, Tt=Tt: e.tensor_tensor(out=Tt[:, 0:nq], in0=po[:, 0:nq], in1=rt[:, 0:nq], op=ALU.mult),
                     reads=[rpo, rrt], writes=[("ded", 0 if s == 0 else 1)])
            S.op("dve", lambda e: e.scalar_tensor_tensor(out=T1[:, 0:nq], in0=T2[:, 0:nq], scalar=lamt[:, 12 + l:13 + l], in1=T1[:, 0:nq],
                                                         op0=ALU.mult, op1=ALU.add),
                 reads=[("ded", 0), ("ded", 1), "lamt"], writes=[("ded", 0)])
            S.op("act", lambda e: e.activation(out=T2[:, 0:nq], in_=T1[:, 0:nq], func=AF.Square), reads=[("ded", 0)], writes=[("ded", 1)])
            sci = rr("sc", 2)
            psc, rsc = psf[sci], ("pf", sci)
            S.op("pe", lambda e: e.matmul(psc[:, 0:nq], lhsT=onesf[:], rhs=T2[:, 0:nq], start=True, stop=True),
                 reads=["onesf", ("ded", 1)], writes=[rsc])
            S.op("act", lambda e: e.activation(out=T2[:, 0:nq], in_=psc[:, 0:nq], func=AF.Sqrt, scale=1.0 / 128, bias=EPS),
                 reads=[rsc], writes=[("ded", 1)])
            S.op("dve", lambda e: e.reciprocal(out=T2[:, 0:nq], in_=T2[:, 0:nq]), reads=[("ded", 1)], writes=[("ded", 1)])
            S.op("dve", lambda e: e.tensor_tensor(out=T1[:, 0:nq], in0=T1[:, 0:nq], in1=T2[:, 0:nq], op=ALU.mult),
                 reads=[("ded", 0), ("ded", 1)], writes=[("ded", 0)])
            ti0, ti1 = qc0 // 128, (qc0 + nq - 1) // 128
            S.op("act", lambda e: e.activation(out=big[:, h, qc0:qc0 + nq], in_=T1[:, 0:nq], func=AF.Copy, scale=gAs[:, l:l + 1]),
                 reads=[("ded", 0), ("gAs", l)], writes=[("big", h, ti) for ti in range(ti0, ti1 + 1)])

        def run_block(l, bi, tiles):
            NT = len(tiles)
            NTOK = NT * 128
            isx = tiles[0].kind != "S"
            cT = constT[l % 2]
            rc = ("constT", 0)
            last = (l == depth - 1)

            for i, t in enumerate(tiles):
                if l == 0:
                    if t.kind == "S":
                        src, nv = xp[t.s, t.g * 128:(t.g + 1) * 128, :], 128
                    elif t.kind == "M":
                        src, nv = meta, NMETA
                    else:
                        src, nv = xs[t.s], DSEQ
                    if nv < 128:
                        S.op("dve", lambda e, i=i: e.memset(xres[:, i, :], 0.0), writes=[("xres", i)])
                else:
                    src, nv = xscr[t.tid * 128:(t.tid + 1) * 128, :], 128
                S.op("sp", lambda e, i=i, src=src, nv=nv: e.dma_start(out=xres[0:nv, i, :], in_=src),
                     writes=[("xres", i)], dma=f"x{i}")
                rmsnorm_to_big(l, i, 0, rc)
            bigall = [("big", c, i) for c in range(16) for i in range(NT)]

            def bigk(kc):
                return [("big", kc, i) for i in range(NT)]

            if isx:
                segs = [(i * 158, 128, i * 128) for i in range(NT)]
                SW = 158
            else:
                segs = [(0, 512, 0)]
                SW = 542
            NS = len(segs)

            allg = [("gT", c) for c in range(8)]
            if isx:
                for i, t in enumerate(tiles):
                    h0 = segs[i][0]
                    if t.kind == "M":
                        S.op("dve", lambda e, h0=h0: e.memset(gT[:, :, h0:h0 + 30], 0.0), writes=allg)
                    else:
                        for half in range(2):
                            stg, rstg = a_tf()
                            S.op("sp", lambda e, t=t, half=half, stg=stg: e.dma_start(
                                out=stg[0:30, :], in_=sc[l, t.s, :, half * 512:(half + 1) * 512]), writes=[rstg], dma=f"tf{rstg[1]}")
                            pf, rpf = a_pf()
                            for c4 in range(4):
                                S.op("pe", lambda e, c4=c4, stg=stg, pf=pf: e.transpose(
                                    out=pf[:, c4 * 32:c4 * 32 + 30], in_=stg[0:30, c4 * 128:(c4 + 1) * 128], identity=ident[0:30, 0:30]),
                                    reads=[rstg, "ident"], writes=[rpf])
                            S.op("dve", lambda e, half=half, h0=h0, pf=pf: e.tensor_copy(
                                out=gT[:, half * 4:(half + 1) * 4, h0:h0 + 30],
                                in_=pf[:, 0:128].rearrange("p (c w) -> p c w", w=32)[:, :, 0:30]),
                                reads=[rpf], writes=[("gT", half * 4 + c4) for c4 in range(4)])
            else:
                if tiles[0].g == 0:
                    S.op("dve", lambda e: e.memset(gT[:, :, 0:14], 0.0), writes=allg)
                    S.op("dve", lambda e: e.tensor_copy(out=gT[:, :, 14:30], in_=metaG[:]), reads=["metaG"], writes=allg)
                else:
                    S.op("dve", lambda e: e.tensor_copy(out=gT[:, :, 0:30], in_=hist[:]), reads=["hist"], writes=allg)

            for j in range(4):
                wa, rwa = load_piece("in", l, (0, D), (3072 + j * 256, 3072 + (j + 1) * 256), "kc")
                wg, rwg = load_piece("in", l, (0, D), (4096 + j * 256, 4096 + (j + 1) * 256), "kc")
                for cc in range(2):
                    c = 2 * j + cc
                    pa, rpa = a_pf()
                    pg, rpg = a_pf()
                    for (wt, rw, pp, rp) in ((wa, rwa, pa, rpa), (wg, rwg, pg, rpg)):
                        for kc in range(16):
                            S.op("pe", lambda e, wt=wt, pp=pp, kc=kc, cc=cc: e.matmul(
                                pp[:, 0:NTOK], lhsT=wt[:, kc, cc * 128:(cc + 1) * 128], rhs=big[:, kc, 0:NTOK],
                                start=(kc == 0), stop=(kc == 15)), reads=[rw] + bigk(kc), writes=[rp])
                    sg, rsg = a_tf()
                    S.op("act", lambda e, pg=pg, sg=sg: e.activation(out=sg[:, 0:NTOK], in_=pg[:, 0:NTOK], func=AF.Sigmoid),
                         reads=[rpg], writes=[rsg])
                    S.op("dve", lambda e, pa=pa, sg=sg, c=c: e.tensor_tensor(
                        out=gT[:, c, 0:NS * SW].rearrange("p (s w) -> p s w", w=SW)[:, :, 30:30 + NTOK // NS],
                        in0=pa[:, 0:NTOK].rearrange("p (s w) -> p s w", s=NS),
                        in1=sg[:, 0:NTOK].rearrange("p (s w) -> p s w", s=NS), op=ALU.mult),
                        reads=[rpa, rsg], writes=[("gT", c)])

            def state_out(dst, col0):
                for half in range(2):
                    pf, rpf = a_pf()
                    for c4 in range(4):
                        c = half * 4 + c4
                        S.op("pe", lambda e, c=c, c4=c4, pf=pf: e.transpose(
                            out=pf[0:30, c4 * 128:(c4 + 1) * 128], in_=gT[:, c, col0:col0 + 30], identity=ident[:]),
                            reads=[("gT", c), "ident"], writes=[rpf])
                    stg, rstg = a_tf()
                    S.op("act", lambda e, pf=pf, stg=stg: e.activation(out=stg[0:30, :], in_=pf[0:30, :], func=AF.Copy),
                         reads=[rpf], writes=[rstg])
                    S.op("act", lambda e, half=half, stg=stg: e.dma_start(out=dst[:, half * 512:(half + 1) * 512], in_=stg[0:30, :]),
                         reads=[rstg], dma=f"tf{rstg[1]}")

            if isx:
                for i, t in enumerate(tiles):
                    h0 = segs[i][0]
                    if t.kind == "M":
                        S.op("dve", lambda e, h0=h0: e.tensor_copy(out=metaG[:], in_=gT[:, :, h0 + 30:h0 + 46]), reads=allg, writes=["metaG"])
                    else:
                        state_out(cso[l, t.s], h0 + 30 + 34)
            else:
                if tiles[0].g == 12:
                    state_out(cp[l, tiles[0].s], 512)
                else:
                    S.op("dve", lambda e: e.tensor_copy(out=hist[:], in_=gT[:, :, 512:542]), reads=allg, writes=["hist"])

            for cb in range(6):
                w0, rw0 = load_piece("in", l, (0, 1024), (cb * 512, (cb + 1) * 512), "kc")
                w1, rw1 = load_piece("in", l, (1024, D), (cb * 512, (cb + 1) * 512), "kc")
                for i, t in enumerate(tiles):
                    pz, rpz = a_pf()
                    for kc in range(16):
                        wt, rw = (w0, rw0) if kc < 8 else (w1, rw1)
                        S.op("pe", lambda e, wt=wt, kc=kc, pz=pz, i=i: e.matmul(
                            pz[:], lhsT=big[:, kc, i * 128:(i + 1) * 128], rhs=wt[:, kc % 8, :],
                            start=(kc == 0), stop=(kc == 15)), reads=[rw, ("big", kc, i)], writes=[rpz])
                    kc0, vt = kslot(t)
                    if cb < 4:
                        isk = cb >= 2
                        h0 = 4 * (cb % 2)
                        sq, rsq = a_tf()
                        S.op("act", lambda e, pz=pz, sq=sq: e.activation(out=sq[:], in_=pz[:], func=AF.Square), reads=[rpz], writes=[rsq])
                        stv, rst = a_st()
                        S.op("dve", lambda e, sq=sq, stv=stv: e.tensor_reduce(out=stv, in_=sq[:].rearrange("p (g d) -> p g d", d=64),
                                                                            axis=AX.X, op=ALU.add), reads=[rsq], writes=[rst])
                        S.op("act", lambda e, stv=stv: e.activation(out=stv, in_=stv, func=AF.Sqrt, scale=1.0 / 64, bias=EPS),
                             reads=[rst], writes=[rst])
                        S.op("dve", lambda e, stv=stv: e.reciprocal(out=stv, in_=stv), reads=[rst], writes=[rst])
                        z, rz = a_tf()
                        z3 = z[:].rearrange("p (g d) -> p g d", d=64)
                        S.op("dve", lambda e, pz=pz, z3=z3, stv=stv: e.tensor_tensor(
                            out=z3, in0=pz[:].rearrange("p (g d) -> p g d", d=64),
                            in1=stv.unsqueeze(2).to_broadcast([128, 8, 64]), op=ALU.mult), reads=[rpz, rst], writes=[rz])
                        gsl = gqk[l % 2][:, 64:128] if isk else gqk[l % 2][:, 0:64]
                        S.op("dve", lambda e, z3=z3, gsl=gsl: e.tensor_tensor(
                            out=z3, in0=z3, in1=gsl.unsqueeze(1).to_broadcast([128, 8, 64]), op=ALU.mult),
                            reads=[rz, ("gqk", 0)], writes=[rz])
                        cs = rope[:, t.ropei, 0:16].unsqueeze(1).to_broadcast([128, 8, 16])
                        sn = rope[:, t.ropei, 16:24].unsqueeze(1).to_broadcast([128, 8, 8])
                        sp_ = rope[:, t.ropei, 24:32].unsqueeze(1).to_broadcast([128, 8, 8])
                        S.op("dve", lambda e, z3=z3, sn=sn: e.tensor_tensor(out=ropet[:, :, 0:8], in0=z3[:, :, 8:16], in1=sn, op=ALU.mult),
                             reads=[rz, "rope"], writes=["ropet"])
                        S.op("dve", lambda e, z3=z3, sp_=sp_: e.tensor_tensor(out=ropet[:, :, 8:16], in0=z3[:, :, 0:8], in1=sp_, op=ALU.mult),
                             reads=[rz, "rope"], writes=["ropet"])
                        S.op("dve", lambda e, z3=z3, cs=cs: e.tensor_tensor(out=z3[:, :, 0:16], in0=z3[:, :, 0:16], in1=cs, op=ALU.mult),
                             reads=[rz, "rope"], writes=[rz])
                        S.op("dve", lambda e, z3=z3: e.tensor_tensor(out=z3[:, :, 0:16], in0=z3[:, :, 0:16], in1=ropet[:], op=ALU.add),
                             reads=[rz, "ropet"], writes=[rz])
                        zb, rzb = a_tb()
                        S.op("act", lambda e, z=z, zb=zb: e.activation(out=zb[:], in_=z[:], func=AF.Copy), reads=[rz], writes=[rzb])
                        if isk:
                            for (dst, nr) in kv_out_aps(l, t, "k", (cb % 2) * 512):
                                S.op("act", lambda e, dst=dst, nr=nr, z=z: e.dma_start(out=dst, in_=z[0:nr, :]), reads=[rz], dma=f"tf{rz[1]}")
                        pb, rpb = a_pb()
                        for hh in range(4):
                            S.op("pe", lambda e, hh=hh, zb=zb, pb=pb: e.transpose(out=pb[:, hh * 128:(hh + 1) * 128],
                                                                                in_=zb[:, hh * 128:(hh + 1) * 128], identity=identb[:]),
                                 reads=[rzb, "identb"], writes=[rpb])
                        if isk:
                            nv = t.nv
                            S.op("act", lambda e, pb=pb, h0=h0, kc0=kc0, nv=nv: e.activation(
                                out=kT[:, h0:h0 + 4, kc0:kc0 + nv],
                                in_=pb[:, 0:512].rearrange("p (h t) -> p h t", t=128)[:, :, 0:nv], func=AF.Copy),
                                reads=[rpb], writes=[("kT", h0 + hh) for hh in range(4)])
                        else:
                            S.op("act", lambda e, pb=pb, h0=h0, i=i: e.activation(
                                out=qT[:, h0:h0 + 4, i * 128:(i + 1) * 128],
                                in_=pb[:, 0:512].rearrange("p (h t) -> p h t", t=128), func=AF.Copy),
                                reads=[rpb], writes=[("qT", h0 + hh) for hh in range(4)])
                    else:
                        c0 = (cb % 2) * 512
                        vf, rvf = a_tf()
                        S.op("act", lambda e, pz=pz, vf=vf: e.activation(out=vf[:], in_=pz[:], func=AF.Copy), reads=[rpz], writes=[rvf])
                        S.op("dve", lambda e, pz=pz, vt=vt, c0=c0: e.tensor_copy(out=vS[:, vt, c0:c0 + 512], in_=pz[:]),
                             reads=[rpz], writes=[("vS", vt)])
                        for (dst, nr) in kv_out_aps(l, t, "v", c0):
                            S.op("act", lambda e, dst=dst, nr=nr, vf=vf: e.dma_start(out=dst, in_=vf[0:nr, :]), reads=[rvf], dma=f"tf{rvf[1]}")

            nseg_tok = NTOK // NS
            for c in range(8):
                yt, ryt = ded[2], ("ded", 2)
                yv = yt[:, 0:NTOK].rearrange("p (s w) -> p s w", s=NS)

                def gview(j):
                    return gT[:, c, 0:NS * SW].rearrange("p (s w) -> p s w", w=SW)[:, :, j:j + nseg_tok]

                wc = lambda j: cT[:, 128 + j * 8 + c:128 + j * 8 + c + 1]
                S.op("dve", lambda e, c=c, yv=yv, g0=gview(0), w0=wc(0): e.tensor_scalar(
                    out=yv, in0=g0, scalar1=w0, scalar2=cT[:, 32 + c:33 + c], op0=ALU.mult, op1=ALU.add),
                    reads=[("gT", c), rc], writes=[ryt])
                for j in range(1, CONVK):
                    outv = yv if j < CONVK - 1 else gview(30)
                    S.op("dve", lambda e, yv=yv, gj=gview(j), wj=wc(j), outv=outv: e.scalar_tensor_tensor(
                        out=outv, in0=gj, scalar=wj, in1=yv, op0=ALU.mult, op1=ALU.add),
                        reads=[("gT", c), rc, ryt], writes=[ryt] if j < CONVK - 1 else [("gT", c)])

            def ytok(c):
                return gT[:, c, 0:NS * SW].rearrange("p (s w) -> p s w", w=SW)[:, :, 30:30 + nseg_tok]

            pm_, rpm_ = a_pf()
            pq_, rpq_ = a_pf()
            pm3 = pm_[:, 0:NTOK].rearrange("p (s w) -> p s w", s=NS)
            pq3 = pq_[:, 0:NTOK].rearrange("p (s w) -> p s w", s=NS)
            for c in range(8):
                ysq, rysq = a_tf()
                ysq3 = ysq[:, 0:NTOK].rearrange("p (s w) -> p s w", s=NS)
                S.op("act", lambda e, c=c, ysq3=ysq3: e.activation(out=ysq3, in_=ytok(c), func=AF.Square), reads=[("gT", c)], writes=[rysq])
                S.op("pe", lambda e, c=c: e.matmul(pm3, lhsT=onesf[:], rhs=ytok(c), start=(c == 0), stop=(c == 7)),
                     reads=["onesf", ("gT", c)], writes=[rpm_])
                S.op("pe", lambda e, c=c, ysq3=ysq3: e.matmul(pq3, lhsT=onesf[:], rhs=ysq3, start=(c == 0), stop=(c == 7)),
                     reads=["onesf", rysq], writes=[rpq_])
            MEAN, RSTD = ded[0], ded[1]
            S.op("dve", lambda e: e.tensor_scalar(out=MEAN[:, 0:NTOK], in0=pm_[:, 0:NTOK], scalar1=1.0 / 1024, scalar2=None, op0=ALU.mult),
                 reads=[rpm_], writes=[("ded", 0)])
            S.op("dve", lambda e: e.tensor_tensor(out=RSTD[:, 0:NTOK], in0=MEAN[:, 0:NTOK], in1=MEAN[:, 0:NTOK], op=ALU.mult),
                 reads=[("ded", 0)], writes=[("ded", 1)])
            S.op("dve", lambda e: e.scalar_tensor_tensor(out=RSTD[:, 0:NTOK], in0=pq_[:, 0:NTOK], scalar=1.0 / 1024, in1=RSTD[:, 0:NTOK],
                                                         op0=ALU.mult, op1=ALU.subtract), reads=[rpq_, ("ded", 1)], writes=[("ded", 1)])
            S.op("dve", lambda e: e.tensor_scalar(out=RSTD[:, 0:NTOK], in0=RSTD[:, 0:NTOK], scalar1=0.0, scalar2=None, op0=ALU.max),
                 reads=[("ded", 1)], writes=[("ded", 1)])
            S.op("act", lambda e: e.activation(out=RSTD[:, 0:NTOK], in_=RSTD[:, 0:NTOK], func=AF.Sqrt, bias=EPS), reads=[("ded", 1)], writes=[("ded", 1)])
            S.op("dve", lambda e: e.reciprocal(out=RSTD[:, 0:NTOK], in_=RSTD[:, 0:NTOK]), reads=[("ded", 1)], writes=[("ded", 1)])
            mean3 = MEAN[:, 0:NTOK].rearrange("p (s w) -> p s w", s=NS)
            rstd3 = RSTD[:, 0:NTOK].rearrange("p (s w) -> p s w", s=NS)
            cbuf = []
            for c in range(8):
                n_, rn_ = a_tf()
                n3 = n_[:, 0:NTOK].rearrange("p (s w) -> p s w", s=NS)
                S.op("dve", lambda e, c=c, n3=n3: e.tensor_tensor(out=n3, in0=ytok(c), in1=mean3, op=ALU.subtract),
                     reads=[("gT", c), ("ded", 0)], writes=[rn_])
                S.op("dve", lambda e, n3=n3: e.tensor_tensor(out=n3, in0=n3, in1=rstd3, op=ALU.mult), reads=[rn_, ("ded", 1)], writes=[rn_])
                cbuf.append((c, n_, rn_))
                S.op("act", lambda e, c=c, n_=n_: e.activation(out=big[:, 8 + c, 0:NTOK], in_=n_[:, 0:NTOK], func=AF.Silu,
                                                              scale=cT[:, 40 + c:41 + c], bias=cT[:, 48 + c:49 + c]),
                     reads=[rn_, rc], writes=[("big", 8 + c, i) for i in range(NT)])

            if isx:
                for i, t in enumerate(tiles):
                    if t.kind == "M":
                        continue
                    S.op("pool", lambda e, t=t: e.dma_start(out=vS[:, 1:17, :], in_=cv[l, t.s].rearrange("(t p) n -> p t n", p=128)),
                         writes=[("vS", 1 + j) for j in range(16)], dma="cvl")
                    for j in range(16):
                        ki = rr("kst", 2)
                        S.op("pool", lambda e, t=t, j=j, ki=ki: e.dma_start(out=kst[ki][:], in_=ck[l, t.s, j * 128:(j + 1) * 128, :]),
                             writes=[("kst", ki)], dma=f"kst{ki}")
                        pb, rpb = a_pb()
                        for hh in range(8):
                            S.op("pe", lambda e, hh=hh, ki=ki, pb=pb: e.transpose(out=pb[:, hh * 128:(hh + 1) * 128],
                                                                                in_=kst[ki][:, hh * 128:(hh + 1) * 128], identity=identb[:]),
                                 reads=[("kst", ki), "identb"], writes=[rpb])
                        S.op("dve" if j % 2 else "act", (lambda e, pb=pb, j=j: e.tensor_copy(
                            out=kT[:, :, NMETA + j * 128:NMETA + (j + 1) * 128], in_=pb[:].rearrange("p (h t) -> p h t", t=128)))
                            if j % 2 else (lambda e, pb=pb, j=j: e.activation(
                                out=kT[:, :, NMETA + j * 128:NMETA + (j + 1) * 128], in_=pb[:].rearrange("p (h t) -> p h t", t=128), func=AF.Copy)),
                            reads=[rpb], writes=[("kT", hh) for hh in range(8)])
                    kc0, vt = kslot(t)
                    entries = [(0, NMETA, 0, 0, False)] + [(NMETA + 128 * j, 128, 1 + j, 0, False) for j in range(16)] + [(kc0, DSEQ, vt, 0, False)]
                    for h in range(NH):
                        attend(l, h, i * 128, DSEQ, entries)
                mi = [i for i, t in enumerate(tiles) if t.kind == "M"][0]
                for h in range(NH):
                    attend(l, h, mi * 128, NMETA, [(0, NMETA, 0, 0, False)])
                for i, t in enumerate(tiles):
                    S.op("dve", lambda e, i=i, t=t: e.memset(big[:, 0:8, i * 128 + t.nv:(i + 1) * 128], 0.0),
                         writes=[("big", h, i) for h in range(8)])
            else:
                g0 = tiles[0].g
                entries = [(0, NMETA, 0, 0, False)] + [(NMETA + 128 * g, 128, 1 + g, 0, False) for g in range(g0)]
                entries += [(NMETA + 128 * (g0 + i), 128, 1 + g0 + i, 128 * i, True) for i in range(NT)]
                for h in range(NH):
                    attend(l, h, 0, NTOK, entries)

            for cb in range(4):
                w0, rw0 = load_piece("out", l, (0, 1024), (cb * 512, (cb + 1) * 512), "kc")
                w1, rw1 = load_piece("out", l, (1024, D), (cb * 512, (cb + 1) * 512), "kc")
                for i in range(NT):
                    pz, rpz = a_pf()
                    for kc in range(16):
                        wt, rw = (w0, rw0) if kc < 8 else (w1, rw1)
                        S.op("pe", lambda e, wt=wt, kc=kc, pz=pz, i=i: e.matmul(
                            pz[:], lhsT=big[:, kc, i * 128:(i + 1) * 128], rhs=wt[:, kc % 8, :],
                            start=(kc == 0), stop=(kc == 15)), reads=[rw, ("big", kc, i)], writes=[rpz])
                    S.op("dve", lambda e, pz=pz, i=i, cb=cb: e.tensor_tensor(out=xres[:, i, cb * 512:(cb + 1) * 512], in0=pz[:],
                                                                          in1=xres[:, i, cb * 512:(cb + 1) * 512], op=ALU.add),
                         reads=[rpz, ("xres", i)], writes=[("xres", i)])
            for i in range(NT):
                rmsnorm_to_big(l, i, 16, rc)

            NG = 16

            def mlp_up(g):
                w0, rw0 = load_piece("up", l, (0, 1024), (g * 512, (g + 1) * 512), "kc")
                w1, rw1 = load_piece("up", l, (1024, D), (g * 512, (g + 1) * 512), "kc")
                for fc in range(4):
                    pu, rpu = a_pf()
                    for kc in range(16):
                        wt, rw = (w0, rw0) if kc < 8 else (w1, rw1)
                        S.op("pe", lambda e, wt=wt, kc=kc, fc=fc, pu=pu: e.matmul(
                            pu[:, 0:NTOK], lhsT=wt[:, kc % 8, fc * 128:(fc + 1) * 128], rhs=big[:, kc, 0:NTOK],
                            start=(kc == 0), stop=(kc == 15)), reads=[rw] + bigk(kc), writes=[rpu])
                    sq, rsq = a_tf()
                    S.op("act", lambda e, pu=pu, sq=sq: e.activation(out=sq[:, 0:NTOK], in_=pu[:, 0:NTOK], func=AF.Square),
                         reads=[rpu], writes=[rsq])
                    S.op("dve", lambda e, pu=pu, sq=sq, fc=fc: e.scalar_tensor_tensor(
                        out=qT[:, (g % 2) * 4 + fc, 0:NTOK], in0=pu[:, 0:NTOK], scalar=0.0, in1=sq[:, 0:NTOK], op0=ALU.is_gt, op1=ALU.mult),
                        reads=[rpu, rsq], writes=[("qT", (g % 2) * 4 + fc)])

            def mlp_down(g):
                halves = []
                for ch in range(2):
                    wd, rwd = load_piece("down", l, (g * 512, (g + 1) * 512), (ch * 1024, (ch + 1) * 1024), "kc")
                    halves.append((wd, rwd))
                for i in range(NT):
                    for cb in range(4):
                        wd, rwd = halves[cb // 2]
                        pz, rpz = a_pf()
                        for fc in range(4):
                            S.op("pe", lambda e, wd=wd, fc=fc, pz=pz, i=i, cb=cb: e.matmul(
                                pz[:], lhsT=qT[:, (g % 2) * 4 + fc, i * 128:(i + 1) * 128],
                                rhs=wd[:, fc, (cb % 2) * 512:(cb % 2 + 1) * 512], start=(fc == 0), stop=(fc == 3)),
                                reads=[rwd, ("qT", (g % 2) * 4 + fc)], writes=[rpz])
                        S.op("dve", lambda e, pz=pz, i=i, cb=cb: e.tensor_tensor(out=xres[:, i, cb * 512:(cb + 1) * 512], in0=pz[:],
                                                                              in1=xres[:, i, cb * 512:(cb + 1) * 512], op=ALU.add),
                             reads=[rpz, ("xres", i)], writes=[("xres", i)])

            mlp_up(0)
            for g in range(NG):
                if g + 1 < NG:
                    mlp_up(g + 1)
                mlp_down(g)

            for i, t in enumerate(tiles):
                if not last:
                    S.op("act", lambda e, i=i, t=t: e.dma_start(out=xscr[t.tid * 128:(t.tid + 1) * 128, :], in_=xres[:, i, :]),
                         reads=[("xres", i)], dma=f"xo{i}")
                elif t.kind == "S":
                    S.op("act", lambda e, i=i, t=t: e.dma_start(out=yp[t.s, t.g * 128:(t.g + 1) * 128, :], in_=xres[:, i, :]),
                         reads=[("xres", i)], dma=f"xo{i}")
                elif t.kind in ("A", "B"):
                    S.op("act", lambda e, i=i, t=t: e.dma_start(out=ys[t.s], in_=xres[0:DSEQ, i, :]), reads=[("xres", i)], dma=f"xo{i}")

        pump_conversions(8, 0)
        nconv_per_layer = sum(conv_chunks(m) for m in ("in", "out", "up", "down"))
        for l in range(depth):
            S.epoch = l
            load_layer_consts(l)
            for bi, tiles in enumerate(blocks):
                run_block(l, bi, tiles)
                if l == 0 and bi == 0:
                    pump_conversions(10 ** 6, 0)
                if l + 1 < depth:
                    pump_conversions((nconv_per_layer + len(blocks) - 1) // len(blocks) + 1, l + 1)
        S.emit()
        build_program.stats = dict(n_ops={e: len(S.ops[e]) for e in ENGS}, n_wait=S.n_wait,
                                   sbuf_left=nc.sbuf_bytes_remaining)
    return nc


def host_consts():
    ident = np.eye(128, dtype=np.float32)
    half = 8
    inv_freq = (np.float32(500000.0) ** (-(np.arange(half, dtype=np.float32) * np.float32(2.0)) / np.float32(16))).astype(np.float32)
    rope = np.zeros((128, 18, 32), np.float32)
    p = np.arange(128)
    for ti in range(18):
        if ti < 16:
            pos = NMETA + 128 * ti + p
        elif ti == 16:
            pos = NMETA + PAST + p
        else:
            pos = p
        ang = pos.astype(np.float32)[:, None] * inv_freq[None, :]
        c = np.cos(ang).astype(np.float32)
        s = np.sin(ang).astype(np.float32)
        rope[:, ti, 0:8] = c
        rope[:, ti, 8:16] = c
        rope[:, ti, 16:24] = -s
        rope[:, ti, 24:32] = s
    return ident, rope


_CACHE = {}


def kernel(x_prompt, x_sample, cache_k, cache_v, state_conv, meta_tokens, norm1_g, w_in,
           q_norm_g, k_norm_g, lam_q1, lam_k1, lam_q2, lam_k2, attn_norm_g, conv_w, conv_b,
           conv_ln_g, conv_ln_b, w_out, norm2_g, w_up, w_down):
    f = lambda a: np.ascontiguousarray(np.asarray(a, dtype=np.float32))
    x_prompt, x_sample, cache_k, cache_v, state_conv = map(f, (x_prompt, x_sample, cache_k, cache_v, state_conv))
    shared = dict(meta=f(meta_tokens), norm1_g=f(norm1_g), w_in=f(w_in), q_norm_g=f(q_norm_g), k_norm_g=f(k_norm_g),
                  lam_q1=f(lam_q1), lam_k1=f(lam_k1), lam_q2=f(lam_q2), lam_k2=f(lam_k2), attn_norm_g=f(attn_norm_g),
                  conv_w=f(conv_w), conv_b=f(conv_b), conv_ln_g=f(conv_ln_g), conv_ln_b=f(conv_ln_b), w_out=f(w_out),
                  norm2_g=f(norm2_g), w_up=f(w_up), w_down=f(w_down))
    ident, rope = host_consts()
    shared["identd"] = ident
    shared["roped"] = rope
    in_maps = []
    for c in range(NCORES):
        b = slice(2 * c, 2 * c + 2)
        m = dict(shared)
        m["xp"] = np.ascontiguousarray(x_prompt[b])
        m["xs"] = np.ascontiguousarray(x_sample[b])
        m["ck"] = np.ascontiguousarray(cache_k[:, b].reshape(DEPTH, 2, PAST, 1024))
        m["cv"] = np.ascontiguousarray(cache_v[:, b].reshape(DEPTH, 2, PAST, 1024))
        m["sc"] = np.ascontiguousarray(state_conv[:, b])
        in_maps.append(m)
    if "nc" not in _CACHE:
        _CACHE["nc"] = build_program()
    res = run_bass_kernel_spmd(_CACHE["nc"], in_maps, core_ids=list(range(NCORES)))
    R = res.results
    cat = lambda name, ax: np.concatenate([np.asarray(r[name]) for r in R], axis=ax)
    y_prompt = cat("yp", 0)
    y_sample = cat("ys", 0)
    k_prompt = cat("kp", 1).reshape(DEPTH, 16, L_P, NH, 128)
    v_prompt = cat("vp", 1).reshape(DEPTH, 16, L_P, NH, 128)
    conv_prompt = cat("cp", 1)
    k_sample = cat("kso", 1).reshape(DEPTH, 16, DSEQ, NH, 128)
    v_sample = cat("vso", 1).reshape(DEPTH, 16, DSEQ, NH, 128)
    conv_sample = cat("cso", 1)
    return (y_prompt, y_sample, k_prompt, v_prompt, conv_prompt, k_sample, v_sample, conv_sample)
```

```python
import contextlib
import math
import numpy as np
import concourse.bass as bass
import concourse.mybir as mybir
from concourse.bass_utils import run_bass_kernel_spmd

F32 = mybir.dt.float32
BF16 = mybir.dt.bfloat16
AF = mybir.ActivationFunctionType
ALU = mybir.AluOpType
AX = mybir.AxisListType

D = 2048
DEPTH = 4
SEQ = 2048
DSEQ = 64
PAST = 2048
NMETA = 16
NH = 8
INW = 5120
DFF = 8192
CONVK = 31
EPS = 1e-6
NCORES = 8
L_P = NMETA + SEQ

ENGS = ("pe", "act", "dve", "pool", "sp")


class Sched:
    def __init__(self, nc, stack, n_epochs=1):
        self.nc = nc
        self.stack = stack
        self.ops = {e: [] for e in ENGS}
        self.epoch = 0
        self.psem = {}
        for e in ENGS:
            if e == "sp":
                continue
            for ep in range(n_epochs):
                self.psem[(e, ep)] = stack.enter_context(nc.semaphore(f"p_{e}_{ep}"))
        self.pcount = {k: 0 for k in self.psem}
        self.dsem = {}
        self.dcount = {}
        self.last_w = {}
        self.readers = {}
        self.waited = {e: {} for e in ENGS}
        self.sem_owner = {}
        for (e, ep), s in self.psem.items():
            self.sem_owner[id(s)] = e
        self.n_wait = 0

    def _dma_sem(self, key):
        if key not in self.dsem:
            self.dsem[key] = self.stack.enter_context(self.nc.semaphore(f"d_{key}"))
            self.dcount[key] = 0
        return self.dsem[key]

    def op(self, eng, fn, reads=(), writes=(), dma=None):
        px = [r for r in reads if isinstance(r, tuple) and r[0] in ("pf", "pb")]
        if px:
            reads = [r for r in reads if r not in px]
            writes = list(writes) + [r for r in px if r not in writes]
        deps = {}

        def add(tok):
            if tok is None:
                return
            s, v = tok
            k = id(s)
            if k not in deps or deps[k][1] < v:
                deps[k] = (s, v)

        for r in reads:
            add(self.last_w.get(r))
        for w in writes:
            add(self.last_w.get(w))
            rd = self.readers.get(w)
            if rd:
                for tok in rd.values():
                    add(tok)
        waits = []
        wd = self.waited[eng]
        for k, (s, v) in deps.items():
            if eng == "pe" and self.sem_owner.get(k) == "pe":
                continue
            if wd.get(k, 0) >= v:
                continue
            wd[k] = v
            waits.append((s, v))
        self.n_wait += len(waits)
        if dma is not None:
            s = self._dma_sem(dma)
            self.dcount[dma] += 16
            tok = (s, self.dcount[dma])
            inc = (s, 16)
        else:
            key = (eng, self.epoch)
            s = self.psem[key]
            self.pcount[key] += 1
            tok = (s, self.pcount[key])
            inc = (s, 1)
        self.ops[eng].append((waits, fn, inc))
        for w in writes:
            self.last_w[w] = tok
            self.readers[w] = {}
        for r in reads:
            d = self.readers.setdefault(r, {})
            k = id(tok[0])
            if k not in d or d[k][1] < tok[1]:
                d[k] = tok
        return tok

    def emit(self):
        nc = self.nc
        final = [(self.dsem[k], self.dcount[k]) for k in self.dsem if self.dcount[k] > 0]
        ops = self.ops
        with nc.Block() as block:
            def run(e, name):
                for waits, fn, inc in ops[name]:
                    for s, v in waits:
                        e.wait_ge(s, v)
                    ins = fn(e)
                    ins.then_inc(inc[0], inc[1])

            @block.tensor
            def _(e):
                run(e, "pe")

            @block.scalar
            def _(e):
                run(e, "act")

            @block.vector
            def _(e):
                run(e, "dve")

            @block.gpsimd
            def _(e):
                run(e, "pool")

            @block.sync
            def _(e):
                run(e, "sp")
                for s, v in final:
                    e.wait_ge(s, v)


class Tile:
    def __init__(self, kind, s, g, nv, tid, ropei):
        self.kind, self.s, self.g, self.nv, self.tid, self.ropei = kind, s, g, nv, tid, ropei


def lam_init(l):
    return 0.8 - 0.6 * math.exp(-0.3 * l)


def build_program(depth=DEPTH, nblocks=None):
    nc = bass.Bass("TRN2", target_bir_lowering=False)

    def din(name, shape, dt=F32):
        return nc.dram_tensor(name, list(shape), dt, kind="ExternalInput").ap()

    def dout(name, shape):
        return nc.dram_tensor(name, list(shape), F32, kind="ExternalOutput").ap()

    xp = din("xp", [2, SEQ, D])
    xs = din("xs", [2, DSEQ, D])
    ck = din("ck", [DEPTH, 2, PAST, 1024])
    cv = din("cv", [DEPTH, 2, PAST, 1024])
    sc = din("sc", [DEPTH, 2, 30, 1024])
    meta = din("meta", [NMETA, D])
    norm1_g = din("norm1_g", [DEPTH, D])
    w_in = din("w_in", [DEPTH, D, INW])
    q_norm_g = din("q_norm_g", [DEPTH, 64])
    k_norm_g = din("k_norm_g", [DEPTH, 64])
    lam_q1 = din("lam_q1", [DEPTH, 64])
    lam_k1 = din("lam_k1", [DEPTH, 64])
    lam_q2 = din("lam_q2", [DEPTH, 64])
    lam_k2 = din("lam_k2", [DEPTH, 64])
    attn_norm_g = din("attn_norm_g", [DEPTH, 128])
    conv_w = din("conv_w", [DEPTH, CONVK, 1024])
    conv_b = din("conv_b", [DEPTH, 1024])
    conv_ln_g = din("conv_ln_g", [DEPTH, 1024])
    conv_ln_b = din("conv_ln_b", [DEPTH, 1024])
    w_out = din("w_out", [DEPTH, D, D])
    norm2_g = din("norm2_g", [DEPTH, D])
    w_up = din("w_up", [DEPTH, D, DFF])
    w_down = din("w_down", [DEPTH, DFF, D])
    identd = din("identd", [128, 128])
    roped = din("roped", [128, 18, 32])

    yp = dout("yp", [2, SEQ, D])
    ys = dout("ys", [2, DSEQ, D])
    kp = dout("kp", [DEPTH, 2, L_P, 1024])
    vp = dout("vp", [DEPTH, 2, L_P, 1024])
    cp = dout("cp", [DEPTH, 2, 30, 1024])
    kso = dout("kso", [DEPTH, 2, DSEQ, 1024])
    vso = dout("vso", [DEPTH, 2, DSEQ, 1024])
    cso = dout("cso", [DEPTH, 2, 30, 1024])

    NTID = 3 + 32
    xscr = nc.dram_tensor("xscr", [NTID * 128, D], F32).ap()
    wb = {
        "in": nc.dram_tensor("wb_in", [DEPTH, D, INW], BF16).ap(),
        "out": nc.dram_tensor("wb_out", [DEPTH, D, D], BF16).ap(),
        "up": nc.dram_tensor("wb_up", [DEPTH, D, DFF], BF16).ap(),
        "down": nc.dram_tensor("wb_down", [DEPTH, DFF, D], BF16).ap(),
    }
    wsrc = {"in": w_in, "out": w_out, "up": w_up, "down": w_down}
    CVROWS = {"in": 256, "out": 512, "up": 256, "down": 1024}
    WROWS = {"in": D, "out": D, "up": D, "down": DFF}

    XA = Tile("A", 0, 0, 64, 0, 16)
    XB = Tile("B", 1, 0, 64, 1, 16)
    XM = Tile("M", 0, 0, 16, 2, 17)
    blocks = [[XA, XB, XM]]
    for s in range(2):
        for j in range(4):
            blocks.append([Tile("S", s, 4 * j + i, 128, 3 + 16 * s + 4 * j + i, 4 * j + i) for i in range(4)])
    if nblocks is not None:
        blocks = blocks[:nblocks]

    with contextlib.ExitStack() as st:
        S = Sched(nc, st, n_epochs=DEPTH)

        def sb(name, shape, dt):
            return st.enter_context(nc.sbuf_tensor(name, list(shape), dt))

        def psum(name, shape, dt):
            return st.enter_context(nc.psum_tensor(name, list(shape), dt))

        xres = sb("xres", [128, 4, D], F32)
        big = sb("big", [128, 16, 512], BF16)
        qT = sb("qT", [128, 8, 512], BF16)
        KCOLS = NMETA + PAST + 2 * DSEQ
        kT = sb("kT", [128, 8, KCOLS], BF16)
        vS = sb("vS", [128, 19, 1024], BF16)
        GW = 30 + 512
        gT = sb("gT", [128, 8, GW], F32)
        NRING = 3
        ring = [sb(f"ring{i}", [128, 8, 512], BF16) for i in range(NRING)]
        xn = [sb(f"xn{i}", [128, D], BF16) for i in range(1)]
        kst = [sb(f"kst{i}", [128, 1024], BF16) for i in range(2)]
        NTF = 4
        tf = [sb(f"tf{i}", [128, 512], F32) for i in range(NTF)]
        ded = [sb(f"ded{i}", [128, 512], F32) for i in range(3)]
        NTB = 4
        tb = [sb(f"tb{i}", [128, 512], BF16) for i in range(NTB)]
        stt = sb("stt", [128, 64], F32)
        constT = [sb("constT0", [128, 384], F32)] * 2
        gqk = [sb("gqk0", [128, 128], F32)] * 2
        ident = sb("ident", [128, 128], F32)
        identb = sb("identb", [128, 128], BF16)
        onesf = sb("onesf", [128, 128], F32)
        onesb = sb("onesb", [128, 128], BF16)
        rope = sb("rope", [128, 18, 32], F32)
        ropet = sb("ropet", [128, 8, 16], F32)
        metaG = sb("metaG", [128, 8, 16], F32)
        hist = sb("hist", [128, 8, 30], F32)
        lamt = sb("lamt", [128, 16], F32)
        gAs = sb("gAs", [128, 4], F32)

        psf = [psum(f"psf{i}", [128, 512], F32) for i in range(8)]

        cnt = {"tf": 0, "tb": 0, "st": 0, "pf": 0, "pb": 0, "ring": 0, "xn": 0, "kst": 0, "sc": 0, "acc": 0}

        def rr(name, n):
            i = cnt[name] % n
            cnt[name] += 1
            return i

        def a_tf():
            i = rr("tf", NTF)
            return tf[i], ("tf", i)

        def a_tb():
            i = rr("tb", NTB)
            return tb[i], ("tb", i)

        def a_st():
            i = rr("st", 8)
            return stt[:, i * 8:(i + 1) * 8], ("st", i)

        def a_pf():
            i = rr("pf", 8)
            return psf[i], ("pf", i)

        def a_pb():
            i = rr("pf", 8)
            return psf[i][:].bitcast(BF16), ("pf", i)

        dq = []

        def drain(n):
            k = 0
            while dq and k < n:
                dq.pop(0)()
                k += 1

        cv_next = {}

        def conv_chunks(mat):
            return WROWS[mat] // CVROWS[mat]

        def record_conversion(mat, l, ci):
            r0 = ci * CVROWS[mat]
            r1 = r0 + CVROWS[mat]
            S.op("pool", lambda e: e.dma_start(out=wb[mat][l, r0:r1, :], in_=wsrc[mat][l, r0:r1, :]),
                 reads=[], writes=[("wb", mat, l, ci), "cvchain"], dma="cv")

        def ensure_converted(mat, l, r0, r1):
            c0 = r0 // CVROWS[mat]
            c1 = (r1 - 1) // CVROWS[mat]
            nxt = cv_next.get((mat, l), 0)
            while nxt <= c1:
                record_conversion(mat, l, nxt)
                nxt += 1
            cv_next[(mat, l)] = nxt
            return [("wb", mat, l, c) for c in range(c0, c1 + 1)]

        conv_order = []
        for l in range(depth):
            for mat in ("in", "out", "up", "down"):
                for ci in range(conv_chunks(mat)):
                    conv_order.append((mat, l, ci))
        conv_pos = [0]

        def pump_conversions(n, upto_layer):
            k = 0
            while k < n and conv_pos[0] < len(conv_order):
                mat, l, ci = conv_order[conv_pos[0]]
                if l > upto_layer:
                    break
                if cv_next.get((mat, l), 0) <= ci:
                    ensure_converted(mat, l, ci * CVROWS[mat], (ci + 1) * CVROWS[mat])
                    k += 1
                conv_pos[0] += 1

        def load_piece(mat, l, rows, cols, view):
            r0, r1 = rows
            c0, c1 = cols
            res = ensure_converted(mat, l, r0, r1)
            i = rr("ring", NRING)
            nchunk = (r1 - r0) // 128
            ncol = c1 - c0
            src = wb[mat][l, r0:r1, c0:c1].rearrange("(c p) n -> p c n", p=128)
            dst = ring[i][:].rearrange("p c n -> p (c n)")[:, 0:nchunk * ncol].rearrange("p (c n) -> p c n", n=ncol)
            S.op("sp", lambda e: e.dma_start(out=dst, in_=src), reads=res, writes=[("ring", i)], dma=f"ring{i}")
            return dst, ("ring", i)

        S.op("sp", lambda e: e.dma_start(out=ident[:], in_=identd), writes=["ident"], dma="c0")
        S.op("sp", lambda e: e.dma_start(out=rope[:], in_=roped), writes=["rope"], dma="c1")
        S.op("dve", lambda e: e.tensor_copy(out=identb[:], in_=ident[:]), reads=["ident"], writes=["identb"])
        S.op("dve", lambda e: e.memset(onesf[:], 1.0), writes=["onesf"])
        S.op("dve", lambda e: e.memset(onesb[:], 1.0), writes=["onesb"])
        lA, rA = a_tf()
        lB, rB = a_tf()
        for j, (src, dstt, off) in enumerate([(lam_q1, lA, 0), (lam_q2, lA, 256), (lam_k1, lB, 0), (lam_k2, lB, 256)]):
            S.op("sp", lambda e, src=src, dstt=dstt, off=off: e.dma_start(
                out=dstt[:, off:off + 256], in_=src.rearrange("l d -> (l d)").partition_broadcast(128)),
                writes=[(rA if dstt is lA else rB)], dma=f"c{2 + j}")
        S.op("dve", lambda e: e.tensor_tensor(out=lA[:], in0=lA[:], in1=lB[:], op=ALU.mult), reads=[rA, rB], writes=[rA])
        S.op("dve", lambda e: e.tensor_reduce(out=lamt[:, 0:8], in_=lA[:].rearrange("p (g d) -> p g d", d=64), axis=AX.X, op=ALU.add),
             reads=[rA], writes=["lamt"])
        S.op("act", lambda e: e.activation(out=lamt[:, 0:8], in_=lamt[:, 0:8], func=AF.Exp), reads=["lamt"], writes=["lamt"])
        S.op("dve", lambda e: e.tensor_tensor(out=lamt[:, 8:12], in0=lamt[:, 0:4], in1=lamt[:, 4:8], op=ALU.subtract),
             reads=["lamt"], writes=["lamt"])
        for l in range(DEPTH):
            S.op("dve", lambda e, l=l: e.tensor_scalar(out=lamt[:, 12 + l:13 + l], in0=lamt[:, 8 + l:9 + l], scalar1=-1.0,
                                                       scalar2=-lam_init(l), op0=ALU.mult, op1=ALU.add),
                 reads=["lamt"], writes=["lamt"])

        def load_layer_consts(l):
            pstage = ded[2][:, 0:384].rearrange("p (g w) -> p g w", w=128)
            cT = constT[l % 2]
            rc = ("constT", 0)
            g = gqk[l % 2]
            rg = ("gqk", 0)
            S.op("dve", lambda e: e.memset(pstage[:], 0.0), writes=[("ded", 2)])
            loads = [
                (pstage[0:16, 0, :], norm1_g[l].rearrange("(c p) -> c p", p=128)),
                (pstage[16:32, 0, :], norm2_g[l].rearrange("(c p) -> c p", p=128)),
                (pstage[32:40, 0, :], conv_b[l].rearrange("(c p) -> c p", p=128)),
                (pstage[40:48, 0, :], conv_ln_g[l].rearrange("(c p) -> c p", p=128)),
                (pstage[48:56, 0, :], conv_ln_b[l].rearrange("(c p) -> c p", p=128)),
                (pstage[56:57, 0, :], attn_norm_g[l].rearrange("(c p) -> c p", p=128)),
                (pstage[0:128, 1, :], conv_w[l].rearrange("j (c p) -> (j c) p", p=128)[0:128, :]),
                (pstage[0:120, 2, :], conv_w[l].rearrange("j (c p) -> (j c) p", p=128)[128:248, :]),
            ]
            for (o, i_) in loads:
                S.op("sp", lambda e, o=o, i_=i_: e.dma_start(out=o, in_=i_), writes=[("ded", 2)], dma="pst")
            pf, rpf = a_pf()
            for gi in range(3):
                S.op("pe", lambda e, gi=gi: e.transpose(out=pf[:, gi * 128:(gi + 1) * 128], in_=pstage[:, gi, :], identity=ident[:]),
                     reads=[("ded", 2), "ident"], writes=[rpf])
            S.op("dve", lambda e: e.tensor_copy(out=cT[:], in_=pf[:, 0:384]), reads=[rpf], writes=[rc])
            S.op("sp", lambda e: e.dma_start(out=g[:, 0:64], in_=q_norm_g[l].partition_broadcast(128)), writes=[rg], dma="gq")
            S.op("sp", lambda e: e.dma_start(out=g[:, 64:128], in_=k_norm_g[l].partition_broadcast(128)), writes=[rg], dma="gq")
            S.op("dve", lambda e: e.tensor_scalar(out=gAs[:, l:l + 1], in0=cT[:, 56:57], scalar1=1.0 - lam_init(l), scalar2=None,
                                                  op0=ALU.mult), reads=[rc], writes=[("gAs", l)])

        def rmsnorm_to_big(l, i, gcol0, rc):
            cT = constT[l % 2]
            x = xres[:, i, :]
            xi = rr("xn", 1)
            xb, rxb = xn[xi], ("xn", xi)
            stv, rst = a_st()
            S.op("act", lambda e: e.activation(out=xb[:], in_=x, func=AF.Square, accum_out=stv[:, 0:1]),
                 reads=[("xres", i)], writes=[rxb, rst])
            S.op("act", lambda e: e.activation(out=stv[:, 1:2], in_=stv[:, 0:1], func=AF.Sqrt, scale=1.0 / D, bias=EPS),
                 reads=[rst], writes=[rst])
            S.op("dve", lambda e: e.reciprocal(out=stv[:, 2:3], in_=stv[:, 1:2]), reads=[rst], writes=[rst])
            S.op("act", lambda e: e.activation(out=xb[:], in_=x, func=AF.Copy, scale=stv[:, 2:3]),
                 reads=[("xres", i), rst], writes=[rxb])
            for half in range(2):
                pb, rpb = a_pb()
                for c8 in range(8):
                    c = half * 8 + c8
                    S.op("pe", lambda e, c=c, c8=c8, pb=pb: e.transpose(out=pb[:, c8 * 128:(c8 + 1) * 128],
                                                                      in_=xb[:, c * 128:(c + 1) * 128], identity=identb[:]),
                         reads=[rxb, "identb"], writes=[rpb])
                S.op("dve", lambda e, half=half, pb=pb: e.tensor_tensor(
                    out=big[:, half * 8:(half + 1) * 8, i * 128:(i + 1) * 128],
                    in0=pb[:].rearrange("p (c t) -> p c t", t=128),
                    in1=cT[:, gcol0 + half * 8:gcol0 + (half + 1) * 8].unsqueeze(2).to_broadcast([128, 8, 128]),
                    op=ALU.mult), reads=[rpb, rc], writes=[("big", half * 8 + c8, i) for c8 in range(8)])

        def kslot(t):
            if t.kind == "M":
                return 0, 0
            if t.kind == "S":
                return NMETA + 128 * t.g, 1 + t.g
            if t.kind == "A":
                return NMETA + PAST, 17
            return NMETA + PAST + DSEQ, 18

        def kv_out_aps(l, t, which, c0):
            if t.kind == "S":
                dst = (kp if which == "k" else vp)[l, t.s, NMETA + 128 * t.g:NMETA + 128 * (t.g + 1), c0:c0 + 512]
                return [(dst, 128)]
            if t.kind == "M":
                o = kp if which == "k" else vp
                return [(o[l, 0, 0:NMETA, c0:c0 + 512], NMETA), (o[l, 1, 0:NMETA, c0:c0 + 512], NMETA)]
            o = kso if which == "k" else vso
            return [(o[l, t.s, :, c0:c0 + 512], DSEQ)]

        def attend(l, h, qc0, nq, entries):
            T1, T2 = ded[0], ded[1]
            for s in range(2):
                par = rr("acc", 2)
                po, rpo = psf[2 + 2 * par], ("pf", 2 + 2 * par)
                pm, rpm = psf[3 + 2 * par], ("pf", 3 + 2 * par)
                ne = len(entries)
                for ei, (kc0, nk, vtile, qoff, diag) in enumerate(entries):
                    n = nq - qoff
                    sci = rr("sc", 2)
                    psc, rsc = psf[sci], ("pf", sci)
                    S.op("pe", lambda e, s=s, kc0=kc0, nk=nk, qoff=qoff, n=n, psc=psc: e.matmul(
                        psc[0:nk, 0:n], lhsT=kT[64 * s:64 * s + 64, h, kc0:kc0 + nk],
                        rhs=qT[64 * s:64 * s + 64, h, qc0 + qoff:qc0 + nq], start=True, stop=True),
                        reads=[("kT", h), ("qT", h)], writes=[rsc])
                    pt, rpt = a_tb()
                    S.op("act", lambda e, nk=nk, n=n, psc=psc, pt=pt: e.activation(out=pt[0:nk, 0:n], in_=psc[0:nk, 0:n],
                                                                                 func=AF.Exp, scale=0.125),
                         reads=[rsc], writes=[rpt])
                    if diag:
                        S.op("dve", lambda e, pt=pt: e.memset(pt[64:128, 0:64], 0.0), writes=[rpt])
                    S.op("pe", lambda e, nk=nk, n=n, vtile=vtile, qoff=qoff, pt=pt, ei=ei, po=po: e.matmul(
                        po[:, qoff:nq], lhsT=vS[0:nk, vtile, h * 128:(h + 1) * 128], rhs=pt[0:nk, 0:n],
                        start=(ei == 0), stop=(ei == ne - 1)), reads=[("vS", vtile), rpt], writes=[rpo])
                    S.op("pe", lambda e, nk=nk, n=n, qoff=qoff, pt=pt, ei=ei, pm=pm: e.matmul(
                        pm[:, qoff:nq], lhsT=onesb[0:nk, :], rhs=pt[0:nk, 0:n],
                        start=(ei == 0), stop=(ei == ne - 1)), reads=["onesb", rpt], writes=[rpm])
                rt, rrt = a_tf()
                S.op("dve", lambda e, pm=pm, rt=rt: e.reciprocal(out=rt[:, 0:nq], in_=pm[:, 0:nq]), reads=[rpm], writes=[rrt])
                Tt = T1 if s == 0 else T2
                S.op("dve", lambda e, po=po, rt=rt, Tt=Tt: e.tensor_tensor(out=Tt[:, 0:nq], in0=po[:, 0:nq], in1=rt[:, 0:nq], op=ALU.mult),
                     reads=[rpo, rrt], writes=[("ded", 0 if s == 0 else 1)])
            S.op("dve", lambda e: e.scalar_tensor_tensor(out=T1[:, 0:nq], in0=T2[:, 0:nq], scalar=lamt[:, 12 + l:13 + l], in1=T1[:, 0:nq],
                                                         op0=ALU.mult, op1=ALU.add),
                 reads=[("ded", 0), ("ded", 1), "lamt"], writes=[("ded", 0)])
            S.op("act", lambda e: e.activation(out=T2[:, 0:nq], in_=T1[:, 0:nq], func=AF.Square), reads=[("ded", 0)], writes=[("ded", 1)])
            sci = rr("sc", 2)
            psc, rsc = psf[sci], ("pf", sci)
            S.op("pe", lambda e: e.matmul(psc[:, 0:nq], lhsT=onesf[:], rhs=T2[:, 0:nq], start=True, stop=True),
                 reads=["onesf", ("ded", 1)], writes=[rsc])
            S.op("act", lambda e: e.activation(out=T2[:, 0:nq], in_=psc[:, 0:nq], func=AF.Sqrt, scale=1.0 / 128, bias=EPS),
                 reads=[rsc], writes=[("ded", 1)])
            S.op("dve", lambda e: e.reciprocal(out=T2[:, 0:nq], in_=T2[:, 0:nq]), reads=[("ded", 1)], writes=[("ded", 1)])
            S.op("dve", lambda e: e.tensor_tensor(out=T1[:, 0:nq], in0=T1[:, 0:nq], in1=T2[:, 0:nq], op=ALU.mult),
                 reads=[("ded", 0), ("ded", 1)], writes=[("ded", 0)])
            ti0, ti1 = qc0 // 128, (qc0 + nq - 1) // 128
            S.op("act", lambda e: e.activation(out=big[:, h, qc0:qc0 + nq], in_=T1[:, 0:nq], func=AF.Copy, scale=gAs[:, l:l + 1]),
                 reads=[("ded", 0), ("gAs", l)], writes=[("big", h, ti) for ti in range(ti0, ti1 + 1)])

        def run_block(l, bi, tiles):
            NT = len(tiles)
            NTOK = NT * 128
            isx = tiles[0].kind != "S"
            cT = constT[l % 2]
            rc = ("constT", 0)
            last = (l == depth - 1)

            for i, t in enumerate(tiles):
                if l == 0:
                    if t.kind == "S":
                        src, nv = xp[t.s, t.g * 128:(t.g + 1) * 128, :], 128
                    elif t.kind == "M":
                        src, nv = meta, NMETA
                    else:
                        src, nv = xs[t.s], DSEQ
                    if nv < 128:
                        S.op("dve", lambda e, i=i: e.memset(xres[:, i, :], 0.0), writes=[("xres", i)])
                else:
                    src, nv = xscr[t.tid * 128:(t.tid + 1) * 128, :], 128
                S.op("sp", lambda e, i=i, src=src, nv=nv: e.dma_start(out=xres[0:nv, i, :], in_=src),
                     writes=[("xres", i)], dma=f"x{i}")
                rmsnorm_to_big(l, i, 0, rc)
            bigall = [("big", c, i) for c in range(16) for i in range(NT)]

            def bigk(kc):
                return [("big", kc, i) for i in range(NT)]

            def cache_fill(t):
                S.op("pool", lambda e, t=t: e.dma_start(out=vS[:, 1:17, :], in_=cv[l, t.s].rearrange("(t p) n -> p t n", p=128)),
                     writes=[("vS", 1 + j) for j in range(16)], dma="cvl")
                for j in range(16):
                    ki = rr("kst", 2)
                    S.op("pool", lambda e, t=t, j=j, ki=ki: e.dma_start(out=kst[ki][:], in_=ck[l, t.s, j * 128:(j + 1) * 128, :]),
                         writes=[("kst", ki)], dma=f"kst{ki}")
                    pb, rpb = a_pb()
                    for hh in range(8):
                        S.op("pe", lambda e, hh=hh, ki=ki, pb=pb: e.transpose(out=pb[:, hh * 128:(hh + 1) * 128],
                                                                            in_=kst[ki][:, hh * 128:(hh + 1) * 128], identity=identb[:]),
                             reads=[("kst", ki), "identb"], writes=[rpb])
                    S.op("dve" if j % 2 else "act", (lambda e, pb=pb, j=j: e.tensor_copy(
                        out=kT[:, :, NMETA + j * 128:NMETA + (j + 1) * 128], in_=pb[:].rearrange("p (h t) -> p h t", t=128)))
                        if j % 2 else (lambda e, pb=pb, j=j: e.activation(
                            out=kT[:, :, NMETA + j * 128:NMETA + (j + 1) * 128], in_=pb[:].rearrange("p (h t) -> p h t", t=128), func=AF.Copy)),
                        reads=[rpb], writes=[("kT", hh) for hh in range(8)])

            if isx:
                cache_fill(tiles[0])

            if isx:
                segs = [(i * 158, 128, i * 128) for i in range(NT)]
                SW = 158
            else:
                segs = [(0, 512, 0)]
                SW = 542
            NS = len(segs)

            allg = [("gT", c) for c in range(8)]
            if isx:
                for i, t in enumerate(tiles):
                    h0 = segs[i][0]
                    if t.kind == "M":
                        S.op("dve", lambda e, h0=h0: e.memset(gT[:, :, h0:h0 + 30], 0.0), writes=allg)
                    else:
                        for half in range(2):
                            stg, rstg = a_tf()
                            S.op("sp", lambda e, t=t, half=half, stg=stg: e.dma_start(
                                out=stg[0:30, :], in_=sc[l, t.s, :, half * 512:(half + 1) * 512]), writes=[rstg], dma=f"tf{rstg[1]}")
                            pf, rpf = a_pf()
                            for c4 in range(4):
                                S.op("pe", lambda e, c4=c4, stg=stg, pf=pf: e.transpose(
                                    out=pf[:, c4 * 32:c4 * 32 + 30], in_=stg[0:30, c4 * 128:(c4 + 1) * 128], identity=ident[0:30, 0:30]),
                                    reads=[rstg, "ident"], writes=[rpf])
                            S.op("dve", lambda e, half=half, h0=h0, pf=pf: e.tensor_copy(
                                out=gT[:, half * 4:(half + 1) * 4, h0:h0 + 30],
                                in_=pf[:, 0:128].rearrange("p (c w) -> p c w", w=32)[:, :, 0:30]),
                                reads=[rpf], writes=[("gT", half * 4 + c4) for c4 in range(4)])
            else:
                if tiles[0].g == 0:
                    S.op("dve", lambda e: e.memset(gT[:, :, 0:14], 0.0), writes=allg)
                    S.op("dve", lambda e: e.tensor_copy(out=gT[:, :, 14:30], in_=metaG[:]), reads=[("metaG", 0), ("metaG", 1)], writes=allg)
                else:
                    S.op("dve", lambda e: e.tensor_copy(out=gT[:, :, 0:30], in_=hist[:]), reads=[("hist", 0), ("hist", 1)], writes=allg)

            nseg_tok = NTOK // NS

            def state_half(dst, col0, half):
                pf, rpf = a_pf()
                for c4 in range(4):
                    c = half * 4 + c4
                    S.op("pe", lambda e, c=c, c4=c4, pf=pf: e.transpose(
                        out=pf[0:30, c4 * 128:(c4 + 1) * 128], in_=gT[:, c, col0:col0 + 30], identity=ident[:]),
                        reads=[("gT", c), "ident"], writes=[rpf])
                stg, rstg = a_tf()
                S.op("act", lambda e, pf=pf, stg=stg: e.activation(out=stg[0:30, :], in_=pf[0:30, :], func=AF.Copy),
                     reads=[rpf], writes=[rstg])
                S.op("act", lambda e, half=half, stg=stg: e.dma_start(out=dst[:, half * 512:(half + 1) * 512], in_=stg[0:30, :]),
                     reads=[rstg], dma=f"tf{rstg[1]}")

            def save_half(half):
                hg = [("gT", half * 4 + c4) for c4 in range(4)]
                cs = slice(half * 4, half * 4 + 4)
                if isx:
                    for i, t in enumerate(tiles):
                        h0 = segs[i][0]
                        if t.kind == "M":
                            S.op("dve", lambda e, h0=h0, cs=cs: e.tensor_copy(out=metaG[:, cs, :], in_=gT[:, cs, h0 + 30:h0 + 46]),
                                 reads=hg, writes=[("metaG", half)])
                        else:
                            state_half(cso[l, t.s], h0 + 30 + 34, half)
                else:
                    if tiles[0].g == 12:
                        state_half(cp[l, tiles[0].s], 512, half)
                    else:
                        S.op("dve", lambda e, cs=cs: e.tensor_copy(out=hist[:, cs, :], in_=gT[:, cs, 512:542]), reads=hg, writes=[("hist", half)])

            def conv_taps(c):
                yt, ryt = ded[2], ("ded", 2)
                yv = yt[:, 0:NTOK].rearrange("p (s w) -> p s w", s=NS)

                def gview(j):
                    return gT[:, c, 0:NS * SW].rearrange("p (s w) -> p s w", w=SW)[:, :, j:j + nseg_tok]

                wc = lambda j: cT[:, 128 + j * 8 + c:128 + j * 8 + c + 1]
                dq.append(lambda yv=yv, g0=gview(0), w0=wc(0): S.op("dve", lambda e: e.tensor_scalar(
                    out=yv, in0=g0, scalar1=w0, scalar2=cT[:, 32 + c:33 + c], op0=ALU.mult, op1=ALU.add),
                    reads=[("gT", c), rc], writes=[ryt]))
                for j in range(1, CONVK):
                    outv = yv if j < CONVK - 1 else gview(30)
                    dq.append(lambda j=j, yv=yv, gj=gview(j), wj=wc(j), outv=outv: S.op("dve", lambda e: e.scalar_tensor_tensor(
                        out=outv, in0=gj, scalar=wj, in1=yv, op0=ALU.mult, op1=ALU.add),
                        reads=[("gT", c), rc, ryt], writes=[ryt] if j < CONVK - 1 else [("gT", c)]))

            for j in range(4):
                banks = {}
                for kind, col0 in (("a", 3072), ("g", 4096)):
                    wt, rw = load_piece("in", l, (0, D), (col0 + j * 256, col0 + (j + 1) * 256), "kc")
                    for cc in range(2):
                        pp, rp = a_pf()
                        banks[(kind, cc)] = (pp, rp)
                        for kc in range(16):
                            S.op("pe", lambda e, wt=wt, pp=pp, kc=kc, cc=cc: e.matmul(
                                pp[:, 0:NTOK], lhsT=wt[:, kc, cc * 128:(cc + 1) * 128], rhs=big[:, kc, 0:NTOK],
                                start=(kc == 0), stop=(kc == 15)), reads=[rw] + bigk(kc), writes=[rp])
                for cc in range(2):
                    c = 2 * j + cc
                    pa, rpa = banks[("a", cc)]
                    pg, rpg = banks[("g", cc)]
                    sg, rsg = a_tf()
                    S.op("act", lambda e, pg=pg, sg=sg: e.activation(out=sg[:, 0:NTOK], in_=pg[:, 0:NTOK], func=AF.Sigmoid),
                         reads=[rpg], writes=[rsg])
                    S.op("dve", lambda e, pa=pa, sg=sg, c=c: e.tensor_tensor(
                        out=gT[:, c, 0:NS * SW].rearrange("p (s w) -> p s w", w=SW)[:, :, 30:30 + NTOK // NS],
                        in0=pa[:, 0:NTOK].rearrange("p (s w) -> p s w", s=NS),
                        in1=sg[:, 0:NTOK].rearrange("p (s w) -> p s w", s=NS), op=ALU.mult),
                        reads=[rpa, rsg], writes=[("gT", c)])
                    drain(8)
                if j % 2 == 1:
                    half = j // 2
                    save_half(half)
                    for c4 in range(4):
                        conv_taps(half * 4 + c4)

            for cb in range(6):
                zb_ = [a_pf() for _ in tiles]
                for kh in range(2):
                    wt, rw = load_piece("in", l, (kh * 1024, (kh + 1) * 1024), (cb * 512, (cb + 1) * 512), "kc")
                    for i, t in enumerate(tiles):
                        pz, rpz = zb_[i]
                        for k8 in range(8):
                            kc = kh * 8 + k8
                            S.op("pe", lambda e, wt=wt, kc=kc, k8=k8, pz=pz, i=i: e.matmul(
                                pz[:], lhsT=big[:, kc, i * 128:(i + 1) * 128], rhs=wt[:, k8, :],
                                start=(kc == 0), stop=(kc == 15)), reads=[rw, ("big", kc, i)], writes=[rpz])
                for i, t in enumerate(tiles):
                    pz, rpz = zb_[i]
                    drain(5)
                    kc0, vt = kslot(t)
                    if cb < 4:
                        isk = cb >= 2
                        h0 = 4 * (cb % 2)
                        sq, rsq = a_tf()
                        S.op("act", lambda e, pz=pz, sq=sq: e.activation(out=sq[:], in_=pz[:], func=AF.Square), reads=[rpz], writes=[rsq])
                        stv, rst = a_st()
                        S.op("dve", lambda e, sq=sq, stv=stv: e.tensor_reduce(out=stv, in_=sq[:].rearrange("p (g d) -> p g d", d=64),
                                                                            axis=AX.X, op=ALU.add), reads=[rsq], writes=[rst])
                        S.op("act", lambda e, stv=stv: e.activation(out=stv, in_=stv, func=AF.Sqrt, scale=1.0 / 64, bias=EPS),
                             reads=[rst], writes=[rst])
                        S.op("dve", lambda e, stv=stv: e.reciprocal(out=stv, in_=stv), reads=[rst], writes=[rst])
                        z, rz = a_tf()
                        z3 = z[:].rearrange("p (g d) -> p g d", d=64)
                        S.op("dve", lambda e, pz=pz, z3=z3, stv=stv: e.tensor_tensor(
                            out=z3, in0=pz[:].rearrange("p (g d) -> p g d", d=64),
                            in1=stv.unsqueeze(2).to_broadcast([128, 8, 64]), op=ALU.mult), reads=[rpz, rst], writes=[rz])
                        gsl = gqk[l % 2][:, 64:128] if isk else gqk[l % 2][:, 0:64]
                        S.op("dve", lambda e, z3=z3, gsl=gsl: e.tensor_tensor(
                            out=z3, in0=z3, in1=gsl.unsqueeze(1).to_broadcast([128, 8, 64]), op=ALU.mult),
                            reads=[rz, ("gqk", 0)], writes=[rz])
                        cs = rope[:, t.ropei, 0:16].unsqueeze(1).to_broadcast([128, 8, 16])
                        sn = rope[:, t.ropei, 16:24].unsqueeze(1).to_broadcast([128, 8, 8])
                        sp_ = rope[:, t.ropei, 24:32].unsqueeze(1).to_broadcast([128, 8, 8])
                        S.op("dve", lambda e, z3=z3, sn=sn: e.tensor_tensor(out=ropet[:, :, 0:8], in0=z3[:, :, 8:16], in1=sn, op=ALU.mult),
                             reads=[rz, "rope"], writes=["ropet"])
                        S.op("dve", lambda e, z3=z3, sp_=sp_: e.tensor_tensor(out=ropet[:, :, 8:16], in0=z3[:, :, 0:8], in1=sp_, op=ALU.mult),
                             reads=[rz, "rope"], writes=["ropet"])
                        S.op("dve", lambda e, z3=z3, cs=cs: e.tensor_tensor(out=z3[:, :, 0:16], in0=z3[:, :, 0:16], in1=cs, op=ALU.mult),
                             reads=[rz, "rope"], writes=[rz])
                        S.op("dve", lambda e, z3=z3: e.tensor_tensor(out=z3[:, :, 0:16], in0=z3[:, :, 0:16], in1=ropet[:], op=ALU.add),
                             reads=[rz, "ropet"], writes=[rz])
                        zb, rzb = a_tb()
                        S.op("act", lambda e, z=z, zb=zb: e.activation(out=zb[:], in_=z[:], func=AF.Copy), reads=[rz], writes=[rzb])
                        if isk:
                            for (dst, nr) in kv_out_aps(l, t, "k", (cb % 2) * 512):
                                S.op("act", lambda e, dst=dst, nr=nr, z=z: e.dma_start(out=dst, in_=z[0:nr, :]), reads=[rz], dma=f"tf{rz[1]}")
                        pb, rpb = a_pb()
                        for hh in range(4):
                            S.op("pe", lambda e, hh=hh, zb=zb, pb=pb: e.transpose(out=pb[:, hh * 128:(hh + 1) * 128],
                                                                                in_=zb[:, hh * 128:(hh + 1) * 128], identity=identb[:]),
                                 reads=[rzb, "identb"], writes=[rpb])
                        if isk:
                            nv = t.nv
                            S.op("act", lambda e, pb=pb, h0=h0, kc0=kc0, nv=nv: e.activation(
                                out=kT[:, h0:h0 + 4, kc0:kc0 + nv],
                                in_=pb[:, 0:512].rearrange("p (h t) -> p h t", t=128)[:, :, 0:nv], func=AF.Copy),
                                reads=[rpb], writes=[("kT", h0 + hh) for hh in range(4)])
                        else:
                            S.op("act", lambda e, pb=pb, h0=h0, i=i: e.activation(
                                out=qT[:, h0:h0 + 4, i * 128:(i + 1) * 128],
                                in_=pb[:, 0:512].rearrange("p (h t) -> p h t", t=128), func=AF.Copy),
                                reads=[rpb], writes=[("qT", h0 + hh) for hh in range(4)])
                    else:
                        c0 = (cb % 2) * 512
                        vf, rvf = a_tf()
                        S.op("act", lambda e, pz=pz, vf=vf: e.activation(out=vf[:], in_=pz[:], func=AF.Copy), reads=[rpz], writes=[rvf])
                        S.op("dve", lambda e, pz=pz, vt=vt, c0=c0: e.tensor_copy(out=vS[:, vt, c0:c0 + 512], in_=pz[:]),
                             reads=[rpz], writes=[("vS", vt)])
                        for (dst, nr) in kv_out_aps(l, t, "v", c0):
                            S.op("act", lambda e, dst=dst, nr=nr, vf=vf: e.dma_start(out=dst, in_=vf[0:nr, :]), reads=[rvf], dma=f"tf{rvf[1]}")

            if isx:
                for i, t in enumerate(tiles):
                    if t.kind == "M":
                        continue
                    if i > 0:
                        cache_fill(t)
                    kc0, vt = kslot(t)
                    entries = [(0, NMETA, 0, 0, False)] + [(NMETA + 128 * j, 128, 1 + j, 0, False) for j in range(16)] + [(kc0, DSEQ, vt, 0, False)]
                    for h in range(NH):
                        attend(l, h, i * 128, DSEQ, entries)
                        drain(6)
                mi = [i for i, t in enumerate(tiles) if t.kind == "M"][0]
                for h in range(NH):
                    attend(l, h, mi * 128, NMETA, [(0, NMETA, 0, 0, False)])
                for i, t in enumerate(tiles):
                    S.op("dve", lambda e, i=i, t=t: e.memset(big[:, 0:8, i * 128 + t.nv:(i + 1) * 128], 0.0),
                         writes=[("big", h, i) for h in range(8)])
            else:
                g0 = tiles[0].g
                entries = [(0, NMETA, 0, 0, False)] + [(NMETA + 128 * g, 128, 1 + g, 0, False) for g in range(g0)]
                entries += [(NMETA + 128 * (g0 + i), 128, 1 + g0 + i, 128 * i, True) for i in range(NT)]
                for h in range(NH):
                    attend(l, h, 0, NTOK, entries)
                    drain(5)

            drain(10 ** 6)
            def ytok(c):
                return gT[:, c, 0:NS * SW].rearrange("p (s w) -> p s w", w=SW)[:, :, 30:30 + nseg_tok]

            pm_, rpm_ = a_pf()
            pq_, rpq_ = a_pf()
            pm3 = pm_[:, 0:NTOK].rearrange("p (s w) -> p s w", s=NS)
            pq3 = pq_[:, 0:NTOK].rearrange("p (s w) -> p s w", s=NS)
            for c in range(8):
                ysq, rysq = a_tf()
                ysq3 = ysq[:, 0:NTOK].rearrange("p (s w) -> p s w", s=NS)
                S.op("act", lambda e, c=c, ysq3=ysq3: e.activation(out=ysq3, in_=ytok(c), func=AF.Square), reads=[("gT", c)], writes=[rysq])
                S.op("pe", lambda e, c=c: e.matmul(pm3, lhsT=onesf[:], rhs=ytok(c), start=(c == 0), stop=(c == 7)),
                     reads=["onesf", ("gT", c)], writes=[rpm_])
                S.op("pe", lambda e, c=c, ysq3=ysq3: e.matmul(pq3, lhsT=onesf[:], rhs=ysq3, start=(c == 0), stop=(c == 7)),
                     reads=["onesf", rysq], writes=[rpq_])
            MEAN, RSTD = ded[0], ded[1]
            S.op("dve", lambda e: e.tensor_scalar(out=MEAN[:, 0:NTOK], in0=pm_[:, 0:NTOK], scalar1=1.0 / 1024, scalar2=None, op0=ALU.mult),
                 reads=[rpm_], writes=[("ded", 0)])
            S.op("dve", lambda e: e.tensor_tensor(out=RSTD[:, 0:NTOK], in0=MEAN[:, 0:NTOK], in1=MEAN[:, 0:NTOK], op=ALU.mult),
                 reads=[("ded", 0)], writes=[("ded", 1)])
            S.op("dve", lambda e: e.scalar_tensor_tensor(out=RSTD[:, 0:NTOK], in0=pq_[:, 0:NTOK], scalar=1.0 / 1024, in1=RSTD[:, 0:NTOK],
                                                         op0=ALU.mult, op1=ALU.subtract), reads=[rpq_, ("ded", 1)], writes=[("ded", 1)])
            S.op("dve", lambda e: e.tensor_scalar(out=RSTD[:, 0:NTOK], in0=RSTD[:, 0:NTOK], scalar1=0.0, scalar2=None, op0=ALU.max),
                 reads=[("ded", 1)], writes=[("ded", 1)])
            S.op("act", lambda e: e.activation(out=RSTD[:, 0:NTOK], in_=RSTD[:, 0:NTOK], func=AF.Sqrt, bias=EPS), reads=[("ded", 1)], writes=[("ded", 1)])
            S.op("dve", lambda e: e.reciprocal(out=RSTD[:, 0:NTOK], in_=RSTD[:, 0:NTOK]), reads=[("ded", 1)], writes=[("ded", 1)])
            mean3 = MEAN[:, 0:NTOK].rearrange("p (s w) -> p s w", s=NS)
            rstd3 = RSTD[:, 0:NTOK].rearrange("p (s w) -> p s w", s=NS)
            cbuf = []
            for c in range(8):
                n_, rn_ = a_tf()
                n3 = n_[:, 0:NTOK].rearrange("p (s w) -> p s w", s=NS)
                S.op("dve", lambda e, c=c, n3=n3: e.tensor_tensor(out=n3, in0=ytok(c), in1=mean3, op=ALU.subtract),
                     reads=[("gT", c), ("ded", 0)], writes=[rn_])
                S.op("dve", lambda e, n3=n3: e.tensor_tensor(out=n3, in0=n3, in1=rstd3, op=ALU.mult), reads=[rn_, ("ded", 1)], writes=[rn_])
                cbuf.append((c, n_, rn_))
                S.op("act", lambda e, c=c, n_=n_: e.activation(out=big[:, 8 + c, 0:NTOK], in_=n_[:, 0:NTOK], func=AF.Silu,
                                                              scale=cT[:, 40 + c:41 + c], bias=cT[:, 48 + c:49 + c]),
                     reads=[rn_, rc], writes=[("big", 8 + c, i) for i in range(NT)])

            for cb in range(4):
                zb_ = [a_pf() for _ in range(NT)]
                for kh in range(2):
                    wt, rw = load_piece("out", l, (kh * 1024, (kh + 1) * 1024), (cb * 512, (cb + 1) * 512), "kc")
                    for i in range(NT):
                        pz, rpz = zb_[i]
                        for k8 in range(8):
                            kc = kh * 8 + k8
                            S.op("pe", lambda e, wt=wt, kc=kc, k8=k8, pz=pz, i=i: e.matmul(
                                pz[:], lhsT=big[:, kc, i * 128:(i + 1) * 128], rhs=wt[:, k8, :],
                                start=(kc == 0), stop=(kc == 15)), reads=[rw, ("big", kc, i)], writes=[rpz])
                for i in range(NT):
                    pz, rpz = zb_[i]
                    S.op("dve", lambda e, pz=pz, i=i, cb=cb: e.tensor_tensor(out=xres[:, i, cb * 512:(cb + 1) * 512], in0=pz[:],
                                                                          in1=xres[:, i, cb * 512:(cb + 1) * 512], op=ALU.add),
                         reads=[rpz, ("xres", i)], writes=[("xres", i)])
            for i in range(NT):
                rmsnorm_to_big(l, i, 16, rc)

            NG = 16

            def mlp_up(g):
                ub_ = [a_pf() for _ in range(4)]
                for kh in range(2):
                    wt, rw = load_piece("up", l, (kh * 1024, (kh + 1) * 1024), (g * 512, (g + 1) * 512), "kc")
                    for fc in range(4):
                        pu, rpu = ub_[fc]
                        for k8 in range(8):
                            kc = kh * 8 + k8
                            S.op("pe", lambda e, wt=wt, kc=kc, k8=k8, fc=fc, pu=pu: e.matmul(
                                pu[:, 0:NTOK], lhsT=wt[:, k8, fc * 128:(fc + 1) * 128], rhs=big[:, kc, 0:NTOK],
                                start=(kc == 0), stop=(kc == 15)), reads=[rw] + bigk(kc), writes=[rpu])
                for fc in range(4):
                    pu, rpu = ub_[fc]
                    sq, rsq = a_tf()
                    S.op("act", lambda e, pu=pu, sq=sq: e.activation(out=sq[:, 0:NTOK], in_=pu[:, 0:NTOK], func=AF.Square),
                         reads=[rpu], writes=[rsq])
                    S.op("dve", lambda e, pu=pu, sq=sq, fc=fc: e.scalar_tensor_tensor(
                        out=qT[:, (g % 2) * 4 + fc, 0:NTOK], in0=pu[:, 0:NTOK], scalar=0.0, in1=sq[:, 0:NTOK], op0=ALU.is_gt, op1=ALU.mult),
                        reads=[rpu, rsq], writes=[("qT", (g % 2) * 4 + fc)])

            def mlp_down(g):
                halves = []
                for ch in range(2):
                    wd, rwd = load_piece("down", l, (g * 512, (g + 1) * 512), (ch * 1024, (ch + 1) * 1024), "kc")
                    halves.append((wd, rwd))
                for i in range(NT):
                    for cb in range(4):
                        wd, rwd = halves[cb // 2]
                        pz, rpz = a_pf()
                        for fc in range(4):
                            S.op("pe", lambda e, wd=wd, fc=fc, pz=pz, i=i, cb=cb: e.matmul(
                                pz[:], lhsT=qT[:, (g % 2) * 4 + fc, i * 128:(i + 1) * 128],
                                rhs=wd[:, fc, (cb % 2) * 512:(cb % 2 + 1) * 512], start=(fc == 0), stop=(fc == 3)),
                                reads=[rwd, ("qT", (g % 2) * 4 + fc)], writes=[rpz])
                        S.op("dve", lambda e, pz=pz, i=i, cb=cb: e.tensor_tensor(out=xres[:, i, cb * 512:(cb + 1) * 512], in0=pz[:],
                                                                              in1=xres[:, i, cb * 512:(cb + 1) * 512], op=ALU.add),
                             reads=[rpz, ("xres", i)], writes=[("xres", i)])

            mlp_up(0)
            for g in range(NG):
                if g + 1 < NG:
                    mlp_up(g + 1)
                mlp_down(g)

            for i, t in enumerate(tiles):
                if not last:
                    S.op("act", lambda e, i=i, t=t: e.dma_start(out=xscr[t.tid * 128:(t.tid + 1) * 128, :], in_=xres[:, i, :]),
                         reads=[("xres", i)], dma=f"xo{i}")
                elif t.kind == "S":
                    S.op("act", lambda e, i=i, t=t: e.dma_start(out=yp[t.s, t.g * 128:(t.g + 1) * 128, :], in_=xres[:, i, :]),
                         reads=[("xres", i)], dma=f"xo{i}")
                elif t.kind in ("A", "B"):
                    S.op("act", lambda e, i=i, t=t: e.dma_start(out=ys[t.s], in_=xres[0:DSEQ, i, :]), reads=[("xres", i)], dma=f"xo{i}")

        pump_conversions(8, 0)
        nconv_per_layer = sum(conv_chunks(m) for m in ("in", "out", "up", "down"))
        for l in range(depth):
            S.epoch = l
            load_layer_consts(l)
            for bi, tiles in enumerate(blocks):
                run_block(l, bi, tiles)
                if l == 0 and bi == 0:
                    pump_conversions(10 ** 6, 0)
                if l + 1 < depth:
                    pump_conversions((nconv_per_layer + len(blocks) - 1) // len(blocks) + 1, l + 1)
        S.emit()
        build_program.stats = dict(n_ops={e: len(S.ops[e]) for e in ENGS}, n_wait=S.n_wait,
                                   sbuf_left=nc.sbuf_bytes_remaining)
    return nc


def host_consts():
    ident = np.eye(128, dtype=np.float32)
    half = 8
    inv_freq = (np.float32(500000.0) ** (-(np.arange(half, dtype=np.float32) * np.float32(2.0)) / np.float32(16))).astype(np.float32)
    rope = np.zeros((128, 18, 32), np.float32)
    p = np.arange(128)
    for ti in range(18):
        if ti < 16:
            pos = NMETA + 128 * ti + p
        elif ti == 16:
            pos = NMETA + PAST + p
        else:
            pos = p
        ang = pos.astype(np.float32)[:, None] * inv_freq[None, :]
        c = np.cos(ang).astype(np.float32)
        s = np.sin(ang).astype(np.float32)
        rope[:, ti, 0:8] = c
        rope[:, ti, 8:16] = c
        rope[:, ti, 16:24] = -s
        rope[:, ti, 24:32] = s
    return ident, rope


_CACHE = {}


def kernel(x_prompt, x_sample, cache_k, cache_v, state_conv, meta_tokens, norm1_g, w_in,
           q_norm_g, k_norm_g, lam_q1, lam_k1, lam_q2, lam_k2, attn_norm_g, conv_w, conv_b,
           conv_ln_g, conv_ln_b, w_out, norm2_g, w_up, w_down):
    f = lambda a: np.ascontiguousarray(np.asarray(a, dtype=np.float32))
    x_prompt, x_sample, cache_k, cache_v, state_conv = map(f, (x_prompt, x_sample, cache_k, cache_v, state_conv))
    shared = dict(meta=f(meta_tokens), norm1_g=f(norm1_g), w_in=f(w_in), q_norm_g=f(q_norm_g), k_norm_g=f(k_norm_g),
                  lam_q1=f(lam_q1), lam_k1=f(lam_k1), lam_q2=f(lam_q2), lam_k2=f(lam_k2), attn_norm_g=f(attn_norm_g),
                  conv_w=f(conv_w), conv_b=f(conv_b), conv_ln_g=f(conv_ln_g), conv_ln_b=f(conv_ln_b), w_out=f(w_out),
                  norm2_g=f(norm2_g), w_up=f(w_up), w_down=f(w_down))
    ident, rope = host_consts()
    shared["identd"] = ident
    shared["roped"] = rope
    in_maps = []
    for c in range(NCORES):
        b = slice(2 * c, 2 * c + 2)
        m = dict(shared)
        m["xp"] = np.ascontiguousarray(x_prompt[b])
        m["xs"] = np.ascontiguousarray(x_sample[b])
        m["ck"] = np.ascontiguousarray(cache_k[:, b].reshape(DEPTH, 2, PAST, 1024))
        m["cv"] = np.ascontiguousarray(cache_v[:, b].reshape(DEPTH, 2, PAST, 1024))
        m["sc"] = np.ascontiguousarray(state_conv[:, b])
        in_maps.append(m)
    if "nc" not in _CACHE:
        _CACHE["nc"] = build_program()
    res = run_bass_kernel_spmd(_CACHE["nc"], in_maps, core_ids=list(range(NCORES)))
    R = res.results
    cat = lambda name, ax: np.concatenate([np.asarray(r[name]) for r in R], axis=ax)
    y_prompt = cat("yp", 0)
    y_sample = cat("ys", 0)
    k_prompt = cat("kp", 1).reshape(DEPTH, 16, L_P, NH, 128)
    v_prompt = cat("vp", 1).reshape(DEPTH, 16, L_P, NH, 128)
    conv_prompt = cat("cp", 1)
    k_sample = cat("kso", 1).reshape(DEPTH, 16, DSEQ, NH, 128)
    v_sample = cat("vso", 1).reshape(DEPTH, 16, DSEQ, NH, 128)
    conv_sample = cat("cso", 1)
    return (y_prompt, y_sample, k_prompt, v_prompt, conv_prompt, k_sample, v_sample, conv_sample)
```

```python
import contextlib
import math
import numpy as np
import concourse.bass as bass
import concourse.mybir as mybir
from concourse.bass_utils import run_bass_kernel_spmd

F32 = mybir.dt.float32
BF16 = mybir.dt.bfloat16
AF = mybir.ActivationFunctionType
ALU = mybir.AluOpType
AX = mybir.AxisListType

D = 2048
DEPTH = 4
SEQ = 2048
DSEQ = 64
PAST = 2048
NMETA = 16
NH = 8
INW = 5120
DFF = 8192
CONVK = 31
EPS = 1e-6
NCORES = 8
L_P = NMETA + SEQ

ENGS = ("pe", "act", "dve", "pool", "sp")


class Sched:
    def __init__(self, nc, stack, n_epochs=1):
        self.nc = nc
        self.stack = stack
        self.ops = {e: [] for e in ENGS}
        self.epoch = 0
        self.psem = {}
        for e in ENGS:
            if e == "sp":
                continue
            for ep in range(n_epochs):
                self.psem[(e, ep)] = stack.enter_context(nc.semaphore(f"p_{e}_{ep}"))
        self.pcount = {k: 0 for k in self.psem}
        self.dsem = {}
        self.dcount = {}
        self.last_w = {}
        self.readers = {}
        self.waited = {e: {} for e in ENGS}
        self.sem_owner = {}
        for (e, ep), s in self.psem.items():
            self.sem_owner[id(s)] = e
        self.n_wait = 0

    def _dma_sem(self, key):
        if key not in self.dsem:
            self.dsem[key] = self.stack.enter_context(self.nc.semaphore(f"d_{key}"))
            self.dcount[key] = 0
        return self.dsem[key]

    def op(self, eng, fn, reads=(), writes=(), dma=None):
        px = [r for r in reads if isinstance(r, tuple) and r[0] in ("pf", "pb")]
        if px:
            reads = [r for r in reads if r not in px]
            writes = list(writes) + [r for r in px if r not in writes]
        deps = {}

        def add(tok):
            if tok is None:
                return
            s, v = tok
            k = id(s)
            if k not in deps or deps[k][1] < v:
                deps[k] = (s, v)

        for r in reads:
            add(self.last_w.get(r))
        for w in writes:
            add(self.last_w.get(w))
            rd = self.readers.get(w)
            if rd:
                for tok in rd.values():
                    add(tok)
        waits = []
        wd = self.waited[eng]
        for k, (s, v) in deps.items():
            if eng == "pe" and self.sem_owner.get(k) == "pe":
                continue
            if wd.get(k, 0) >= v:
                continue
            wd[k] = v
            waits.append((s, v))
        self.n_wait += len(waits)
        if dma is not None:
            s = self._dma_sem(dma)
            self.dcount[dma] += 16
            tok = (s, self.dcount[dma])
            inc = (s, 16)
        else:
            key = (eng, self.epoch)
            s = self.psem[key]
            self.pcount[key] += 1
            tok = (s, self.pcount[key])
            inc = (s, 1)
        self.ops[eng].append((waits, fn, inc))
        for w in writes:
            self.last_w[w] = tok
            self.readers[w] = {}
        for r in reads:
            d = self.readers.setdefault(r, {})
            k = id(tok[0])
            if k not in d or d[k][1] < tok[1]:
                d[k] = tok
        return tok

    def emit(self):
        nc = self.nc
        final = [(self.dsem[k], self.dcount[k]) for k in self.dsem if self.dcount[k] > 0]
        ops = self.ops
        with nc.Block() as block:
            def run(e, name):
                for waits, fn, inc in ops[name]:
                    for s, v in waits:
                        e.wait_ge(s, v)
                    ins = fn(e)
                    ins.then_inc(inc[0], inc[1])

            @block.tensor
            def _(e):
                run(e, "pe")

            @block.scalar
            def _(e):
                run(e, "act")

            @block.vector
            def _(e):
                run(e, "dve")

            @block.gpsimd
            def _(e):
                run(e, "pool")

            @block.sync
            def _(e):
                run(e, "sp")
                for s, v in final:
                    e.wait_ge(s, v)


class Tile:
    def __init__(self, kind, s, g, nv, tid, ropei):
        self.kind, self.s, self.g, self.nv, self.tid, self.ropei = kind, s, g, nv, tid, ropei


def lam_init(l):
    return 0.8 - 0.6 * math.exp(-0.3 * l)


def build_program(depth=DEPTH, nblocks=None):
    nc = bass.Bass("TRN2", target_bir_lowering=False)

    def din(name, shape, dt=F32):
        return nc.dram_tensor(name, list(shape), dt, kind="ExternalInput").ap()

    def dout(name, shape):
        return nc.dram_tensor(name, list(shape), F32, kind="ExternalOutput").ap()

    xp = din("xp", [2, SEQ, D])
    xs = din("xs", [2, DSEQ, D])
    ck = din("ck", [DEPTH, 2, PAST, 1024])
    cv = din("cv", [DEPTH, 2, PAST, 1024])
    sc = din("sc", [DEPTH, 2, 30, 1024])
    meta = din("meta", [NMETA, D])
    norm1_g = din("norm1_g", [DEPTH, D])
    w_in = din("w_in", [DEPTH, D, INW])
    q_norm_g = din("q_norm_g", [DEPTH, 64])
    k_norm_g = din("k_norm_g", [DEPTH, 64])
    lam_q1 = din("lam_q1", [DEPTH, 64])
    lam_k1 = din("lam_k1", [DEPTH, 64])
    lam_q2 = din("lam_q2", [DEPTH, 64])
    lam_k2 = din("lam_k2", [DEPTH, 64])
    attn_norm_g = din("attn_norm_g", [DEPTH, 128])
    conv_w = din("conv_w", [DEPTH, CONVK, 1024])
    conv_b = din("conv_b", [DEPTH, 1024])
    conv_ln_g = din("conv_ln_g", [DEPTH, 1024])
    conv_ln_b = din("conv_ln_b", [DEPTH, 1024])
    w_out = din("w_out", [DEPTH, D, D])
    norm2_g = din("norm2_g", [DEPTH, D])
    w_up = din("w_up", [DEPTH, D, DFF])
    w_down = din("w_down", [DEPTH, DFF, D])
    identd = din("identd", [128, 128])
    roped = din("roped", [128, 18, 32])

    yp = dout("yp", [2, SEQ, D])
    ys = dout("ys", [2, DSEQ, D])
    kp = dout("kp", [DEPTH, 2, L_P, 1024])
    vp = dout("vp", [DEPTH, 2, L_P, 1024])
    cp = dout("cp", [DEPTH, 2, 30, 1024])
    kso = dout("kso", [DEPTH, 2, DSEQ, 1024])
    vso = dout("vso", [DEPTH, 2, DSEQ, 1024])
    cso = dout("cso", [DEPTH, 2, 30, 1024])

    NTID = 3 + 32
    xscr = nc.dram_tensor("xscr", [NTID * 128, D], F32).ap()
    wb = {
        "in": nc.dram_tensor("wb_in", [DEPTH, D, INW], BF16).ap(),
        "out": nc.dram_tensor("wb_out", [DEPTH, D, D], BF16).ap(),
        "up": nc.dram_tensor("wb_up", [DEPTH, D, DFF], BF16).ap(),
        "down": nc.dram_tensor("wb_down", [DEPTH, DFF, D], BF16).ap(),
    }
    wsrc = {"in": w_in, "out": w_out, "up": w_up, "down": w_down}
    CVROWS = {"in": 256, "out": 512, "up": 256, "down": 1024}
    WROWS = {"in": D, "out": D, "up": D, "down": DFF}

    XA = Tile("A", 0, 0, 64, 0, 16)
    XB = Tile("B", 1, 0, 64, 1, 16)
    XM = Tile("M", 0, 0, 16, 2, 17)
    blocks = [[XA, XB, XM]]
    for s in range(2):
        for j in range(4):
            blocks.append([Tile("S", s, 4 * j + i, 128, 3 + 16 * s + 4 * j + i, 4 * j + i) for i in range(4)])
    if nblocks is not None:
        blocks = blocks[:nblocks]

    with contextlib.ExitStack() as st:
        S = Sched(nc, st, n_epochs=DEPTH)

        def sb(name, shape, dt):
            return st.enter_context(nc.sbuf_tensor(name, list(shape), dt))

        def psum(name, shape, dt):
            return st.enter_context(nc.psum_tensor(name, list(shape), dt))

        xres = sb("xres", [128, 4, D], F32)
        big = sb("big", [128, 16, 512], BF16)
        qT = sb("qT", [128, 8, 512], BF16)
        KCOLS = NMETA + PAST + 2 * DSEQ
        kT = sb("kT", [128, 8, KCOLS], BF16)
        vS = sb("vS", [128, 19, 1024], BF16)
        GW = 30 + 512
        gT = sb("gT", [128, 8, GW], F32)
        NRING = 3
        ring = [sb(f"ring{i}", [128, 8, 512], BF16) for i in range(NRING)]
        xn = [sb(f"xn{i}", [128, D], BF16) for i in range(1)]
        kst = [sb(f"kst{i}", [128, 1024], BF16) for i in range(2)]
        NTF = 4
        tf = [sb(f"tf{i}", [128, 512], F32) for i in range(NTF)]
        ded = [sb(f"ded{i}", [128, 512], F32) for i in range(5)]
        NTB = 4
        tb = [sb(f"tb{i}", [128, 512], BF16) for i in range(NTB)]
        stt = sb("stt", [128, 64], F32)
        constT = [sb("constT0", [128, 384], F32)] * 2
        gqk = [sb("gqk0", [128, 128], F32)] * 2
        ident = sb("ident", [128, 128], F32)
        identb = sb("identb", [128, 128], BF16)
        onesf = sb("onesf", [128, 128], F32)
        onesb = sb("onesb", [128, 128], BF16)
        rope = sb("rope", [128, 18, 32], F32)
        ropet = sb("ropet", [128, 8, 16], F32)
        metaG = sb("metaG", [128, 8, 16], F32)
        hist = sb("hist", [128, 8, 30], F32)
        lamt = sb("lamt", [128, 16], F32)
        gAs = sb("gAs", [128, 4], F32)

        psf = [psum(f"psf{i}", [128, 512], F32) for i in range(8)]

        cnt = {"tf": 0, "tb": 0, "st": 0, "pf": 0, "pb": 0, "ring": 0, "xn": 0, "kst": 0, "sc": 0, "acc": 0}

        def rr(name, n):
            i = cnt[name] % n
            cnt[name] += 1
            return i

        def a_tf():
            i = rr("tf", NTF)
            return tf[i], ("tf", i)

        def a_tb():
            i = rr("tb", NTB)
            return tb[i], ("tb", i)

        def a_st():
            i = rr("st", 8)
            return stt[:, i * 8:(i + 1) * 8], ("st", i)

        def a_pf():
            i = rr("pf", 8)
            return psf[i], ("pf", i)

        def a_pb():
            i = rr("pf", 8)
            return psf[i][:].bitcast(BF16), ("pf", i)

        dq = []

        def drain(n):
            k = 0
            while dq and k < n:
                dq.pop(0)()
                k += 1

        cv_next = {}

        def conv_chunks(mat):
            return WROWS[mat] // CVROWS[mat]

        def record_conversion(mat, l, ci):
            r0 = ci * CVROWS[mat]
            r1 = r0 + CVROWS[mat]
            S.op("pool", lambda e: e.dma_start(out=wb[mat][l, r0:r1, :], in_=wsrc[mat][l, r0:r1, :]),
                 reads=[], writes=[("wb", mat, l, ci), "cvchain"], dma="cv")

        def ensure_converted(mat, l, r0, r1):
            c0 = r0 // CVROWS[mat]
            c1 = (r1 - 1) // CVROWS[mat]
            nxt = cv_next.get((mat, l), 0)
            while nxt <= c1:
                record_conversion(mat, l, nxt)
                nxt += 1
            cv_next[(mat, l)] = nxt
            return [("wb", mat, l, c) for c in range(c0, c1 + 1)]

        conv_order = []
        for l in range(depth):
            for mat in ("in", "out", "up", "down"):
                for ci in range(conv_chunks(mat)):
                    conv_order.append((mat, l, ci))
        conv_pos = [0]

        def pump_conversions(n, upto_layer):
            k = 0
            while k < n and conv_pos[0] < len(conv_order):
                mat, l, ci = conv_order[conv_pos[0]]
                if l > upto_layer:
                    break
                if cv_next.get((mat, l), 0) <= ci:
                    ensure_converted(mat, l, ci * CVROWS[mat], (ci + 1) * CVROWS[mat])
                    k += 1
                conv_pos[0] += 1

        def load_piece(mat, l, rows, cols, view):
            r0, r1 = rows
            c0, c1 = cols
            res = ensure_converted(mat, l, r0, r1)
            i = rr("ring", NRING)
            nchunk = (r1 - r0) // 128
            ncol = c1 - c0
            src = wb[mat][l, r0:r1, c0:c1].rearrange("(c p) n -> p c n", p=128)
            dst = ring[i][:].rearrange("p c n -> p (c n)")[:, 0:nchunk * ncol].rearrange("p (c n) -> p c n", n=ncol)
            S.op("sp", lambda e: e.dma_start(out=dst, in_=src), reads=res, writes=[("ring", i)], dma=f"ring{i}")
            return dst, ("ring", i)

        S.op("sp", lambda e: e.dma_start(out=ident[:], in_=identd), writes=["ident"], dma="c0")
        S.op("sp", lambda e: e.dma_start(out=rope[:], in_=roped), writes=["rope"], dma="c1")
        S.op("dve", lambda e: e.tensor_copy(out=identb[:], in_=ident[:]), reads=["ident"], writes=["identb"])
        S.op("dve", lambda e: e.memset(onesf[:], 1.0), writes=["onesf"])
        S.op("dve", lambda e: e.memset(onesb[:], 1.0), writes=["onesb"])
        lA, rA = a_tf()
        lB, rB = a_tf()
        for j, (src, dstt, off) in enumerate([(lam_q1, lA, 0), (lam_q2, lA, 256), (lam_k1, lB, 0), (lam_k2, lB, 256)]):
            S.op("sp", lambda e, src=src, dstt=dstt, off=off: e.dma_start(
                out=dstt[:, off:off + 256], in_=src.rearrange("l d -> (l d)").partition_broadcast(128)),
                writes=[(rA if dstt is lA else rB)], dma=f"c{2 + j}")
        S.op("dve", lambda e: e.tensor_tensor(out=lA[:], in0=lA[:], in1=lB[:], op=ALU.mult), reads=[rA, rB], writes=[rA])
        S.op("dve", lambda e: e.tensor_reduce(out=lamt[:, 0:8], in_=lA[:].rearrange("p (g d) -> p g d", d=64), axis=AX.X, op=ALU.add),
             reads=[rA], writes=["lamt"])
        S.op("act", lambda e: e.activation(out=lamt[:, 0:8], in_=lamt[:, 0:8], func=AF.Exp), reads=["lamt"], writes=["lamt"])
        S.op("dve", lambda e: e.tensor_tensor(out=lamt[:, 8:12], in0=lamt[:, 0:4], in1=lamt[:, 4:8], op=ALU.subtract),
             reads=["lamt"], writes=["lamt"])
        for l in range(DEPTH):
            S.op("dve", lambda e, l=l: e.tensor_scalar(out=lamt[:, 12 + l:13 + l], in0=lamt[:, 8 + l:9 + l], scalar1=-1.0,
                                                       scalar2=-lam_init(l), op0=ALU.mult, op1=ALU.add),
                 reads=["lamt"], writes=["lamt"])

        def load_layer_consts(l):
            pstage = ded[2][:, 0:384].rearrange("p (g w) -> p g w", w=128)
            cT = constT[l % 2]
            rc = ("constT", 0)
            g = gqk[l % 2]
            rg = ("gqk", 0)
            S.op("dve", lambda e: e.memset(pstage[:], 0.0), writes=[("ded", 2)])
            loads = [
                (pstage[0:16, 0, :], norm1_g[l].rearrange("(c p) -> c p", p=128)),
                (pstage[16:32, 0, :], norm2_g[l].rearrange("(c p) -> c p", p=128)),
                (pstage[32:40, 0, :], conv_b[l].rearrange("(c p) -> c p", p=128)),
                (pstage[40:48, 0, :], conv_ln_g[l].rearrange("(c p) -> c p", p=128)),
                (pstage[48:56, 0, :], conv_ln_b[l].rearrange("(c p) -> c p", p=128)),
                (pstage[56:57, 0, :], attn_norm_g[l].rearrange("(c p) -> c p", p=128)),
                (pstage[0:128, 1, :], conv_w[l].rearrange("j (c p) -> (j c) p", p=128)[0:128, :]),
                (pstage[0:120, 2, :], conv_w[l].rearrange("j (c p) -> (j c) p", p=128)[128:248, :]),
            ]
            for (o, i_) in loads:
                S.op("sp", lambda e, o=o, i_=i_: e.dma_start(out=o, in_=i_), writes=[("ded", 2)], dma="pst")
            pf, rpf = a_pf()
            for gi in range(3):
                S.op("pe", lambda e, gi=gi: e.transpose(out=pf[:, gi * 128:(gi + 1) * 128], in_=pstage[:, gi, :], identity=ident[:]),
                     reads=[("ded", 2), "ident"], writes=[rpf])
            S.op("dve", lambda e: e.tensor_copy(out=cT[:], in_=pf[:, 0:384]), reads=[rpf], writes=[rc])
            S.op("sp", lambda e: e.dma_start(out=g[:, 0:64], in_=q_norm_g[l].partition_broadcast(128)), writes=[rg], dma="gq")
            S.op("sp", lambda e: e.dma_start(out=g[:, 64:128], in_=k_norm_g[l].partition_broadcast(128)), writes=[rg], dma="gq")
            S.op("dve", lambda e: e.tensor_scalar(out=gAs[:, l:l + 1], in0=cT[:, 56:57], scalar1=1.0 - lam_init(l), scalar2=None,
                                                  op0=ALU.mult), reads=[rc], writes=[("gAs", l)])

        def rmsnorm_to_big(l, i, gcol0, rc):
            cT = constT[l % 2]
            x = xres[:, i, :]
            xi = rr("xn", 1)
            xb, rxb = xn[xi], ("xn", xi)
            stv, rst = a_st()
            S.op("act", lambda e: e.activation(out=xb[:], in_=x, func=AF.Square, accum_out=stv[:, 0:1]),
                 reads=[("xres", i)], writes=[rxb, rst])
            S.op("act", lambda e: e.activation(out=stv[:, 1:2], in_=stv[:, 0:1], func=AF.Sqrt, scale=1.0 / D, bias=EPS),
                 reads=[rst], writes=[rst])
            S.op("dve", lambda e: e.reciprocal(out=stv[:, 2:3], in_=stv[:, 1:2]), reads=[rst], writes=[rst])
            S.op("act", lambda e: e.activation(out=xb[:], in_=x, func=AF.Copy, scale=stv[:, 2:3]),
                 reads=[("xres", i), rst], writes=[rxb])
            for half in range(2):
                pb, rpb = a_pb()
                for c8 in range(8):
                    c = half * 8 + c8
                    S.op("pe", lambda e, c=c, c8=c8, pb=pb: e.transpose(out=pb[:, c8 * 128:(c8 + 1) * 128],
                                                                      in_=xb[:, c * 128:(c + 1) * 128], identity=identb[:]),
                         reads=[rxb, "identb"], writes=[rpb])
                S.op("dve", lambda e, half=half, pb=pb: e.tensor_tensor(
                    out=big[:, half * 8:(half + 1) * 8, i * 128:(i + 1) * 128],
                    in0=pb[:].rearrange("p (c t) -> p c t", t=128),
                    in1=cT[:, gcol0 + half * 8:gcol0 + (half + 1) * 8].unsqueeze(2).to_broadcast([128, 8, 128]),
                    op=ALU.mult), reads=[rpb, rc], writes=[("big", half * 8 + c8, i) for c8 in range(8)])

        def kslot(t):
            if t.kind == "M":
                return 0, 0
            if t.kind == "S":
                return NMETA + 128 * t.g, 1 + t.g
            if t.kind == "A":
                return NMETA + PAST, 17
            return NMETA + PAST + DSEQ, 18

        def kv_out_aps(l, t, which, c0):
            if t.kind == "S":
                dst = (kp if which == "k" else vp)[l, t.s, NMETA + 128 * t.g:NMETA + 128 * (t.g + 1), c0:c0 + 512]
                return [(dst, 128)]
            if t.kind == "M":
                o = kp if which == "k" else vp
                return [(o[l, 0, 0:NMETA, c0:c0 + 512], NMETA), (o[l, 1, 0:NMETA, c0:c0 + 512], NMETA)]
            o = kso if which == "k" else vso
            return [(o[l, t.s, :, c0:c0 + 512], DSEQ)]

        def attend_jobs(l, jobs, drain_total):
            LA, DN = 2, min(8, 3 * min(len(j[3]) for j in jobs) - 1)
            SCB = [0, 1, 6, 7]
            seq = [(ji, s, ei) for ji, job in enumerate(jobs) for s in range(2) for ei in range(len(job[3]))]
            K = len(seq)
            state = {}
            accb = {}
            pending = []

            def tbuf(ji):
                a = [0, 1, 3, 4]
                i1, i2 = a[2 * (ji % 2)], a[2 * (ji % 2) + 1]
                return ded[i1], ("ded", i1), ded[i2], ("ded", i2)

            def T(k):
                ji, s, ei = seq[k]
                h, qc0, nq, entries = jobs[ji]
                kc0, nk, vtile, qoff, diag = entries[ei]
                n = nq - qoff
                sci = SCB[rr("sc", 4)]
                psc, rsc = psf[sci], ("pf", sci)
                S.op("pe", lambda e: e.matmul(psc[0:nk, 0:n], lhsT=kT[64 * s:64 * s + 64, h, kc0:kc0 + nk],
                                              rhs=qT[64 * s:64 * s + 64, h, qc0 + qoff:qc0 + nq], start=True, stop=True),
                     reads=[("kT", h), ("qT", h)], writes=[rsc])
                pt, rpt = a_tb()
                S.op("act", lambda e: e.activation(out=pt[0:nk, 0:n], in_=psc[0:nk, 0:n], func=AF.Exp, scale=0.125),
                     reads=[rsc], writes=[rpt])
                if diag:
                    S.op("dve", lambda e: e.memset(pt[64:128, 0:64], 0.0), writes=[rpt])
                state[k] = (pt, rpt)

            def C(k):
                ji, s, ei = seq[k]
                h, qc0, nq, entries = jobs[ji]
                ne = len(entries)
                kc0, nk, vtile, qoff, diag = entries[ei]
                n = nq - qoff
                if ei == 0:
                    accb[(ji, s)] = rr("acc", 2)
                par = accb[(ji, s)]
                po, rpo = psf[2 + 2 * par], ("pf", 2 + 2 * par)
                pm, rpm = psf[3 + 2 * par], ("pf", 3 + 2 * par)
                pt, rpt = state.pop(k)
                S.op("pe", lambda e: e.matmul(po[:, qoff:nq], lhsT=vS[0:nk, vtile, h * 128:(h + 1) * 128], rhs=pt[0:nk, 0:n],
                                              start=(ei == 0), stop=(ei == ne - 1)), reads=[("vS", vtile), rpt], writes=[rpo])
                S.op("pe", lambda e: e.matmul(pm[:, qoff:nq], lhsT=onesb[0:nk, :], rhs=pt[0:nk, 0:n],
                                              start=(ei == 0), stop=(ei == ne - 1)), reads=["onesb", rpt], writes=[rpm])
                if ei < ne - 1:
                    return
                T1, r1, T2, r2 = tbuf(ji)
                rt, rrt = a_tf()
                S.op("dve", lambda e: e.reciprocal(out=rt[:, 0:nq], in_=pm[:, 0:nq]), reads=[rpm], writes=[rrt])
                Tt, rT = (T1, r1) if s == 0 else (T2, r2)
                S.op("dve", lambda e: e.tensor_tensor(out=Tt[:, 0:nq], in0=po[:, 0:nq], in1=rt[:, 0:nq], op=ALU.mult),
                     reads=[rpo, rrt], writes=[rT])
                if s == 1:
                    S.op("dve", lambda e: e.scalar_tensor_tensor(out=T1[:, 0:nq], in0=T2[:, 0:nq], scalar=lamt[:, 12 + l:13 + l],
                                                                 in1=T1[:, 0:nq], op0=ALU.mult, op1=ALU.add),
                         reads=[r1, r2, "lamt"], writes=[r1])
                    S.op("act", lambda e: e.activation(out=T2[:, 0:nq], in_=T1[:, 0:nq], func=AF.Square), reads=[r1], writes=[r2])
                    pending.append((k + DN, ji))

            def N(ji):
                h, qc0, nq, entries = jobs[ji]
                T1, r1, T2, r2 = tbuf(ji)
                sci = SCB[rr("sc", 4)]
                psc, rsc = psf[sci], ("pf", sci)
                S.op("pe", lambda e: e.matmul(psc[:, 0:nq], lhsT=onesf[:], rhs=T2[:, 0:nq], start=True, stop=True),
                     reads=["onesf", r2], writes=[rsc])
                S.op("act", lambda e: e.activation(out=T2[:, 0:nq], in_=psc[:, 0:nq], func=AF.Sqrt, scale=1.0 / 128, bias=EPS),
                     reads=[rsc], writes=[r2])
                S.op("dve", lambda e: e.reciprocal(out=T2[:, 0:nq], in_=T2[:, 0:nq]), reads=[r2], writes=[r2])
                S.op("dve", lambda e: e.tensor_tensor(out=T1[:, 0:nq], in0=T1[:, 0:nq], in1=T2[:, 0:nq], op=ALU.mult),
                     reads=[r1, r2], writes=[r1])
                ti0, ti1 = qc0 // 128, (qc0 + nq - 1) // 128
                S.op("act", lambda e: e.activation(out=big[:, h, qc0:qc0 + nq], in_=T1[:, 0:nq], func=AF.Copy, scale=gAs[:, l:l + 1]),
                     reads=[r1, ("gAs", l)], writes=[("big", h, ti) for ti in range(ti0, ti1 + 1)])

            for k in range(min(LA, K)):
                T(k)
            quota = 0.0
            for k in range(K):
                if k + LA < K:
                    T(k + LA)
                C(k)
                quota += drain_total / K
                drain(int(quota))
                quota -= int(quota)
                while pending and pending[0][0] <= k:
                    N(pending.pop(0)[1])
            for (_, ji) in pending:
                N(ji)

        def run_block(l, bi, tiles):
            NT = len(tiles)
            NTOK = NT * 128
            isx = tiles[0].kind != "S"
            cT = constT[l % 2]
            rc = ("constT", 0)
            last = (l == depth - 1)

            for i, t in enumerate(tiles):
                if l == 0:
                    if t.kind == "S":
                        src, nv = xp[t.s, t.g * 128:(t.g + 1) * 128, :], 128
                    elif t.kind == "M":
                        src, nv = meta, NMETA
                    else:
                        src, nv = xs[t.s], DSEQ
                    if nv < 128:
                        S.op("dve", lambda e, i=i: e.memset(xres[:, i, :], 0.0), writes=[("xres", i)])
                else:
                    src, nv = xscr[t.tid * 128:(t.tid + 1) * 128, :], 128
                S.op("sp", lambda e, i=i, src=src, nv=nv: e.dma_start(out=xres[0:nv, i, :], in_=src),
                     writes=[("xres", i)], dma=f"x{i}")
                rmsnorm_to_big(l, i, 0, rc)
            bigall = [("big", c, i) for c in range(16) for i in range(NT)]

            def bigk(kc):
                return [("big", kc, i) for i in range(NT)]

            def cache_fill(t):
                S.op("pool", lambda e, t=t: e.dma_start(out=vS[:, 1:17, :], in_=cv[l, t.s].rearrange("(t p) n -> p t n", p=128)),
                     writes=[("vS", 1 + j) for j in range(16)], dma="cvl")
                for j in range(16):
                    ki = rr("kst", 2)
                    S.op("pool", lambda e, t=t, j=j, ki=ki: e.dma_start(out=kst[ki][:], in_=ck[l, t.s, j * 128:(j + 1) * 128, :]),
                         writes=[("kst", ki)], dma=f"kst{ki}")
                    pb, rpb = a_pb()
                    for hh in range(8):
                        S.op("pe", lambda e, hh=hh, ki=ki, pb=pb: e.transpose(out=pb[:, hh * 128:(hh + 1) * 128],
                                                                            in_=kst[ki][:, hh * 128:(hh + 1) * 128], identity=identb[:]),
                             reads=[("kst", ki), "identb"], writes=[rpb])
                    S.op("dve" if j % 2 else "act", (lambda e, pb=pb, j=j: e.tensor_copy(
                        out=kT[:, :, NMETA + j * 128:NMETA + (j + 1) * 128], in_=pb[:].rearrange("p (h t) -> p h t", t=128)))
                        if j % 2 else (lambda e, pb=pb, j=j: e.activation(
                            out=kT[:, :, NMETA + j * 128:NMETA + (j + 1) * 128], in_=pb[:].rearrange("p (h t) -> p h t", t=128), func=AF.Copy)),
                        reads=[rpb], writes=[("kT", hh) for hh in range(8)])

            if isx:
                cache_fill(tiles[0])

            if isx:
                segs = [(i * 158, 128, i * 128) for i in range(NT)]
                SW = 158
            else:
                segs = [(0, 512, 0)]
                SW = 542
            NS = len(segs)

            allg = [("gT", c) for c in range(8)]
            if isx:
                for i, t in enumerate(tiles):
                    h0 = segs[i][0]
                    if t.kind == "M":
                        S.op("dve", lambda e, h0=h0: e.memset(gT[:, :, h0:h0 + 30], 0.0), writes=allg)
                    else:
                        for half in range(2):
                            stg, rstg = a_tf()
                            S.op("sp", lambda e, t=t, half=half, stg=stg: e.dma_start(
                                out=stg[0:30, :], in_=sc[l, t.s, :, half * 512:(half + 1) * 512]), writes=[rstg], dma=f"tf{rstg[1]}")
                            pf, rpf = a_pf()
                            for c4 in range(4):
                                S.op("pe", lambda e, c4=c4, stg=stg, pf=pf: e.transpose(
                                    out=pf[:, c4 * 32:c4 * 32 + 30], in_=stg[0:30, c4 * 128:(c4 + 1) * 128], identity=ident[0:30, 0:30]),
                                    reads=[rstg, "ident"], writes=[rpf])
                            S.op("dve", lambda e, half=half, h0=h0, pf=pf: e.tensor_copy(
                                out=gT[:, half * 4:(half + 1) * 4, h0:h0 + 30],
                                in_=pf[:, 0:128].rearrange("p (c w) -> p c w", w=32)[:, :, 0:30]),
                                reads=[rpf], writes=[("gT", half * 4 + c4) for c4 in range(4)])
            else:
                if tiles[0].g == 0:
                    S.op("dve", lambda e: e.memset(gT[:, :, 0:14], 0.0), writes=allg)
                    S.op("dve", lambda e: e.tensor_copy(out=gT[:, :, 14:30], in_=metaG[:]), reads=[("metaG", 0), ("metaG", 1)], writes=allg)
                else:
                    S.op("dve", lambda e: e.tensor_copy(out=gT[:, :, 0:30], in_=hist[:]), reads=[("hist", 0), ("hist", 1)], writes=allg)

            nseg_tok = NTOK // NS

            def state_half(dst, col0, half):
                pf, rpf = a_pf()
                for c4 in range(4):
                    c = half * 4 + c4
                    S.op("pe", lambda e, c=c, c4=c4, pf=pf: e.transpose(
                        out=pf[0:30, c4 * 128:(c4 + 1) * 128], in_=gT[:, c, col0:col0 + 30], identity=ident[:]),
                        reads=[("gT", c), "ident"], writes=[rpf])
                stg, rstg = a_tf()
                S.op("act", lambda e, pf=pf, stg=stg: e.activation(out=stg[0:30, :], in_=pf[0:30, :], func=AF.Copy),
                     reads=[rpf], writes=[rstg])
                S.op("act", lambda e, half=half, stg=stg: e.dma_start(out=dst[:, half * 512:(half + 1) * 512], in_=stg[0:30, :]),
                     reads=[rstg], dma=f"tf{rstg[1]}")

            def save_half(half):
                hg = [("gT", half * 4 + c4) for c4 in range(4)]
                cs = slice(half * 4, half * 4 + 4)
                if isx:
                    for i, t in enumerate(tiles):
                        h0 = segs[i][0]
                        if t.kind == "M":
                            S.op("dve", lambda e, h0=h0, cs=cs: e.tensor_copy(out=metaG[:, cs, :], in_=gT[:, cs, h0 + 30:h0 + 46]),
                                 reads=hg, writes=[("metaG", half)])
                        else:
                            state_half(cso[l, t.s], h0 + 30 + 34, half)
                else:
                    if tiles[0].g == 12:
                        state_half(cp[l, tiles[0].s], 512, half)
                    else:
                        S.op("dve", lambda e, cs=cs: e.tensor_copy(out=hist[:, cs, :], in_=gT[:, cs, 512:542]), reads=hg, writes=[("hist", half)])

            def conv_taps(c):
                yt, ryt = ded[2], ("ded", 2)
                yv = yt[:, 0:NTOK].rearrange("p (s w) -> p s w", s=NS)

                def gview(j):
                    return gT[:, c, 0:NS * SW].rearrange("p (s w) -> p s w", w=SW)[:, :, j:j + nseg_tok]

                wc = lambda j: cT[:, 128 + j * 8 + c:128 + j * 8 + c + 1]
                dq.append(lambda yv=yv, g0=gview(0), w0=wc(0): S.op("dve", lambda e: e.tensor_scalar(
                    out=yv, in0=g0, scalar1=w0, scalar2=cT[:, 32 + c:33 + c], op0=ALU.mult, op1=ALU.add),
                    reads=[("gT", c), rc], writes=[ryt]))
                for j in range(1, CONVK):
                    outv = yv if j < CONVK - 1 else gview(30)
                    dq.append(lambda j=j, yv=yv, gj=gview(j), wj=wc(j), outv=outv: S.op("dve", lambda e: e.scalar_tensor_tensor(
                        out=outv, in0=gj, scalar=wj, in1=yv, op0=ALU.mult, op1=ALU.add),
                        reads=[("gT", c), rc, ryt], writes=[ryt] if j < CONVK - 1 else [("gT", c)]))

            for j in range(4):
                banks = {}
                for kind, col0 in (("a", 3072), ("g", 4096)):
                    wt, rw = load_piece("in", l, (0, D), (col0 + j * 256, col0 + (j + 1) * 256), "kc")
                    for cc in range(2):
                        pp, rp = a_pf()
                        banks[(kind, cc)] = (pp, rp)
                        for kc in range(16):
                            S.op("pe", lambda e, wt=wt, pp=pp, kc=kc, cc=cc: e.matmul(
                                pp[:, 0:NTOK], lhsT=wt[:, kc, cc * 128:(cc + 1) * 128], rhs=big[:, kc, 0:NTOK],
                                start=(kc == 0), stop=(kc == 15)), reads=[rw] + bigk(kc), writes=[rp])
                for cc in range(2):
                    c = 2 * j + cc
                    pa, rpa = banks[("a", cc)]
                    pg, rpg = banks[("g", cc)]
                    sg, rsg = a_tf()
                    S.op("act", lambda e, pg=pg, sg=sg: e.activation(out=sg[:, 0:NTOK], in_=pg[:, 0:NTOK], func=AF.Sigmoid),
                         reads=[rpg], writes=[rsg])
                    S.op("dve", lambda e, pa=pa, sg=sg, c=c: e.tensor_tensor(
                        out=gT[:, c, 0:NS * SW].rearrange("p (s w) -> p s w", w=SW)[:, :, 30:30 + NTOK // NS],
                        in0=pa[:, 0:NTOK].rearrange("p (s w) -> p s w", s=NS),
                        in1=sg[:, 0:NTOK].rearrange("p (s w) -> p s w", s=NS), op=ALU.mult),
                        reads=[rpa, rsg], writes=[("gT", c)])
                    drain(14)
                if j % 2 == 1:
                    half = j // 2
                    save_half(half)
                    for c4 in range(4):
                        conv_taps(half * 4 + c4)

            n2b = 5 if (isx or tiles[0].g == 0) else (1 if tiles[0].g == 4 else 0)
            for cb in range(6):
                zb_ = [a_pf() for _ in tiles]
                for kh in range(2):
                    wt, rw = load_piece("in", l, (kh * 1024, (kh + 1) * 1024), (cb * 512, (cb + 1) * 512), "kc")
                    for i, t in enumerate(tiles):
                        pz, rpz = zb_[i]
                        for k8 in range(8):
                            kc = kh * 8 + k8
                            S.op("pe", lambda e, wt=wt, kc=kc, k8=k8, pz=pz, i=i: e.matmul(
                                pz[:], lhsT=big[:, kc, i * 128:(i + 1) * 128], rhs=wt[:, k8, :],
                                start=(kc == 0), stop=(kc == 15)), reads=[rw, ("big", kc, i)], writes=[rpz])
                for i, t in enumerate(tiles):
                    pz, rpz = zb_[i]
                    drain(n2b)
                    kc0, vt = kslot(t)
                    if cb < 4:
                        isk = cb >= 2
                        h0 = 4 * (cb % 2)
                        sq, rsq = a_tf()
                        S.op("act", lambda e, pz=pz, sq=sq: e.activation(out=sq[:], in_=pz[:], func=AF.Square), reads=[rpz], writes=[rsq])
                        stv, rst = a_st()
                        S.op("dve", lambda e, sq=sq, stv=stv: e.tensor_reduce(out=stv, in_=sq[:].rearrange("p (g d) -> p g d", d=64),
                                                                            axis=AX.X, op=ALU.add), reads=[rsq], writes=[rst])
                        S.op("act", lambda e, stv=stv: e.activation(out=stv, in_=stv, func=AF.Sqrt, scale=1.0 / 64, bias=EPS),
                             reads=[rst], writes=[rst])
                        S.op("dve", lambda e, stv=stv: e.reciprocal(out=stv, in_=stv), reads=[rst], writes=[rst])
                        z, rz = a_tf()
                        z3 = z[:].rearrange("p (g d) -> p g d", d=64)
                        S.op("dve", lambda e, pz=pz, z3=z3, stv=stv: e.tensor_tensor(
                            out=z3, in0=pz[:].rearrange("p (g d) -> p g d", d=64),
                            in1=stv.unsqueeze(2).to_broadcast([128, 8, 64]), op=ALU.mult), reads=[rpz, rst], writes=[rz])
                        gsl = gqk[l % 2][:, 64:128] if isk else gqk[l % 2][:, 0:64]
                        S.op("dve", lambda e, z3=z3, gsl=gsl: e.tensor_tensor(
                            out=z3, in0=z3, in1=gsl.unsqueeze(1).to_broadcast([128, 8, 64]), op=ALU.mult),
                            reads=[rz, ("gqk", 0)], writes=[rz])
                        cs = rope[:, t.ropei, 0:16].unsqueeze(1).to_broadcast([128, 8, 16])
                        sn = rope[:, t.ropei, 16:24].unsqueeze(1).to_broadcast([128, 8, 8])
                        sp_ = rope[:, t.ropei, 24:32].unsqueeze(1).to_broadcast([128, 8, 8])
                        S.op("dve", lambda e, z3=z3, sn=sn: e.tensor_tensor(out=ropet[:, :, 0:8], in0=z3[:, :, 8:16], in1=sn, op=ALU.mult),
                             reads=[rz, "rope"], writes=["ropet"])
                        S.op("dve", lambda e, z3=z3, sp_=sp_: e.tensor_tensor(out=ropet[:, :, 8:16], in0=z3[:, :, 0:8], in1=sp_, op=ALU.mult),
                             reads=[rz, "rope"], writes=["ropet"])
                        S.op("dve", lambda e, z3=z3, cs=cs: e.tensor_tensor(out=z3[:, :, 0:16], in0=z3[:, :, 0:16], in1=cs, op=ALU.mult),
                             reads=[rz, "rope"], writes=[rz])
                        S.op("dve", lambda e, z3=z3: e.tensor_tensor(out=z3[:, :, 0:16], in0=z3[:, :, 0:16], in1=ropet[:], op=ALU.add),
                             reads=[rz, "ropet"], writes=[rz])
                        zb, rzb = a_tb()
                        S.op("act", lambda e, z=z, zb=zb: e.activation(out=zb[:], in_=z[:], func=AF.Copy), reads=[rz], writes=[rzb])
                        if isk:
                            for (dst, nr) in kv_out_aps(l, t, "k", (cb % 2) * 512):
                                S.op("act", lambda e, dst=dst, nr=nr, z=z: e.dma_start(out=dst, in_=z[0:nr, :]), reads=[rz], dma=f"tf{rz[1]}")
                        pb, rpb = a_pb()
                        for hh in range(4):
                            S.op("pe", lambda e, hh=hh, zb=zb, pb=pb: e.transpose(out=pb[:, hh * 128:(hh + 1) * 128],
                                                                                in_=zb[:, hh * 128:(hh + 1) * 128], identity=identb[:]),
                                 reads=[rzb, "identb"], writes=[rpb])
                        if isk:
                            nv = t.nv
                            S.op("act", lambda e, pb=pb, h0=h0, kc0=kc0, nv=nv: e.activation(
                                out=kT[:, h0:h0 + 4, kc0:kc0 + nv],
                                in_=pb[:, 0:512].rearrange("p (h t) -> p h t", t=128)[:, :, 0:nv], func=AF.Copy),
                                reads=[rpb], writes=[("kT", h0 + hh) for hh in range(4)])
                        else:
                            S.op("act", lambda e, pb=pb, h0=h0, i=i: e.activation(
                                out=qT[:, h0:h0 + 4, i * 128:(i + 1) * 128],
                                in_=pb[:, 0:512].rearrange("p (h t) -> p h t", t=128), func=AF.Copy),
                                reads=[rpb], writes=[("qT", h0 + hh) for hh in range(4)])
                    else:
                        c0 = (cb % 2) * 512
                        vf, rvf = a_tf()
                        S.op("act", lambda e, pz=pz, vf=vf: e.activation(out=vf[:], in_=pz[:], func=AF.Copy), reads=[rpz], writes=[rvf])
                        S.op("dve", lambda e, pz=pz, vt=vt, c0=c0: e.tensor_copy(out=vS[:, vt, c0:c0 + 512], in_=pz[:]),
                             reads=[rpz], writes=[("vS", vt)])
                        for (dst, nr) in kv_out_aps(l, t, "v", c0):
                            S.op("act", lambda e, dst=dst, nr=nr, vf=vf: e.dma_start(out=dst, in_=vf[0:nr, :]), reads=[rvf], dma=f"tf{rvf[1]}")

            if isx:
                for i, t in enumerate(tiles):
                    if t.kind == "M":
                        continue
                    if i > 0:
                        cache_fill(t)
                    kc0, vt = kslot(t)
                    entries = [(0, NMETA, 0, 0, False)] + [(NMETA + 128 * j, 128, 1 + j, 0, False) for j in range(16)] + [(kc0, DSEQ, vt, 0, False)]
                    attend_jobs(l, [(h, i * 128, DSEQ, entries) for h in range(NH)], len(dq) // (2 - i))
                mi = [i for i, t in enumerate(tiles) if t.kind == "M"][0]
                attend_jobs(l, [(h, mi * 128, NMETA, [(0, NMETA, 0, 0, False)]) for h in range(NH)], 0)
                for i, t in enumerate(tiles):
                    S.op("dve", lambda e, i=i, t=t: e.memset(big[:, 0:8, i * 128 + t.nv:(i + 1) * 128], 0.0),
                         writes=[("big", h, i) for h in range(8)])
            else:
                g0 = tiles[0].g
                entries = [(0, NMETA, 0, 0, False)] + [(NMETA + 128 * g, 128, 1 + g, 0, False) for g in range(g0)]
                entries += [(NMETA + 128 * (g0 + i), 128, 1 + g0 + i, 128 * i, True) for i in range(NT)]
                attend_jobs(l, [(h, 0, NTOK, entries) for h in range(NH)], len(dq))

            drain(10 ** 6)
            def ytok(c):
                return gT[:, c, 0:NS * SW].rearrange("p (s w) -> p s w", w=SW)[:, :, 30:30 + nseg_tok]

            pm_, rpm_ = a_pf()
            pq_, rpq_ = a_pf()
            pm3 = pm_[:, 0:NTOK].rearrange("p (s w) -> p s w", s=NS)
            pq3 = pq_[:, 0:NTOK].rearrange("p (s w) -> p s w", s=NS)
            for c in range(8):
                ysq, rysq = a_tf()
                ysq3 = ysq[:, 0:NTOK].rearrange("p (s w) -> p s w", s=NS)
                S.op("act", lambda e, c=c, ysq3=ysq3: e.activation(out=ysq3, in_=ytok(c), func=AF.Square), reads=[("gT", c)], writes=[rysq])
                S.op("pe", lambda e, c=c: e.matmul(pm3, lhsT=onesf[:], rhs=ytok(c), start=(c == 0), stop=(c == 7)),
                     reads=["onesf", ("gT", c)], writes=[rpm_])
                S.op("pe", lambda e, c=c, ysq3=ysq3: e.matmul(pq3, lhsT=onesf[:], rhs=ysq3, start=(c == 0), stop=(c == 7)),
                     reads=["onesf", rysq], writes=[rpq_])
            MEAN, RSTD = ded[0], ded[1]
            S.op("dve", lambda e: e.tensor_scalar(out=MEAN[:, 0:NTOK], in0=pm_[:, 0:NTOK], scalar1=1.0 / 1024, scalar2=None, op0=ALU.mult),
                 reads=[rpm_], writes=[("ded", 0)])
            S.op("dve", lambda e: e.tensor_tensor(out=RSTD[:, 0:NTOK], in0=MEAN[:, 0:NTOK], in1=MEAN[:, 0:NTOK], op=ALU.mult),
                 reads=[("ded", 0)], writes=[("ded", 1)])
            S.op("dve", lambda e: e.scalar_tensor_tensor(out=RSTD[:, 0:NTOK], in0=pq_[:, 0:NTOK], scalar=1.0 / 1024, in1=RSTD[:, 0:NTOK],
                                                         op0=ALU.mult, op1=ALU.subtract), reads=[rpq_, ("ded", 1)], writes=[("ded", 1)])
            S.op("dve", lambda e: e.tensor_scalar(out=RSTD[:, 0:NTOK], in0=RSTD[:, 0:NTOK], scalar1=0.0, scalar2=None, op0=ALU.max),
                 reads=[("ded", 1)], writes=[("ded", 1)])
            S.op("act", lambda e: e.activation(out=RSTD[:, 0:NTOK], in_=RSTD[:, 0:NTOK], func=AF.Sqrt, bias=EPS), reads=[("ded", 1)], writes=[("ded", 1)])
            S.op("dve", lambda e: e.reciprocal(out=RSTD[:, 0:NTOK], in_=RSTD[:, 0:NTOK]), reads=[("ded", 1)], writes=[("ded", 1)])
            mean3 = MEAN[:, 0:NTOK].rearrange("p (s w) -> p s w", s=NS)
            rstd3 = RSTD[:, 0:NTOK].rearrange("p (s w) -> p s w", s=NS)
            cbuf = []
            for c in range(8):
                n_, rn_ = a_tf()
                n3 = n_[:, 0:NTOK].rearrange("p (s w) -> p s w", s=NS)
                S.op("dve", lambda e, c=c, n3=n3: e.tensor_tensor(out=n3, in0=ytok(c), in1=mean3, op=ALU.subtract),
                     reads=[("gT", c), ("ded", 0)], writes=[rn_])
                S.op("dve", lambda e, n3=n3: e.tensor_tensor(out=n3, in0=n3, in1=rstd3, op=ALU.mult), reads=[rn_, ("ded", 1)], writes=[rn_])
                cbuf.append((c, n_, rn_))
                S.op("act", lambda e, c=c, n_=n_: e.activation(out=big[:, 8 + c, 0:NTOK], in_=n_[:, 0:NTOK], func=AF.Silu,
                                                              scale=cT[:, 40 + c:41 + c], bias=cT[:, 48 + c:49 + c]),
                     reads=[rn_, rc], writes=[("big", 8 + c, i) for i in range(NT)])

            for cb in range(4):
                zb_ = [a_pf() for _ in range(NT)]
                for kh in range(2):
                    wt, rw = load_piece("out", l, (kh * 1024, (kh + 1) * 1024), (cb * 512, (cb + 1) * 512), "kc")
                    for i in range(NT):
                        pz, rpz = zb_[i]
                        for k8 in range(8):
                            kc = kh * 8 + k8
                            S.op("pe", lambda e, wt=wt, kc=kc, k8=k8, pz=pz, i=i: e.matmul(
                                pz[:], lhsT=big[:, kc, i * 128:(i + 1) * 128], rhs=wt[:, k8, :],
                                start=(kc == 0), stop=(kc == 15)), reads=[rw, ("big", kc, i)], writes=[rpz])
                for i in range(NT):
                    pz, rpz = zb_[i]
                    S.op("dve", lambda e, pz=pz, i=i, cb=cb: e.tensor_tensor(out=xres[:, i, cb * 512:(cb + 1) * 512], in0=pz[:],
                                                                          in1=xres[:, i, cb * 512:(cb + 1) * 512], op=ALU.add),
                         reads=[rpz, ("xres", i)], writes=[("xres", i)])
            for i in range(NT):
                rmsnorm_to_big(l, i, 16, rc)

            NG = 16

            def mlp_up(g):
                ub_ = [a_pf() for _ in range(4)]
                for kh in range(2):
                    wt, rw = load_piece("up", l, (kh * 1024, (kh + 1) * 1024), (g * 512, (g + 1) * 512), "kc")
                    for fc in range(4):
                        pu, rpu = ub_[fc]
                        for k8 in range(8):
                            kc = kh * 8 + k8
                            S.op("pe", lambda e, wt=wt, kc=kc, k8=k8, fc=fc, pu=pu: e.matmul(
                                pu[:, 0:NTOK], lhsT=wt[:, k8, fc * 128:(fc + 1) * 128], rhs=big[:, kc, 0:NTOK],
                                start=(kc == 0), stop=(kc == 15)), reads=[rw] + bigk(kc), writes=[rpu])
                for fc in range(4):
                    pu, rpu = ub_[fc]
                    sq, rsq = a_tf()
                    S.op("act", lambda e, pu=pu, sq=sq: e.activation(out=sq[:, 0:NTOK], in_=pu[:, 0:NTOK], func=AF.Square),
                         reads=[rpu], writes=[rsq])
                    S.op("dve", lambda e, pu=pu, sq=sq, fc=fc: e.scalar_tensor_tensor(
                        out=qT[:, (g % 2) * 4 + fc, 0:NTOK], in0=pu[:, 0:NTOK], scalar=0.0, in1=sq[:, 0:NTOK], op0=ALU.is_gt, op1=ALU.mult),
                        reads=[rpu, rsq], writes=[("qT", (g % 2) * 4 + fc)])

            def mlp_down(g):
                halves = []
                for ch in range(2):
                    wd, rwd = load_piece("down", l, (g * 512, (g + 1) * 512), (ch * 1024, (ch + 1) * 1024), "kc")
                    halves.append((wd, rwd))
                for i in range(NT):
                    for cb in range(4):
                        wd, rwd = halves[cb // 2]
                        pz, rpz = a_pf()
                        for fc in range(4):
                            S.op("pe", lambda e, wd=wd, fc=fc, pz=pz, i=i, cb=cb: e.matmul(
                                pz[:], lhsT=qT[:, (g % 2) * 4 + fc, i * 128:(i + 1) * 128],
                                rhs=wd[:, fc, (cb % 2) * 512:(cb % 2 + 1) * 512], start=(fc == 0), stop=(fc == 3)),
                                reads=[rwd, ("qT", (g % 2) * 4 + fc)], writes=[rpz])
                        S.op("dve", lambda e, pz=pz, i=i, cb=cb: e.tensor_tensor(out=xres[:, i, cb * 512:(cb + 1) * 512], in0=pz[:],
                                                                              in1=xres[:, i, cb * 512:(cb + 1) * 512], op=ALU.add),
                             reads=[rpz, ("xres", i)], writes=[("xres", i)])

            mlp_up(0)
            for g in range(NG):
                if g + 1 < NG:
                    mlp_up(g + 1)
                mlp_down(g)

            for i, t in enumerate(tiles):
                if not last:
                    S.op("act", lambda e, i=i, t=t: e.dma_start(out=xscr[t.tid * 128:(t.tid + 1) * 128, :], in_=xres[:, i, :]),
                         reads=[("xres", i)], dma=f"xo{i}")
                elif t.kind == "S":
                    S.op("act", lambda e, i=i, t=t: e.dma_start(out=yp[t.s, t.g * 128:(t.g + 1) * 128, :], in_=xres[:, i, :]),
                         reads=[("xres", i)], dma=f"xo{i}")
                elif t.kind in ("A", "B"):
                    S.op("act", lambda e, i=i, t=t: e.dma_start(out=ys[t.s], in_=xres[0:DSEQ, i, :]), reads=[("xres", i)], dma=f"xo{i}")

        pump_conversions(8, 0)
        nconv_per_layer = sum(conv_chunks(m) for m in ("in", "out", "up", "down"))
        for l in range(depth):
            S.epoch = l
            load_layer_consts(l)
            for bi, tiles in enumerate(blocks):
                run_block(l, bi, tiles)
                if l == 0 and bi == 0:
                    pump_conversions(10 ** 6, 0)
                if l + 1 < depth:
                    pump_conversions((nconv_per_layer + len(blocks) - 1) // len(blocks) + 1, l + 1)
        S.emit()
        build_program.stats = dict(n_ops={e: len(S.ops[e]) for e in ENGS}, n_wait=S.n_wait,
                                   sbuf_left=nc.sbuf_bytes_remaining)
    return nc


def host_consts():
    ident = np.eye(128, dtype=np.float32)
    half = 8
    inv_freq = (np.float32(500000.0) ** (-(np.arange(half, dtype=np.float32) * np.float32(2.0)) / np.float32(16))).astype(np.float32)
    rope = np.zeros((128, 18, 32), np.float32)
    p = np.arange(128)
    for ti in range(18):
        if ti < 16:
            pos = NMETA + 128 * ti + p
        elif ti == 16:
            pos = NMETA + PAST + p
        else:
            pos = p
        ang = pos.astype(np.float32)[:, None] * inv_freq[None, :]
        c = np.cos(ang).astype(np.float32)
        s = np.sin(ang).astype(np.float32)
        rope[:, ti, 0:8] = c
        rope[:, ti, 8:16] = c
        rope[:, ti, 16:24] = -s
        rope[:, ti, 24:32] = s
    return ident, rope


_CACHE = {}


def kernel(x_prompt, x_sample, cache_k, cache_v, state_conv, meta_tokens, norm1_g, w_in,
           q_norm_g, k_norm_g, lam_q1, lam_k1, lam_q2, lam_k2, attn_norm_g, conv_w, conv_b,
           conv_ln_g, conv_ln_b, w_out, norm2_g, w_up, w_down):
    f = lambda a: np.ascontiguousarray(np.asarray(a, dtype=np.float32))
    x_prompt, x_sample, cache_k, cache_v, state_conv = map(f, (x_prompt, x_sample, cache_k, cache_v, state_conv))
    shared = dict(meta=f(meta_tokens), norm1_g=f(norm1_g), w_in=f(w_in), q_norm_g=f(q_norm_g), k_norm_g=f(k_norm_g),
                  lam_q1=f(lam_q1), lam_k1=f(lam_k1), lam_q2=f(lam_q2), lam_k2=f(lam_k2), attn_norm_g=f(attn_norm_g),
                  conv_w=f(conv_w), conv_b=f(conv_b), conv_ln_g=f(conv_ln_g), conv_ln_b=f(conv_ln_b), w_out=f(w_out),
                  norm2_g=f(norm2_g), w_up=f(w_up), w_down=f(w_down))
    ident, rope = host_consts()
    shared["identd"] = ident
    shared["roped"] = rope
    in_maps = []
    for c in range(NCORES):
        b = slice(2 * c, 2 * c + 2)
        m = dict(shared)
        m["xp"] = np.ascontiguousarray(x_prompt[b])
        m["xs"] = np.ascontiguousarray(x_sample[b])
        m["ck"] = np.ascontiguousarray(cache_k[:, b].reshape(DEPTH, 2, PAST, 1024))
        m["cv"] = np.ascontiguousarray(cache_v[:, b].reshape(DEPTH, 2, PAST, 1024))
        m["sc"] = np.ascontiguousarray(state_conv[:, b])
        in_maps.append(m)
    if "nc" not in _CACHE:
        _CACHE["nc"] = build_program()
    res = run_bass_kernel_spmd(_CACHE["nc"], in_maps, core_ids=list(range(NCORES)))
    R = res.results
    cat = lambda name, ax: np.concatenate([np.asarray(r[name]) for r in R], axis=ax)
    y_prompt = cat("yp", 0)
    y_sample = cat("ys", 0)
    k_prompt = cat("kp", 1).reshape(DEPTH, 16, L_P, NH, 128)
    v_prompt = cat("vp", 1).reshape(DEPTH, 16, L_P, NH, 128)
    conv_prompt = cat("cp", 1)
    k_sample = cat("kso", 1).reshape(DEPTH, 16, DSEQ, NH, 128)
    v_sample = cat("vso", 1).reshape(DEPTH, 16, DSEQ, NH, 128)
    conv_sample = cat("cso", 1)
    return (y_prompt, y_sample, k_prompt, v_prompt, conv_prompt, k_sample, v_sample, conv_sample)
```

```python
import contextlib
import math
import numpy as np
import concourse.bass as bass
import concourse.mybir as mybir
from concourse.bass_utils import run_bass_kernel_spmd

F32 = mybir.dt.float32
BF16 = mybir.dt.bfloat16
AF = mybir.ActivationFunctionType
ALU = mybir.AluOpType
AX = mybir.AxisListType

D = 2048
DEPTH = 4
SEQ = 2048
DSEQ = 64
PAST = 2048
NMETA = 16
NH = 8
INW = 5120
DFF = 8192
CONVK = 31
EPS = 1e-6
NCORES = 8
L_P = NMETA + SEQ

ENGS = ("pe", "act", "dve", "pool", "sp")


class Sched:
    def __init__(self, nc, stack, n_epochs=1):
        self.nc = nc
        self.stack = stack
        self.ops = {e: [] for e in ENGS}
        self.epoch = 0
        self.psem = {}
        for e in ENGS:
            if e == "sp":
                continue
            for ep in range(n_epochs):
                self.psem[(e, ep)] = stack.enter_context(nc.semaphore(f"p_{e}_{ep}"))
        self.pcount = {k: 0 for k in self.psem}
        self.dsem = {}
        self.dcount = {}
        self.last_w = {}
        self.readers = {}
        self.waited = {e: {} for e in ENGS}
        self.sem_owner = {}
        for (e, ep), s in self.psem.items():
            self.sem_owner[id(s)] = e
        self.n_wait = 0

    def _dma_sem(self, key):
        if key not in self.dsem:
            self.dsem[key] = self.stack.enter_context(self.nc.semaphore(f"d_{key}"))
            self.dcount[key] = 0
        return self.dsem[key]

    def op(self, eng, fn, reads=(), writes=(), dma=None):
        px = [r for r in reads if isinstance(r, tuple) and r[0] in ("pf", "pb")]
        if px:
            reads = [r for r in reads if r not in px]
            writes = list(writes) + [r for r in px if r not in writes]
        deps = {}

        def add(tok):
            if tok is None:
                return
            s, v = tok
            k = id(s)
            if k not in deps or deps[k][1] < v:
                deps[k] = (s, v)

        for r in reads:
            add(self.last_w.get(r))
        for w in writes:
            add(self.last_w.get(w))
            rd = self.readers.get(w)
            if rd:
                for tok in rd.values():
                    add(tok)
        waits = []
        wd = self.waited[eng]
        for k, (s, v) in deps.items():
            if eng == "pe" and self.sem_owner.get(k) == "pe":
                continue
            if wd.get(k, 0) >= v:
                continue
            wd[k] = v
            waits.append((s, v))
        self.n_wait += len(waits)
        if dma is not None:
            s = self._dma_sem(dma)
            self.dcount[dma] += 16
            tok = (s, self.dcount[dma])
            inc = (s, 16)
        else:
            key = (eng, self.epoch)
            s = self.psem[key]
            self.pcount[key] += 1
            tok = (s, self.pcount[key])
            inc = (s, 1)
        self.ops[eng].append((waits, fn, inc))
        for w in writes:
            self.last_w[w] = tok
            self.readers[w] = {}
        for r in reads:
            d = self.readers.setdefault(r, {})
            k = id(tok[0])
            if k not in d or d[k][1] < tok[1]:
                d[k] = tok
        return tok

    def emit(self):
        nc = self.nc
        final = [(self.dsem[k], self.dcount[k]) for k in self.dsem if self.dcount[k] > 0]
        ops = self.ops
        with nc.Block() as block:
            def run(e, name):
                for waits, fn, inc in ops[name]:
                    for s, v in waits:
                        e.wait_ge(s, v)
                    ins = fn(e)
                    ins.then_inc(inc[0], inc[1])

            @block.tensor
            def _(e):
                run(e, "pe")

            @block.scalar
            def _(e):
                run(e, "act")

            @block.vector
            def _(e):
                run(e, "dve")

            @block.gpsimd
            def _(e):
                run(e, "pool")

            @block.sync
            def _(e):
                run(e, "sp")
                for s, v in final:
                    e.wait_ge(s, v)


class Tile:
    def __init__(self, kind, s, g, nv, tid, ropei):
        self.kind, self.s, self.g, self.nv, self.tid, self.ropei = kind, s, g, nv, tid, ropei


def lam_init(l):
    return 0.8 - 0.6 * math.exp(-0.3 * l)


def build_program(depth=DEPTH, nblocks=None):
    nc = bass.Bass("TRN2", target_bir_lowering=False)

    def din(name, shape, dt=F32):
        return nc.dram_tensor(name, list(shape), dt, kind="ExternalInput").ap()

    def dout(name, shape):
        return nc.dram_tensor(name, list(shape), F32, kind="ExternalOutput").ap()

    xp = din("xp", [2, SEQ, D])
    xs = din("xs", [2, DSEQ, D])
    ck = din("ck", [DEPTH, 2, PAST, 1024])
    cv = din("cv", [DEPTH, 2, PAST, 1024])
    sc = din("sc", [DEPTH, 2, 30, 1024])
    meta = din("meta", [NMETA, D])
    norm1_g = din("norm1_g", [DEPTH, D])
    w_in = din("w_in", [DEPTH, D, INW])
    q_norm_g = din("q_norm_g", [DEPTH, 64])
    k_norm_g = din("k_norm_g", [DEPTH, 64])
    lam_q1 = din("lam_q1", [DEPTH, 64])
    lam_k1 = din("lam_k1", [DEPTH, 64])
    lam_q2 = din("lam_q2", [DEPTH, 64])
    lam_k2 = din("lam_k2", [DEPTH, 64])
    attn_norm_g = din("attn_norm_g", [DEPTH, 128])
    conv_w = din("conv_w", [DEPTH, CONVK, 1024])
    conv_b = din("conv_b", [DEPTH, 1024])
    conv_ln_g = din("conv_ln_g", [DEPTH, 1024])
    conv_ln_b = din("conv_ln_b", [DEPTH, 1024])
    w_out = din("w_out", [DEPTH, D, D])
    norm2_g = din("norm2_g", [DEPTH, D])
    w_up = din("w_up", [DEPTH, D, DFF])
    w_down = din("w_down", [DEPTH, DFF, D])
    identd = din("identd", [128, 128])
    roped = din("roped", [128, 18, 32])

    yp = dout("yp", [2, SEQ, D])
    ys = dout("ys", [2, DSEQ, D])
    kp = dout("kp", [DEPTH, 2, L_P, 1024])
    vp = dout("vp", [DEPTH, 2, L_P, 1024])
    cp = dout("cp", [DEPTH, 2, 30, 1024])
    kso = dout("kso", [DEPTH, 2, DSEQ, 1024])
    vso = dout("vso", [DEPTH, 2, DSEQ, 1024])
    cso = dout("cso", [DEPTH, 2, 30, 1024])

    NTID = 3 + 32
    xscr = nc.dram_tensor("xscr", [NTID * 128, D], F32).ap()
    wb = {
        "in": nc.dram_tensor("wb_in", [DEPTH, D, INW], BF16).ap(),
        "out": nc.dram_tensor("wb_out", [DEPTH, D, D], BF16).ap(),
        "up": nc.dram_tensor("wb_up", [DEPTH, D, DFF], BF16).ap(),
        "down": nc.dram_tensor("wb_down", [DEPTH, DFF, D], BF16).ap(),
    }
    wsrc = {"in": w_in, "out": w_out, "up": w_up, "down": w_down}
    CVROWS = {"in": 256, "out": 512, "up": 256, "down": 1024}
    WROWS = {"in": D, "out": D, "up": D, "down": DFF}

    XA = Tile("A", 0, 0, 64, 0, 16)
    XB = Tile("B", 1, 0, 64, 1, 16)
    XM = Tile("M", 0, 0, 16, 2, 17)
    blocks = [[XA, XB, XM]]
    for s in range(2):
        for j in range(4):
            blocks.append([Tile("S", s, 4 * j + i, 128, 3 + 16 * s + 4 * j + i, 4 * j + i) for i in range(4)])
    if nblocks is not None:
        blocks = blocks[:nblocks]

    with contextlib.ExitStack() as st:
        S = Sched(nc, st, n_epochs=DEPTH)

        def sb(name, shape, dt):
            return st.enter_context(nc.sbuf_tensor(name, list(shape), dt))

        def psum(name, shape, dt):
            return st.enter_context(nc.psum_tensor(name, list(shape), dt))

        xres = sb("xres", [128, 4, D], F32)
        big = sb("big", [128, 16, 512], BF16)
        qT = sb("qT", [128, 8, 512], BF16)
        KCOLS = NMETA + PAST + 2 * DSEQ
        kT = sb("kT", [128, 8, KCOLS], BF16)
        vS = sb("vS", [128, 19, 1024], BF16)
        GW = 30 + 512
        gT = sb("gT", [128, 8, GW], F32)
        NRING = 3
        ring = [sb(f"ring{i}", [128, 8, 512], BF16) for i in range(NRING)]
        xn = [sb(f"xn{i}", [128, D], BF16) for i in range(1)]
        kst = [sb(f"kst{i}", [128, 1024], BF16) for i in range(2)]
        NTF = 4
        tf = [sb(f"tf{i}", [128, 512], F32) for i in range(NTF)]
        ded = [sb(f"ded{i}", [128, 512], F32) for i in range(5)]
        NTB = 4
        tb = [sb(f"tb{i}", [128, 512], BF16) for i in range(NTB)]
        stt = sb("stt", [128, 64], F32)
        constT = [sb("constT0", [128, 384], F32)] * 2
        gqk = [sb("gqk0", [128, 128], F32)] * 2
        ident = sb("ident", [128, 128], F32)
        identb = sb("identb", [128, 128], BF16)
        onesf = sb("onesf", [128, 128], F32)
        onesb = sb("onesb", [128, 128], BF16)
        rope = sb("rope", [128, 18, 32], F32)
        ropet = sb("ropet", [128, 8, 16], F32)
        metaG = sb("metaG", [128, 8, 16], F32)
        hist = sb("hist", [128, 8, 30], F32)
        lamt = sb("lamt", [128, 16], F32)
        gAs = sb("gAs", [128, 4], F32)

        psf = [psum(f"psf{i}", [128, 512], F32) for i in range(8)]

        cnt = {"tf": 0, "tb": 0, "st": 0, "pf": 0, "pb": 0, "ring": 0, "xn": 0, "kst": 0, "sc": 0, "acc": 0}

        def rr(name, n):
            i = cnt[name] % n
            cnt[name] += 1
            return i

        def a_tf():
            i = rr("tf", NTF)
            return tf[i], ("tf", i)

        def a_tb():
            i = rr("tb", NTB)
            return tb[i], ("tb", i)

        def a_st():
            i = rr("st", 8)
            return stt[:, i * 8:(i + 1) * 8], ("st", i)

        def a_pf():
            i = rr("pf", 8)
            return psf[i], ("pf", i)

        def a_pb():
            i = rr("pf", 8)
            return psf[i][:].bitcast(BF16), ("pf", i)

        dq = []

        def drain(n):
            k = 0
            while dq and k < n:
                dq.pop(0)()
                k += 1

        cv_next = {}

        def conv_chunks(mat):
            return WROWS[mat] // CVROWS[mat]

        def record_conversion(mat, l, ci):
            r0 = ci * CVROWS[mat]
            r1 = r0 + CVROWS[mat]
            S.op("pool", lambda e: e.dma_start(out=wb[mat][l, r0:r1, :], in_=wsrc[mat][l, r0:r1, :]),
                 reads=[], writes=[("wb", mat, l, ci), "cvchain"], dma="cv")

        def ensure_converted(mat, l, r0, r1):
            c0 = r0 // CVROWS[mat]
            c1 = (r1 - 1) // CVROWS[mat]
            nxt = cv_next.get((mat, l), 0)
            while nxt <= c1:
                record_conversion(mat, l, nxt)
                nxt += 1
            cv_next[(mat, l)] = nxt
            return [("wb", mat, l, c) for c in range(c0, c1 + 1)]

        conv_order = []
        for l in range(depth):
            for mat in ("in", "out", "up", "down"):
                for ci in range(conv_chunks(mat)):
                    conv_order.append((mat, l, ci))
        conv_pos = [0]

        def pump_conversions(n, upto_layer):
            k = 0
            while k < n and conv_pos[0] < len(conv_order):
                mat, l, ci = conv_order[conv_pos[0]]
                if l > upto_layer:
                    break
                if cv_next.get((mat, l), 0) <= ci:
                    ensure_converted(mat, l, ci * CVROWS[mat], (ci + 1) * CVROWS[mat])
                    k += 1
                conv_pos[0] += 1

        def load_piece(mat, l, rows, cols, view):
            r0, r1 = rows
            c0, c1 = cols
            res = ensure_converted(mat, l, r0, r1)
            i = rr("ring", NRING)
            nchunk = (r1 - r0) // 128
            ncol = c1 - c0
            src = wb[mat][l, r0:r1, c0:c1].rearrange("(c p) n -> p c n", p=128)
            dst = ring[i][:].rearrange("p c n -> p (c n)")[:, 0:nchunk * ncol].rearrange("p (c n) -> p c n", n=ncol)
            S.op("sp", lambda e: e.dma_start(out=dst, in_=src), reads=res, writes=[("ring", i)], dma=f"ring{i}")
            return dst, ("ring", i)

        S.op("sp", lambda e: e.dma_start(out=ident[:], in_=identd), writes=["ident"], dma="c0")
        S.op("sp", lambda e: e.dma_start(out=rope[:], in_=roped), writes=["rope"], dma="c1")
        S.op("dve", lambda e: e.tensor_copy(out=identb[:], in_=ident[:]), reads=["ident"], writes=["identb"])
        S.op("dve", lambda e: e.memset(onesf[:], 1.0), writes=["onesf"])
        S.op("dve", lambda e: e.memset(onesb[:], 1.0), writes=["onesb"])
        lA, rA = a_tf()
        lB, rB = a_tf()
        for j, (src, dstt, off) in enumerate([(lam_q1, lA, 0), (lam_q2, lA, 256), (lam_k1, lB, 0), (lam_k2, lB, 256)]):
            S.op("sp", lambda e, src=src, dstt=dstt, off=off: e.dma_start(
                out=dstt[:, off:off + 256], in_=src.rearrange("l d -> (l d)").partition_broadcast(128)),
                writes=[(rA if dstt is lA else rB)], dma=f"c{2 + j}")
        S.op("dve", lambda e: e.tensor_tensor(out=lA[:], in0=lA[:], in1=lB[:], op=ALU.mult), reads=[rA, rB], writes=[rA])
        S.op("dve", lambda e: e.tensor_reduce(out=lamt[:, 0:8], in_=lA[:].rearrange("p (g d) -> p g d", d=64), axis=AX.X, op=ALU.add),
             reads=[rA], writes=["lamt"])
        S.op("act", lambda e: e.activation(out=lamt[:, 0:8], in_=lamt[:, 0:8], func=AF.Exp), reads=["lamt"], writes=["lamt"])
        S.op("dve", lambda e: e.tensor_tensor(out=lamt[:, 8:12], in0=lamt[:, 0:4], in1=lamt[:, 4:8], op=ALU.subtract),
             reads=["lamt"], writes=["lamt"])
        for l in range(DEPTH):
            S.op("dve", lambda e, l=l: e.tensor_scalar(out=lamt[:, 12 + l:13 + l], in0=lamt[:, 8 + l:9 + l], scalar1=-1.0,
                                                       scalar2=-lam_init(l), op0=ALU.mult, op1=ALU.add),
                 reads=["lamt"], writes=["lamt"])

        def load_layer_consts(l):
            pstage = ded[2][:, 0:384].rearrange("p (g w) -> p g w", w=128)
            cT = constT[l % 2]
            rc = ("constT", 0)
            g = gqk[l % 2]
            rg = ("gqk", 0)
            S.op("dve", lambda e: e.memset(pstage[:], 0.0), writes=[("ded", 2)])
            loads = [
                (pstage[0:16, 0, :], norm1_g[l].rearrange("(c p) -> c p", p=128)),
                (pstage[16:32, 0, :], norm2_g[l].rearrange("(c p) -> c p", p=128)),
                (pstage[32:40, 0, :], conv_b[l].rearrange("(c p) -> c p", p=128)),
                (pstage[40:48, 0, :], conv_ln_g[l].rearrange("(c p) -> c p", p=128)),
                (pstage[48:56, 0, :], conv_ln_b[l].rearrange("(c p) -> c p", p=128)),
                (pstage[56:57, 0, :], attn_norm_g[l].rearrange("(c p) -> c p", p=128)),
                (pstage[0:128, 1, :], conv_w[l].rearrange("j (c p) -> (j c) p", p=128)[0:128, :]),
                (pstage[0:120, 2, :], conv_w[l].rearrange("j (c p) -> (j c) p", p=128)[128:248, :]),
            ]
            for (o, i_) in loads:
                S.op("sp", lambda e, o=o, i_=i_: e.dma_start(out=o, in_=i_), writes=[("ded", 2)], dma="pst")
            pf, rpf = a_pf()
            for gi in range(3):
                S.op("pe", lambda e, gi=gi: e.transpose(out=pf[:, gi * 128:(gi + 1) * 128], in_=pstage[:, gi, :], identity=ident[:]),
                     reads=[("ded", 2), "ident"], writes=[rpf])
            S.op("dve", lambda e: e.tensor_copy(out=cT[:], in_=pf[:, 0:384]), reads=[rpf], writes=[rc])
            S.op("sp", lambda e: e.dma_start(out=g[:, 0:64], in_=q_norm_g[l].partition_broadcast(128)), writes=[rg], dma="gq")
            S.op("sp", lambda e: e.dma_start(out=g[:, 64:128], in_=k_norm_g[l].partition_broadcast(128)), writes=[rg], dma="gq")
            S.op("dve", lambda e: e.tensor_scalar(out=gAs[:, l:l + 1], in0=cT[:, 56:57], scalar1=1.0 - lam_init(l), scalar2=None,
                                                  op0=ALU.mult), reads=[rc], writes=[("gAs", l)])

        def rmsnorm_to_big(l, i, gcol0, rc):
            cT = constT[l % 2]
            x = xres[:, i, :]
            xi = rr("xn", 1)
            xb, rxb = xn[xi], ("xn", xi)
            stv, rst = a_st()
            S.op("act", lambda e: e.activation(out=xb[:], in_=x, func=AF.Square, accum_out=stv[:, 0:1]),
                 reads=[("xres", i)], writes=[rxb, rst])
            S.op("act", lambda e: e.activation(out=stv[:, 1:2], in_=stv[:, 0:1], func=AF.Sqrt, scale=1.0 / D, bias=EPS),
                 reads=[rst], writes=[rst])
            S.op("dve", lambda e: e.reciprocal(out=stv[:, 2:3], in_=stv[:, 1:2]), reads=[rst], writes=[rst])
            S.op("act", lambda e: e.activation(out=xb[:], in_=x, func=AF.Copy, scale=stv[:, 2:3]),
                 reads=[("xres", i), rst], writes=[rxb])
            for half in range(2):
                pb, rpb = a_pb()
                for c8 in range(8):
                    c = half * 8 + c8
                    S.op("pe", lambda e, c=c, c8=c8, pb=pb: e.transpose(out=pb[:, c8 * 128:(c8 + 1) * 128],
                                                                      in_=xb[:, c * 128:(c + 1) * 128], identity=identb[:]),
                         reads=[rxb, "identb"], writes=[rpb])
                S.op("dve", lambda e, half=half, pb=pb: e.tensor_tensor(
                    out=big[:, half * 8:(half + 1) * 8, i * 128:(i + 1) * 128],
                    in0=pb[:].rearrange("p (c t) -> p c t", t=128),
                    in1=cT[:, gcol0 + half * 8:gcol0 + (half + 1) * 8].unsqueeze(2).to_broadcast([128, 8, 128]),
                    op=ALU.mult), reads=[rpb, rc], writes=[("big", half * 8 + c8, i) for c8 in range(8)])

        def kslot(t):
            if t.kind == "M":
                return 0, 0
            if t.kind == "S":
                return NMETA + 128 * t.g, 1 + t.g
            if t.kind == "A":
                return NMETA + PAST, 17
            return NMETA + PAST + DSEQ, 18

        def kv_out_aps(l, t, which, c0):
            if t.kind == "S":
                dst = (kp if which == "k" else vp)[l, t.s, NMETA + 128 * t.g:NMETA + 128 * (t.g + 1), c0:c0 + 512]
                return [(dst, 128)]
            if t.kind == "M":
                o = kp if which == "k" else vp
                return [(o[l, 0, 0:NMETA, c0:c0 + 512], NMETA), (o[l, 1, 0:NMETA, c0:c0 + 512], NMETA)]
            o = kso if which == "k" else vso
            return [(o[l, t.s, :, c0:c0 + 512], DSEQ)]

        def attend_jobs(l, jobs, drain_n):
            LA, DN = 2, min(8, 3 * min(len(j[3]) for j in jobs) - 1)
            SCB = [0, 1, 6, 7]
            seq = [(ji, s, ei) for ji, job in enumerate(jobs) for s in range(2) for ei in range(len(job[3]))]
            K = len(seq)
            state = {}
            accb = {}
            pending = []

            def tbuf(ji):
                a = [0, 1, 3, 4]
                i1, i2 = a[2 * (ji % 2)], a[2 * (ji % 2) + 1]
                return ded[i1], ("ded", i1), ded[i2], ("ded", i2)

            def T(k):
                ji, s, ei = seq[k]
                h, qc0, nq, entries = jobs[ji]
                kc0, nk, vtile, qoff, diag = entries[ei]
                n = nq - qoff
                sci = SCB[rr("sc", 4)]
                psc, rsc = psf[sci], ("pf", sci)
                S.op("pe", lambda e: e.matmul(psc[0:nk, 0:n], lhsT=kT[64 * s:64 * s + 64, h, kc0:kc0 + nk],
                                              rhs=qT[64 * s:64 * s + 64, h, qc0 + qoff:qc0 + nq], start=True, stop=True),
                     reads=[("kT", h), ("qT", h)], writes=[rsc])
                pt, rpt = a_tb()
                S.op("act", lambda e: e.activation(out=pt[0:nk, 0:n], in_=psc[0:nk, 0:n], func=AF.Exp, scale=0.125),
                     reads=[rsc], writes=[rpt])
                if diag:
                    S.op("dve", lambda e: e.memset(pt[64:128, 0:64], 0.0), writes=[rpt])
                state[k] = (pt, rpt)

            def C(k):
                ji, s, ei = seq[k]
                h, qc0, nq, entries = jobs[ji]
                ne = len(entries)
                kc0, nk, vtile, qoff, diag = entries[ei]
                n = nq - qoff
                if ei == 0:
                    accb[(ji, s)] = rr("acc", 2)
                par = accb[(ji, s)]
                po, rpo = psf[2 + 2 * par], ("pf", 2 + 2 * par)
                pm, rpm = psf[3 + 2 * par], ("pf", 3 + 2 * par)
                pt, rpt = state.pop(k)
                S.op("pe", lambda e: e.matmul(po[:, qoff:nq], lhsT=vS[0:nk, vtile, h * 128:(h + 1) * 128], rhs=pt[0:nk, 0:n],
                                              start=(ei == 0), stop=(ei == ne - 1)), reads=[("vS", vtile), rpt], writes=[rpo])
                S.op("pe", lambda e: e.matmul(pm[:, qoff:nq], lhsT=onesb[0:nk, :], rhs=pt[0:nk, 0:n],
                                              start=(ei == 0), stop=(ei == ne - 1)), reads=["onesb", rpt], writes=[rpm])
                if ei < ne - 1:
                    return
                T1, r1, T2, r2 = tbuf(ji)
                rt, rrt = a_tf()
                S.op("dve", lambda e: e.reciprocal(out=rt[:, 0:nq], in_=pm[:, 0:nq]), reads=[rpm], writes=[rrt])
                Tt, rT = (T1, r1) if s == 0 else (T2, r2)
                S.op("dve", lambda e: e.tensor_tensor(out=Tt[:, 0:nq], in0=po[:, 0:nq], in1=rt[:, 0:nq], op=ALU.mult),
                     reads=[rpo, rrt], writes=[rT])
                if s == 1:
                    S.op("dve", lambda e: e.scalar_tensor_tensor(out=T1[:, 0:nq], in0=T2[:, 0:nq], scalar=lamt[:, 12 + l:13 + l],
                                                                 in1=T1[:, 0:nq], op0=ALU.mult, op1=ALU.add),
                         reads=[r1, r2, "lamt"], writes=[r1])
                    S.op("act", lambda e: e.activation(out=T2[:, 0:nq], in_=T1[:, 0:nq], func=AF.Square), reads=[r1], writes=[r2])
                    pending.append((k + DN, ji))
                    drain(drain_n)

            def N(ji):
                h, qc0, nq, entries = jobs[ji]
                T1, r1, T2, r2 = tbuf(ji)
                sci = SCB[rr("sc", 4)]
                psc, rsc = psf[sci], ("pf", sci)
                S.op("pe", lambda e: e.matmul(psc[:, 0:nq], lhsT=onesf[:], rhs=T2[:, 0:nq], start=True, stop=True),
                     reads=["onesf", r2], writes=[rsc])
                S.op("act", lambda e: e.activation(out=T2[:, 0:nq], in_=psc[:, 0:nq], func=AF.Sqrt, scale=1.0 / 128, bias=EPS),
                     reads=[rsc], writes=[r2])
                S.op("dve", lambda e: e.reciprocal(out=T2[:, 0:nq], in_=T2[:, 0:nq]), reads=[r2], writes=[r2])
                S.op("dve", lambda e: e.tensor_tensor(out=T1[:, 0:nq], in0=T1[:, 0:nq], in1=T2[:, 0:nq], op=ALU.mult),
                     reads=[r1, r2], writes=[r1])
                ti0, ti1 = qc0 // 128, (qc0 + nq - 1) // 128
                S.op("act", lambda e: e.activation(out=big[:, h, qc0:qc0 + nq], in_=T1[:, 0:nq], func=AF.Copy, scale=gAs[:, l:l + 1]),
                     reads=[r1, ("gAs", l)], writes=[("big", h, ti) for ti in range(ti0, ti1 + 1)])

            for k in range(min(LA, K)):
                T(k)
            for k in range(K):
                if k + LA < K:
                    T(k + LA)
                C(k)
                while pending and pending[0][0] <= k:
                    N(pending.pop(0)[1])
            for (_, ji) in pending:
                N(ji)

        def run_block(l, bi, tiles):
            NT = len(tiles)
            NTOK = NT * 128
            isx = tiles[0].kind != "S"
            cT = constT[l % 2]
            rc = ("constT", 0)
            last = (l == depth - 1)

            for i, t in enumerate(tiles):
                if l == 0:
                    if t.kind == "S":
                        src, nv = xp[t.s, t.g * 128:(t.g + 1) * 128, :], 128
                    elif t.kind == "M":
                        src, nv = meta, NMETA
                    else:
                        src, nv = xs[t.s], DSEQ
                    if nv < 128:
                        S.op("dve", lambda e, i=i: e.memset(xres[:, i, :], 0.0), writes=[("xres", i)])
                else:
                    src, nv = xscr[t.tid * 128:(t.tid + 1) * 128, :], 128
                S.op("sp", lambda e, i=i, src=src, nv=nv: e.dma_start(out=xres[0:nv, i, :], in_=src),
                     writes=[("xres", i)], dma=f"x{i}")
                rmsnorm_to_big(l, i, 0, rc)
            bigall = [("big", c, i) for c in range(16) for i in range(NT)]

            def bigk(kc):
                return [("big", kc, i) for i in range(NT)]

            def cache_fill(t):
                S.op("pool", lambda e, t=t: e.dma_start(out=vS[:, 1:17, :], in_=cv[l, t.s].rearrange("(t p) n -> p t n", p=128)),
                     writes=[("vS", 1 + j) for j in range(16)], dma="cvl")
                for j in range(16):
                    ki = rr("kst", 2)
                    S.op("pool", lambda e, t=t, j=j, ki=ki: e.dma_start(out=kst[ki][:], in_=ck[l, t.s, j * 128:(j + 1) * 128, :]),
                         writes=[("kst", ki)], dma=f"kst{ki}")
                    pb, rpb = a_pb()
                    for hh in range(8):
                        S.op("pe", lambda e, hh=hh, ki=ki, pb=pb: e.transpose(out=pb[:, hh * 128:(hh + 1) * 128],
                                                                            in_=kst[ki][:, hh * 128:(hh + 1) * 128], identity=identb[:]),
                             reads=[("kst", ki), "identb"], writes=[rpb])
                    S.op("dve" if j % 2 else "act", (lambda e, pb=pb, j=j: e.tensor_copy(
                        out=kT[:, :, NMETA + j * 128:NMETA + (j + 1) * 128], in_=pb[:].rearrange("p (h t) -> p h t", t=128)))
                        if j % 2 else (lambda e, pb=pb, j=j: e.activation(
                            out=kT[:, :, NMETA + j * 128:NMETA + (j + 1) * 128], in_=pb[:].rearrange("p (h t) -> p h t", t=128), func=AF.Copy)),
                        reads=[rpb], writes=[("kT", hh) for hh in range(8)])

            if isx:
                cache_fill(tiles[0])

            if isx:
                segs = [(i * 158, 128, i * 128) for i in range(NT)]
                SW = 158
            else:
                segs = [(0, 512, 0)]
                SW = 542
            NS = len(segs)

            allg = [("gT", c) for c in range(8)]
            if isx:
                for i, t in enumerate(tiles):
                    h0 = segs[i][0]
                    if t.kind == "M":
                        S.op("dve", lambda e, h0=h0: e.memset(gT[:, :, h0:h0 + 30], 0.0), writes=allg)
                    else:
                        for half in range(2):
                            stg, rstg = a_tf()
                            S.op("sp", lambda e, t=t, half=half, stg=stg: e.dma_start(
                                out=stg[0:30, :], in_=sc[l, t.s, :, half * 512:(half + 1) * 512]), writes=[rstg], dma=f"tf{rstg[1]}")
                            pf, rpf = a_pf()
                            for c4 in range(4):
                                S.op("pe", lambda e, c4=c4, stg=stg, pf=pf: e.transpose(
                                    out=pf[:, c4 * 32:c4 * 32 + 30], in_=stg[0:30, c4 * 128:(c4 + 1) * 128], identity=ident[0:30, 0:30]),
                                    reads=[rstg, "ident"], writes=[rpf])
                            S.op("dve", lambda e, half=half, h0=h0, pf=pf: e.tensor_copy(
                                out=gT[:, half * 4:(half + 1) * 4, h0:h0 + 30],
                                in_=pf[:, 0:128].rearrange("p (c w) -> p c w", w=32)[:, :, 0:30]),
                                reads=[rpf], writes=[("gT", half * 4 + c4) for c4 in range(4)])
            else:
                if tiles[0].g == 0:
                    S.op("dve", lambda e: e.memset(gT[:, :, 0:14], 0.0), writes=allg)
                    S.op("dve", lambda e: e.tensor_copy(out=gT[:, :, 14:30], in_=metaG[:]), reads=[("metaG", 0), ("metaG", 1)], writes=allg)
                else:
                    S.op("dve", lambda e: e.tensor_copy(out=gT[:, :, 0:30], in_=hist[:]), reads=[("hist", 0), ("hist", 1)], writes=allg)

            nseg_tok = NTOK // NS

            def state_half(dst, col0, half):
                pf, rpf = a_pf()
                for c4 in range(4):
                    c = half * 4 + c4
                    S.op("pe", lambda e, c=c, c4=c4, pf=pf: e.transpose(
                        out=pf[0:30, c4 * 128:(c4 + 1) * 128], in_=gT[:, c, col0:col0 + 30], identity=ident[:]),
                        reads=[("gT", c), "ident"], writes=[rpf])
                stg, rstg = a_tf()
                S.op("act", lambda e, pf=pf, stg=stg: e.activation(out=stg[0:30, :], in_=pf[0:30, :], func=AF.Copy),
                     reads=[rpf], writes=[rstg])
                S.op("act", lambda e, half=half, stg=stg: e.dma_start(out=dst[:, half * 512:(half + 1) * 512], in_=stg[0:30, :]),
                     reads=[rstg], dma=f"tf{rstg[1]}")

            def save_half(half):
                hg = [("gT", half * 4 + c4) for c4 in range(4)]
                cs = slice(half * 4, half * 4 + 4)
                if isx:
                    for i, t in enumerate(tiles):
                        h0 = segs[i][0]
                        if t.kind == "M":
                            S.op("dve", lambda e, h0=h0, cs=cs: e.tensor_copy(out=metaG[:, cs, :], in_=gT[:, cs, h0 + 30:h0 + 46]),
                                 reads=hg, writes=[("metaG", half)])
                        else:
                            state_half(cso[l, t.s], h0 + 30 + 34, half)
                else:
                    if tiles[0].g == 12:
                        state_half(cp[l, tiles[0].s], 512, half)
                    else:
                        S.op("dve", lambda e, cs=cs: e.tensor_copy(out=hist[:, cs, :], in_=gT[:, cs, 512:542]), reads=hg, writes=[("hist", half)])

            def conv_taps(c):
                yt, ryt = ded[2], ("ded", 2)
                yv = yt[:, 0:NTOK].rearrange("p (s w) -> p s w", s=NS)

                def gview(j):
                    return gT[:, c, 0:NS * SW].rearrange("p (s w) -> p s w", w=SW)[:, :, j:j + nseg_tok]

                wc = lambda j: cT[:, 128 + j * 8 + c:128 + j * 8 + c + 1]
                dq.append(lambda yv=yv, g0=gview(0), w0=wc(0): S.op("dve", lambda e: e.tensor_scalar(
                    out=yv, in0=g0, scalar1=w0, scalar2=cT[:, 32 + c:33 + c], op0=ALU.mult, op1=ALU.add),
                    reads=[("gT", c), rc], writes=[ryt]))
                for j in range(1, CONVK):
                    outv = yv if j < CONVK - 1 else gview(30)
                    dq.append(lambda j=j, yv=yv, gj=gview(j), wj=wc(j), outv=outv: S.op("dve", lambda e: e.scalar_tensor_tensor(
                        out=outv, in0=gj, scalar=wj, in1=yv, op0=ALU.mult, op1=ALU.add),
                        reads=[("gT", c), rc, ryt], writes=[ryt] if j < CONVK - 1 else [("gT", c)]))

            for j in range(4):
                banks = {}
                for kind, col0 in (("a", 3072), ("g", 4096)):
                    wt, rw = load_piece("in", l, (0, D), (col0 + j * 256, col0 + (j + 1) * 256), "kc")
                    for cc in range(2):
                        pp, rp = a_pf()
                        banks[(kind, cc)] = (pp, rp)
                        for kc in range(16):
                            S.op("pe", lambda e, wt=wt, pp=pp, kc=kc, cc=cc: e.matmul(
                                pp[:, 0:NTOK], lhsT=wt[:, kc, cc * 128:(cc + 1) * 128], rhs=big[:, kc, 0:NTOK],
                                start=(kc == 0), stop=(kc == 15)), reads=[rw] + bigk(kc), writes=[rp])
                for cc in range(2):
                    c = 2 * j + cc
                    pa, rpa = banks[("a", cc)]
                    pg, rpg = banks[("g", cc)]
                    sg, rsg = a_tf()
                    S.op("act", lambda e, pg=pg, sg=sg: e.activation(out=sg[:, 0:NTOK], in_=pg[:, 0:NTOK], func=AF.Sigmoid),
                         reads=[rpg], writes=[rsg])
                    S.op("dve", lambda e, pa=pa, sg=sg, c=c: e.tensor_tensor(
                        out=gT[:, c, 0:NS * SW].rearrange("p (s w) -> p s w", w=SW)[:, :, 30:30 + NTOK // NS],
                        in0=pa[:, 0:NTOK].rearrange("p (s w) -> p s w", s=NS),
                        in1=sg[:, 0:NTOK].rearrange("p (s w) -> p s w", s=NS), op=ALU.mult),
                        reads=[rpa, rsg], writes=[("gT", c)])
                    drain(14)
                if j % 2 == 1:
                    half = j // 2
                    save_half(half)
                    for c4 in range(4):
                        conv_taps(half * 4 + c4)

            for cb in range(6):
                zb_ = [a_pf() for _ in tiles]
                for kh in range(2):
                    wt, rw = load_piece("in", l, (kh * 1024, (kh + 1) * 1024), (cb * 512, (cb + 1) * 512), "kc")
                    for i, t in enumerate(tiles):
                        pz, rpz = zb_[i]
                        for k8 in range(8):
                            kc = kh * 8 + k8
                            S.op("pe", lambda e, wt=wt, kc=kc, k8=k8, pz=pz, i=i: e.matmul(
                                pz[:], lhsT=big[:, kc, i * 128:(i + 1) * 128], rhs=wt[:, k8, :],
                                start=(kc == 0), stop=(kc == 15)), reads=[rw, ("big", kc, i)], writes=[rpz])
                for i, t in enumerate(tiles):
                    pz, rpz = zb_[i]
                    drain(5)
                    kc0, vt = kslot(t)
                    if cb < 4:
                        isk = cb >= 2
                        h0 = 4 * (cb % 2)
                        sq, rsq = a_tf()
                        S.op("act", lambda e, pz=pz, sq=sq: e.activation(out=sq[:], in_=pz[:], func=AF.Square), reads=[rpz], writes=[rsq])
                        stv, rst = a_st()
                        S.op("dve", lambda e, sq=sq, stv=stv: e.tensor_reduce(out=stv, in_=sq[:].rearrange("p (g d) -> p g d", d=64),
                                                                            axis=AX.X, op=ALU.add), reads=[rsq], writes=[rst])
                        S.op("act", lambda e, stv=stv: e.activation(out=stv, in_=stv, func=AF.Sqrt, scale=1.0 / 64, bias=EPS),
                             reads=[rst], writes=[rst])
                        S.op("dve", lambda e, stv=stv: e.reciprocal(out=stv, in_=stv), reads=[rst], writes=[rst])
                        z, rz = a_tf()
                        z3 = z[:].rearrange("p (g d) -> p g d", d=64)
                        S.op("dve", lambda e, pz=pz, z3=z3, stv=stv: e.tensor_tensor(
                            out=z3, in0=pz[:].rearrange("p (g d) -> p g d", d=64),
                            in1=stv.unsqueeze(2).to_broadcast([128, 8, 64]), op=ALU.mult), reads=[rpz, rst], writes=[rz])
                        gsl = gqk[l % 2][:, 64:128] if isk else gqk[l % 2][:, 0:64]
                        S.op("dve", lambda e, z3=z3, gsl=gsl: e.tensor_tensor(
                            out=z3, in0=z3, in1=gsl.unsqueeze(1).to_broadcast([128, 8, 64]), op=ALU.mult),
                            reads=[rz, ("gqk", 0)], writes=[rz])
                        cs = rope[:, t.ropei, 0:16].unsqueeze(1).to_broadcast([128, 8, 16])
                        sn = rope[:, t.ropei, 16:24].unsqueeze(1).to_broadcast([128, 8, 8])
                        sp_ = rope[:, t.ropei, 24:32].unsqueeze(1).to_broadcast([128, 8, 8])
                        S.op("dve", lambda e, z3=z3, sn=sn: e.tensor_tensor(out=ropet[:, :, 0:8], in0=z3[:, :, 8:16], in1=sn, op=ALU.mult),
                             reads=[rz, "rope"], writes=["ropet"])
                        S.op("dve", lambda e, z3=z3, sp_=sp_: e.tensor_tensor(out=ropet[:, :, 8:16], in0=z3[:, :, 0:8], in1=sp_, op=ALU.mult),
                             reads=[rz, "rope"], writes=["ropet"])
                        S.op("dve", lambda e, z3=z3, cs=cs: e.tensor_tensor(out=z3[:, :, 0:16], in0=z3[:, :, 0:16], in1=cs, op=ALU.mult),
                             reads=[rz, "rope"], writes=[rz])
                        S.op("dve", lambda e, z3=z3: e.tensor_tensor(out=z3[:, :, 0:16], in0=z3[:, :, 0:16], in1=ropet[:], op=ALU.add),
                             reads=[rz, "ropet"], writes=[rz])
                        zb, rzb = a_tb()
                        S.op("act", lambda e, z=z, zb=zb: e.activation(out=zb[:], in_=z[:], func=AF.Copy), reads=[rz], writes=[rzb])
                        if isk:
                            for (dst, nr) in kv_out_aps(l, t, "k", (cb % 2) * 512):
                                S.op("act", lambda e, dst=dst, nr=nr, z=z: e.dma_start(out=dst, in_=z[0:nr, :]), reads=[rz], dma=f"tf{rz[1]}")
                        pb, rpb = a_pb()
                        for hh in range(4):
                            S.op("pe", lambda e, hh=hh, zb=zb, pb=pb: e.transpose(out=pb[:, hh * 128:(hh + 1) * 128],
                                                                                in_=zb[:, hh * 128:(hh + 1) * 128], identity=identb[:]),
                                 reads=[rzb, "identb"], writes=[rpb])
                        if isk:
                            nv = t.nv
                            S.op("act", lambda e, pb=pb, h0=h0, kc0=kc0, nv=nv: e.activation(
                                out=kT[:, h0:h0 + 4, kc0:kc0 + nv],
                                in_=pb[:, 0:512].rearrange("p (h t) -> p h t", t=128)[:, :, 0:nv], func=AF.Copy),
                                reads=[rpb], writes=[("kT", h0 + hh) for hh in range(4)])
                        else:
                            S.op("act", lambda e, pb=pb, h0=h0, i=i: e.activation(
                                out=qT[:, h0:h0 + 4, i * 128:(i + 1) * 128],
                                in_=pb[:, 0:512].rearrange("p (h t) -> p h t", t=128), func=AF.Copy),
                                reads=[rpb], writes=[("qT", h0 + hh) for hh in range(4)])
                    else:
                        c0 = (cb % 2) * 512
                        vf, rvf = a_tf()
                        S.op("act", lambda e, pz=pz, vf=vf: e.activation(out=vf[:], in_=pz[:], func=AF.Copy), reads=[rpz], writes=[rvf])
                        S.op("dve", lambda e, pz=pz, vt=vt, c0=c0: e.tensor_copy(out=vS[:, vt, c0:c0 + 512], in_=pz[:]),
                             reads=[rpz], writes=[("vS", vt)])
                        for (dst, nr) in kv_out_aps(l, t, "v", c0):
                            S.op("act", lambda e, dst=dst, nr=nr, vf=vf: e.dma_start(out=dst, in_=vf[0:nr, :]), reads=[rvf], dma=f"tf{rvf[1]}")

            if isx:
                for i, t in enumerate(tiles):
                    if t.kind == "M":
                        continue
                    if i > 0:
                        cache_fill(t)
                    kc0, vt = kslot(t)
                    entries = [(0, NMETA, 0, 0, False)] + [(NMETA + 128 * j, 128, 1 + j, 0, False) for j in range(16)] + [(kc0, DSEQ, vt, 0, False)]
                    attend_jobs(l, [(h, i * 128, DSEQ, entries) for h in range(NH)], 6)
                mi = [i for i, t in enumerate(tiles) if t.kind == "M"][0]
                attend_jobs(l, [(h, mi * 128, NMETA, [(0, NMETA, 0, 0, False)]) for h in range(NH)], 0)
                for i, t in enumerate(tiles):
                    S.op("dve", lambda e, i=i, t=t: e.memset(big[:, 0:8, i * 128 + t.nv:(i + 1) * 128], 0.0),
                         writes=[("big", h, i) for h in range(8)])
            else:
                g0 = tiles[0].g
                entries = [(0, NMETA, 0, 0, False)] + [(NMETA + 128 * g, 128, 1 + g, 0, False) for g in range(g0)]
                entries += [(NMETA + 128 * (g0 + i), 128, 1 + g0 + i, 128 * i, True) for i in range(NT)]
                attend_jobs(l, [(h, 0, NTOK, entries) for h in range(NH)], 5 if g0 == 0 else 9)

            drain(10 ** 6)
            def ytok(c):
                return gT[:, c, 0:NS * SW].rearrange("p (s w) -> p s w", w=SW)[:, :, 30:30 + nseg_tok]

            pm_, rpm_ = a_pf()
            pq_, rpq_ = a_pf()
            pm3 = pm_[:, 0:NTOK].rearrange("p (s w) -> p s w", s=NS)
            pq3 = pq_[:, 0:NTOK].rearrange("p (s w) -> p s w", s=NS)
            for c in range(8):
                ysq, rysq = a_tf()
                ysq3 = ysq[:, 0:NTOK].rearrange("p (s w) -> p s w", s=NS)
                S.op("act", lambda e, c=c, ysq3=ysq3: e.activation(out=ysq3, in_=ytok(c), func=AF.Square), reads=[("gT", c)], writes=[rysq])
                S.op("pe", lambda e, c=c: e.matmul(pm3, lhsT=onesf[:], rhs=ytok(c), start=(c == 0), stop=(c == 7)),
                     reads=["onesf", ("gT", c)], writes=[rpm_])
                S.op("pe", lambda e, c=c, ysq3=ysq3: e.matmul(pq3, lhsT=onesf[:], rhs=ysq3, start=(c == 0), stop=(c == 7)),
                     reads=["onesf", rysq], writes=[rpq_])
            MEAN, RSTD = ded[0], ded[1]
            S.op("dve", lambda e: e.tensor_scalar(out=MEAN[:, 0:NTOK], in0=pm_[:, 0:NTOK], scalar1=1.0 / 1024, scalar2=None, op0=ALU.mult),
                 reads=[rpm_], writes=[("ded", 0)])
            S.op("dve", lambda e: e.tensor_tensor(out=RSTD[:, 0:NTOK], in0=MEAN[:, 0:NTOK], in1=MEAN[:, 0:NTOK], op=ALU.mult),
                 reads=[("ded", 0)], writes=[("ded", 1)])
            S.op("dve", lambda e: e.scalar_tensor_tensor(out=RSTD[:, 0:NTOK], in0=pq_[:, 0:NTOK], scalar=1.0 / 1024, in1=RSTD[:, 0:NTOK],
                                                         op0=ALU.mult, op1=ALU.subtract), reads=[rpq_, ("ded", 1)], writes=[("ded", 1)])
            S.op("dve", lambda e: e.tensor_scalar(out=RSTD[:, 0:NTOK], in0=RSTD[:, 0:NTOK], scalar1=0.0, scalar2=None, op0=ALU.max),
                 reads=[("ded", 1)], writes=[("ded", 1)])
            S.op("act", lambda e: e.activation(out=RSTD[:, 0:NTOK], in_=RSTD[:, 0:NTOK], func=AF.Sqrt, bias=EPS), reads=[("ded", 1)], writes=[("ded", 1)])
            S.op("dve", lambda e: e.reciprocal(out=RSTD[:, 0:NTOK], in_=RSTD[:, 0:NTOK]), reads=[("ded", 1)], writes=[("ded", 1)])
            mean3 = MEAN[:, 0:NTOK].rearrange("p (s w) -> p s w", s=NS)
            rstd3 = RSTD[:, 0:NTOK].rearrange("p (s w) -> p s w", s=NS)
            cbuf = []
            for c in range(8):
                n_, rn_ = a_tf()
                n3 = n_[:, 0:NTOK].rearrange("p (s w) -> p s w", s=NS)
                S.op("dve", lambda e, c=c, n3=n3: e.tensor_tensor(out=n3, in0=ytok(c), in1=mean3, op=ALU.subtract),
                     reads=[("gT", c), ("ded", 0)], writes=[rn_])
                S.op("dve", lambda e, n3=n3: e.tensor_tensor(out=n3, in0=n3, in1=rstd3, op=ALU.mult), reads=[rn_, ("ded", 1)], writes=[rn_])
                cbuf.append((c, n_, rn_))
                S.op("act", lambda e, c=c, n_=n_: e.activation(out=big[:, 8 + c, 0:NTOK], in_=n_[:, 0:NTOK], func=AF.Silu,
                                                              scale=cT[:, 40 + c:41 + c], bias=cT[:, 48 + c:49 + c]),
                     reads=[rn_, rc], writes=[("big", 8 + c, i) for i in range(NT)])

            for cb in range(4):
                zb_ = [a_pf() for _ in range(NT)]
                for kh in range(2):
                    wt, rw = load_piece("out", l, (kh * 1024, (kh + 1) * 1024), (cb * 512, (cb + 1) * 512), "kc")
                    for i in range(NT):
                        pz, rpz = zb_[i]
                        for k8 in range(8):
                            kc = kh * 8 + k8
                            S.op("pe", lambda e, wt=wt, kc=kc, k8=k8, pz=pz, i=i: e.matmul(
                                pz[:], lhsT=big[:, kc, i * 128:(i + 1) * 128], rhs=wt[:, k8, :],
                                start=(kc == 0), stop=(kc == 15)), reads=[rw, ("big", kc, i)], writes=[rpz])
                for i in range(NT):
                    pz, rpz = zb_[i]
                    S.op("dve", lambda e, pz=pz, i=i, cb=cb: e.tensor_tensor(out=xres[:, i, cb * 512:(cb + 1) * 512], in0=pz[:],
                                                                          in1=xres[:, i, cb * 512:(cb + 1) * 512], op=ALU.add),
                         reads=[rpz, ("xres", i)], writes=[("xres", i)])
            for i in range(NT):
                rmsnorm_to_big(l, i, 16, rc)

            NG = 16

            def mlp_up(g):
                ub_ = [a_pf() for _ in range(4)]
                for kh in range(2):
                    wt, rw = load_piece("up", l, (kh * 1024, (kh + 1) * 1024), (g * 512, (g + 1) * 512), "kc")
                    for fc in range(4):
                        pu, rpu = ub_[fc]
                        for k8 in range(8):
                            kc = kh * 8 + k8
                            S.op("pe", lambda e, wt=wt, kc=kc, k8=k8, fc=fc, pu=pu: e.matmul(
                                pu[:, 0:NTOK], lhsT=wt[:, k8, fc * 128:(fc + 1) * 128], rhs=big[:, kc, 0:NTOK],
                                start=(kc == 0), stop=(kc == 15)), reads=[rw] + bigk(kc), writes=[rpu])
                for fc in range(4):
                    pu, rpu = ub_[fc]
                    sq, rsq = a_tf()
                    S.op("act", lambda e, pu=pu, sq=sq: e.activation(out=sq[:, 0:NTOK], in_=pu[:, 0:NTOK], func=AF.Square),
                         reads=[rpu], writes=[rsq])
                    S.op("dve", lambda e, pu=pu, sq=sq, fc=fc: e.scalar_tensor_tensor(
                        out=qT[:, (g % 2) * 4 + fc, 0:NTOK], in0=pu[:, 0:NTOK], scalar=0.0, in1=sq[:, 0:NTOK], op0=ALU.is_gt, op1=ALU.mult),
                        reads=[rpu, rsq], writes=[("qT", (g % 2) * 4 + fc)])

            def mlp_down(g):
                halves = []
                for ch in range(2):
                    wd, rwd = load_piece("down", l, (g * 512, (g + 1) * 512), (ch * 1024, (ch + 1) * 1024), "kc")
                    halves.append((wd, rwd))
                for i in range(NT):
                    for cb in range(4):
                        wd, rwd = halves[cb // 2]
                        pz, rpz = a_pf()
                        for fc in range(4):
                            S.op("pe", lambda e, wd=wd, fc=fc, pz=pz, i=i, cb=cb: e.matmul(
                                pz[:], lhsT=qT[:, (g % 2) * 4 + fc, i * 128:(i + 1) * 128],
                                rhs=wd[:, fc, (cb % 2) * 512:(cb % 2 + 1) * 512], start=(fc == 0), stop=(fc == 3)),
                                reads=[rwd, ("qT", (g % 2) * 4 + fc)], writes=[rpz])
                        S.op("dve", lambda e, pz=pz, i=i, cb=cb: e.tensor_tensor(out=xres[:, i, cb * 512:(cb + 1) * 512], in0=pz[:],
                                                                              in1=xres[:, i, cb * 512:(cb + 1) * 512], op=ALU.add),
                             reads=[rpz, ("xres", i)], writes=[("xres", i)])

            mlp_up(0)
            for g in range(NG):
                if g + 1 < NG:
                    mlp_up(g + 1)
                mlp_down(g)

            for i, t in enumerate(tiles):
                if not last:
                    S.op("act", lambda e, i=i, t=t: e.dma_start(out=xscr[t.tid * 128:(t.tid + 1) * 128, :], in_=xres[:, i, :]),
                         reads=[("xres", i)], dma=f"xo{i}")
                elif t.kind == "S":
                    S.op("act", lambda e, i=i, t=t: e.dma_start(out=yp[t.s, t.g * 128:(t.g + 1) * 128, :], in_=xres[:, i, :]),
                         reads=[("xres", i)], dma=f"xo{i}")
                elif t.kind in ("A", "B"):
                    S.op("act", lambda e, i=i, t=t: e.dma_start(out=ys[t.s], in_=xres[0:DSEQ, i, :]), reads=[("xres", i)], dma=f"xo{i}")

        pump_conversions(8, 0)
        nconv_per_layer = sum(conv_chunks(m) for m in ("in", "out", "up", "down"))
        for l in range(depth):
            S.epoch = l
            load_layer_consts(l)
            for bi, tiles in enumerate(blocks):
                run_block(l, bi, tiles)
                if l == 0 and bi == 0:
                    pump_conversions(10 ** 6, 0)
                if l + 1 < depth:
                    pump_conversions((nconv_per_layer + len(blocks) - 1) // len(blocks) + 1, l + 1)
        S.emit()
        build_program.stats = dict(n_ops={e: len(S.ops[e]) for e in ENGS}, n_wait=S.n_wait,
                                   sbuf_left=nc.sbuf_bytes_remaining)
    return nc


def host_consts():
    ident = np.eye(128, dtype=np.float32)
    half = 8
    inv_freq = (np.float32(500000.0) ** (-(np.arange(half, dtype=np.float32) * np.float32(2.0)) / np.float32(16))).astype(np.float32)
    rope = np.zeros((128, 18, 32), np.float32)
    p = np.arange(128)
    for ti in range(18):
        if ti < 16:
            pos = NMETA + 128 * ti + p
        elif ti == 16:
            pos = NMETA + PAST + p
        else:
            pos = p
        ang = pos.astype(np.float32)[:, None] * inv_freq[None, :]
        c = np.cos(ang).astype(np.float32)
        s = np.sin(ang).astype(np.float32)
        rope[:, ti, 0:8] = c
        rope[:, ti, 8:16] = c
        rope[:, ti, 16:24] = -s
        rope[:, ti, 24:32] = s
    return ident, rope


_CACHE = {}


def kernel(x_prompt, x_sample, cache_k, cache_v, state_conv, meta_tokens, norm1_g, w_in,
           q_norm_g, k_norm_g, lam_q1, lam_k1, lam_q2, lam_k2, attn_norm_g, conv_w, conv_b,
           conv_ln_g, conv_ln_b, w_out, norm2_g, w_up, w_down):
    f = lambda a: np.ascontiguousarray(np.asarray(a, dtype=np.float32))
    x_prompt, x_sample, cache_k, cache_v, state_conv = map(f, (x_prompt, x_sample, cache_k, cache_v, state_conv))
    shared = dict(meta=f(meta_tokens), norm1_g=f(norm1_g), w_in=f(w_in), q_norm_g=f(q_norm_g), k_norm_g=f(k_norm_g),
                  lam_q1=f(lam_q1), lam_k1=f(lam_k1), lam_q2=f(lam_q2), lam_k2=f(lam_k2), attn_norm_g=f(attn_norm_g),
                  conv_w=f(conv_w), conv_b=f(conv_b), conv_ln_g=f(conv_ln_g), conv_ln_b=f(conv_ln_b), w_out=f(w_out),
                  norm2_g=f(norm2_g), w_up=f(w_up), w_down=f(w_down))
    ident, rope = host_consts()
    shared["identd"] = ident
    shared["roped"] = rope
    in_maps = []
    for c in range(NCORES):
        b = slice(2 * c, 2 * c + 2)
        m = dict(shared)
        m["xp"] = np.ascontiguousarray(x_prompt[b])
        m["xs"] = np.ascontiguousarray(x_sample[b])
        m["ck"] = np.ascontiguousarray(cache_k[:, b].reshape(DEPTH, 2, PAST, 1024))
        m["cv"] = np.ascontiguousarray(cache_v[:, b].reshape(DEPTH, 2, PAST, 1024))
        m["sc"] = np.ascontiguousarray(state_conv[:, b])
        in_maps.append(m)
    if "nc" not in _CACHE:
        _CACHE["nc"] = build_program()
    res = run_bass_kernel_spmd(_CACHE["nc"], in_maps, core_ids=list(range(NCORES)))
    R = res.results
    cat = lambda name, ax: np.concatenate([np.asarray(r[name]) for r in R], axis=ax)
    y_prompt = cat("yp", 0)
    y_sample = cat("ys", 0)
    k_prompt = cat("kp", 1).reshape(DEPTH, 16, L_P, NH, 128)
    v_prompt = cat("vp", 1).reshape(DEPTH, 16, L_P, NH, 128)
    conv_prompt = cat("cp", 1)
    k_sample = cat("kso", 1).reshape(DEPTH, 16, DSEQ, NH, 128)
    v_sample = cat("vso", 1).reshape(DEPTH, 16, DSEQ, NH, 128)
    conv_sample = cat("cso", 1)
    return (y_prompt, y_sample, k_prompt, v_prompt, conv_prompt, k_sample, v_sample, conv_sample)
```
